# Optimizing a Trainium2 kernel written in Bass

```python
import math
import jax, jax.numpy as jnp
from jax import lax
import numpy as np

D_MODEL = 1024
BATCH = 8
SEQ = 2048
DEPTH = 4

GRID_W = 64
CTX_LEN = 256
MIX_WIDTH = D_MODEL
S5_WIDTH = (3 * MIX_WIDTH) // 4
FNET_WIDTH = MIX_WIDTH - S5_WIDTH
S5_H = 16
S5_GROUPS = S5_WIDTH // S5_H
S5_P = 64
FNET_GROUPS = 4
FNET_GW = FNET_WIDTH // FNET_GROUPS
POOL_WIDTH = MIX_WIDTH // 2
CONV_WIDTH = MIX_WIDTH - POOL_WIDTH
POOL_WINDOWS = (2, 4, 8, 16)
POOL_GROUPS = len(POOL_WINDOWS)
POOL_GW = POOL_WIDTH // POOL_GROUPS
CONV_K = 3
N_EVEN = (DEPTH + 1) // 2
N_ODD = DEPTH // 2
EVEN_IN = S5_WIDTH + FNET_WIDTH + MIX_WIDTH
ODD_IN = POOL_WIDTH + 3 * CONV_WIDTH + MIX_WIDTH
STEP_MIN = 1e-3
STEP_MAX = 1e-1
EPS = 1e-6
POS_BASE = 10000.0

kernel_name = 'hybrid_s5_fnet_pool_shortconv_prefix'


def rmsnorm(x, g):
    xf = x.astype(jnp.float32)
    y = xf * lax.rsqrt(jnp.mean(xf * xf, axis=-1, keepdims=True) + EPS)
    return (y * g.astype(jnp.float32)).astype(x.dtype)


def ada_params(cond, w, b):
    m = jax.nn.silu(cond) @ w + b
    return jnp.split(m, 3, axis=-1)


def sincos_2d(n_tok, dim):
    rows = n_tok // GRID_W
    r = jnp.arange(rows, dtype=jnp.float32)
    col = jnp.arange(GRID_W, dtype=jnp.float32)
    rr, cc = jnp.meshgrid(r, col, indexing='ij')
    rr = rr.reshape(-1, 1)
    cc = cc.reshape(-1, 1)
    quarter = dim // 4
    omega = POS_BASE ** (-jnp.arange(quarter, dtype=jnp.float32) / quarter)
    return jnp.concatenate([jnp.sin(rr * omega), jnp.cos(rr * omega),
                            jnp.sin(cc * omega), jnp.cos(cc * omega)], axis=-1)


def _linear_recurrence(e1, e2):
    a1, b1 = e1
    a2, b2 = e2
    return a2 * a1, a2 * b1 + b2


def s5_direction(u, lam_re, lam_im, log_step, b_re, b_im, c_re, c_im):
    f32 = jnp.float32
    lam = lax.complex(lam_re.astype(f32), lam_im.astype(f32))
    step = jnp.exp(log_step.astype(f32))[:, None]
    lam_bar = jnp.exp(lam * step)
    b_bar = ((lam_bar - 1.0) / lam)[..., None] * lax.complex(b_re.astype(f32), b_im.astype(f32))
    bu = lax.complex(jnp.einsum('btgh,gph->btgp', u, b_bar.real),
                     jnp.einsum('btgh,gph->btgp', u, b_bar.imag))
    a = jnp.broadcast_to(lam_bar, bu.shape)
    _, s = lax.associative_scan(_linear_recurrence, (a, bu), axis=1)
    return (jnp.einsum('btgp,ghp->btgh', s.real, c_re.astype(f32))
            - jnp.einsum('btgp,ghp->btgh', s.imag, c_im.astype(f32)))


def s5_bidir(u_lat, u_ctx, need_ctx, lam_re, lam_im, log_step, b_re, b_im, c_re, c_im, d_skip):
    bsz, t_lat, _ = u_lat.shape
    t_ctx = u_ctx.shape[1]
    ul = u_lat.astype(jnp.float32).reshape(bsz, t_lat, S5_GROUPS, S5_H)
    uc = u_ctx.astype(jnp.float32).reshape(bsz, t_ctx, S5_GROUPS, S5_H)
    yf = s5_direction(jnp.concatenate([uc, ul], axis=1),
                      lam_re[0], lam_im[0], log_step[0], b_re[0], b_im[0], c_re[0], c_im[0])
    yb = s5_direction(jnp.concatenate([jnp.flip(uc, 1), jnp.flip(ul, 1)], axis=1),
                      lam_re[1], lam_im[1], log_step[1], b_re[1], b_im[1], c_re[1], c_im[1])
    d = d_skip.astype(jnp.float32)
    y_lat = yf[:, t_ctx:] + jnp.flip(yb[:, t_ctx:], 1) + d * ul
    y_lat = y_lat.reshape(bsz, t_lat, S5_WIDTH)
    if not need_ctx:
        return y_lat, None
    y_ctx = yf[:, :t_ctx] + jnp.flip(yb[:, :t_ctx], 1) + d * uc
    return y_lat, y_ctx.reshape(bsz, t_ctx, S5_WIDTH)


def fnet_mix(u, f_w):
    bsz, t, _ = u.shape
    g = u.astype(jnp.float32).reshape(bsz, t, FNET_GROUPS, FNET_GW).transpose(0, 2, 1, 3)
    f = jnp.fft.fft2(g, axes=(-2, -1), norm='ortho').real
    y = jnp.einsum('bgtc,gcd->btgd', f, f_w.astype(jnp.float32))
    return y.reshape(bsz, t, FNET_WIDTH)


def even_branch_out(ya, ub, z, glu_w, glu_b, f_w, w_out):
    ya = jax.nn.gelu(ya)
    ya = ya * jax.nn.sigmoid(ya @ glu_w.astype(jnp.float32) + glu_b.astype(jnp.float32))
    yb = fnet_mix(ub, f_w)
    y = jnp.concatenate([ya, yb], axis=-1).astype(z.dtype) * jax.nn.silu(z)
    return y @ w_out


def even_mixer(hl, hc, need_ctx, w_in, w_out, lam_re, lam_im, log_step, b_re, b_im,
               c_re, c_im, d_skip, glu_w, glu_b, f_w):
    pl = hl @ w_in
    pc = hc @ w_in if need_ctx else hc @ w_in[:, :S5_WIDTH]
    ya_l, ya_c = s5_bidir(pl[..., :S5_WIDTH], pc[..., :S5_WIDTH], need_ctx,
                          lam_re, lam_im, log_step, b_re, b_im, c_re, c_im, d_skip)
    out_l = even_branch_out(ya_l, pl[..., S5_WIDTH:MIX_WIDTH], pl[..., MIX_WIDTH:],
                            glu_w, glu_b, f_w, w_out)
    if not need_ctx:
        return out_l, None
    out_c = even_branch_out(ya_c, pc[..., S5_WIDTH:MIX_WIDTH], pc[..., MIX_WIDTH:],
                            glu_w, glu_b, f_w, w_out)
    return out_l, out_c


def centered_mean(v, window):
    bsz, t, ch = v.shape
    s = jnp.concatenate([jnp.zeros((bsz, 1, ch), v.dtype), jnp.cumsum(v, axis=1)], axis=1)
    pos = jnp.arange(t)
    lo = jnp.clip(pos - window // 2, 0, t)
    hi = jnp.clip(pos + window // 2, 0, t)
    total = jnp.take(s, hi, axis=1) - jnp.take(s, lo, axis=1)
    return total / (hi - lo).astype(v.dtype)[None, :, None]


def pool_mix(u, p_w, p_scale):
    bsz, t, _ = u.shape
    g = u.astype(jnp.float32).reshape(bsz, t, POOL_GROUPS, POOL_GW)
    pooled = jnp.stack([centered_mean(g[:, :, i], POOL_WINDOWS[i]) - g[:, :, i]
                        for i in range(POOL_GROUPS)], axis=2)
    y = jnp.einsum('btgc,gcd->btgd', pooled, p_w.astype(jnp.float32)).reshape(bsz, t, POOL_WIDTH)
    return (y * p_scale.astype(jnp.float32)).astype(u.dtype)


def short_conv(v, w):
    return lax.conv_general_dilated(v, w[:, None, :].astype(v.dtype), window_strides=(1,),
                                    padding=[(CONV_K // 2, CONV_K // 2)],
                                    dimension_numbers=('NWC', 'WIO', 'NWC'),
                                    feature_group_count=v.shape[-1])


def odd_mixer(h, w_in, w_out, p_w, p_scale, conv_w):
    p = h @ w_in
    o1 = POOL_WIDTH
    o2 = o1 + CONV_WIDTH
    o3 = o2 + CONV_WIDTH
    o4 = o3 + CONV_WIDTH
    y_c = pool_mix(p[..., :o1], p_w, p_scale)
    c_gate, x_path, b_gate, z = p[..., o1:o2], p[..., o2:o3], p[..., o3:o4], p[..., o4:]
    y_d = b_gate * short_conv(c_gate * x_path, conv_w)
    y = jnp.concatenate([y_c, y_d], axis=-1) * jax.nn.silu(z)
    return y @ w_out


def setup_inputs(seed: int = 0) -> dict:
    key = jax.random.key(seed)
    ks = jax.random.split(key, 26)
    f32 = jnp.float32
    D = D_MODEL

    def nrm(k, shape, s):
        return jax.random.normal(k, shape, f32) * s

    s5_state_shape = (N_EVEN, 2, S5_GROUPS, S5_P)
    return {
        'x': nrm(ks[0], (BATCH, SEQ, D), 1.0),
        'c': nrm(ks[1], (BATCH, D), 1.0),
        'ctx': nrm(ks[2], (BATCH, CTX_LEN, D), 1.0),
        'c_ctx': nrm(ks[3], (D,), 1.0),
        'norm_g': 1.0 + nrm(ks[4], (DEPTH, D), 0.02),
        'ada_w': nrm(ks[5], (DEPTH, D, 3 * D), 0.5 * D ** -0.5),
        'ada_b': nrm(ks[6], (DEPTH, 3 * D), 0.02),
        'even_w_in': nrm(ks[7], (N_EVEN, D, EVEN_IN), D ** -0.5),
        'even_w_out': nrm(ks[8], (N_EVEN, MIX_WIDTH, D), MIX_WIDTH ** -0.5),
        's5_lam_re': -0.5 + nrm(ks[9], s5_state_shape, 0.01),
        's5_lam_im': jnp.pi * jnp.arange(S5_P, dtype=f32) + nrm(ks[10], s5_state_shape, 0.01),
        's5_log_step': jax.random.uniform(ks[11], (N_EVEN, 2, S5_GROUPS), f32,
                                          math.log(STEP_MIN), math.log(STEP_MAX)),
        's5_b_re': nrm(ks[12], (N_EVEN, 2, S5_GROUPS, S5_P, S5_H), (2 * S5_H) ** -0.5),
        's5_b_im': nrm(ks[13], (N_EVEN, 2, S5_GROUPS, S5_P, S5_H), (2 * S5_H) ** -0.5),
        's5_c_re': nrm(ks[14], (N_EVEN, 2, S5_GROUPS, S5_H, S5_P), S5_P ** -0.5),
        's5_c_im': nrm(ks[15], (N_EVEN, 2, S5_GROUPS, S5_H, S5_P), S5_P ** -0.5),
        's5_d': 1.0 + nrm(ks[16], (N_EVEN, S5_GROUPS, S5_H), 0.1),
        's5_glu_w': nrm(ks[17], (N_EVEN, S5_WIDTH, S5_WIDTH), S5_WIDTH ** -0.5),
        's5_glu_b': nrm(ks[18], (N_EVEN, S5_WIDTH), 0.02),
        'fnet_w': nrm(ks[19], (N_EVEN, FNET_GROUPS, FNET_GW, FNET_GW), FNET_GW ** -0.5),
        'odd_w_in': nrm(ks[20], (N_ODD, D, ODD_IN), D ** -0.5),
        'odd_w_out': nrm(ks[21], (N_ODD, MIX_WIDTH, D), MIX_WIDTH ** -0.5),
        'pool_w': nrm(ks[22], (N_ODD, POOL_GROUPS, POOL_GW, POOL_GW), POOL_GW ** -0.5),
        'pool_scale': 1.0 + nrm(ks[23], (N_ODD, POOL_WIDTH), 0.1),
        'conv_w': nrm(ks[24], (N_ODD, CONV_K, CONV_WIDTH), CONV_K ** -0.5),
        'final_g': 1.0 + nrm(ks[25], (D,), 0.02),
    }


def reference(x, c, ctx, c_ctx, norm_g, ada_w, ada_b, even_w_in, even_w_out,
              s5_lam_re, s5_lam_im, s5_log_step, s5_b_re, s5_b_im, s5_c_re, s5_c_im,
              s5_d, s5_glu_w, s5_glu_b, fnet_w, odd_w_in, odd_w_out, pool_w,
              pool_scale, conv_w, final_g):
    n_tok = x.shape[1]
    xl = x + sincos_2d(n_tok, x.shape[-1]).astype(x.dtype)[None]
    xc = ctx
    need_ctx = [any(j % 2 == 0 for j in range(l + 1, DEPTH)) for l in range(DEPTH)]
    for l in range(DEPTH):
        i = l // 2
        shift, scale, gate = ada_params(c, ada_w[l], ada_b[l])
        hl = rmsnorm(xl, norm_g[l]) * (1.0 + scale[:, None]) + shift[:, None]
        use_ctx_in = (l % 2 == 0) or need_ctx[l]
        if use_ctx_in:
            shift_c, scale_c, gate_c = ada_params(c_ctx, ada_w[l], ada_b[l])
            hc = rmsnorm(xc, norm_g[l]) * (1.0 + scale_c) + shift_c
        if l % 2 == 0:
            out_l, out_c = even_mixer(hl, hc, need_ctx[l], even_w_in[i], even_w_out[i],
                                      s5_lam_re[i], s5_lam_im[i], s5_log_step[i],
                                      s5_b_re[i], s5_b_im[i], s5_c_re[i], s5_c_im[i],
                                      s5_d[i], s5_glu_w[i], s5_glu_b[i], fnet_w[i])
        else:
            out_l = odd_mixer(hl, odd_w_in[i], odd_w_out[i], pool_w[i], pool_scale[i], conv_w[i])
            out_c = (odd_mixer(hc, odd_w_in[i], odd_w_out[i], pool_w[i], pool_scale[i], conv_w[i])
                     if need_ctx[l] else None)
        xl = xl + gate[:, None] * out_l.astype(xl.dtype)
        if need_ctx[l]:
            xc = xc + gate_c * out_c.astype(xc.dtype)
    return rmsnorm(xl, final_g)
```

```python
import math
from contextlib import ExitStack

import numpy as np
import ml_dtypes
import concourse.bass as bass
import concourse.mybir as mybir
from concourse.bass_utils import run_bass_kernel_spmd

F32 = mybir.dt.float32
BF16 = mybir.dt.bfloat16
I32 = mybir.dt.int32
AF = mybir.ActivationFunctionType
ALU = mybir.AluOpType

D = 1024
T = 2048
TC = 256
TT = T + TC
KC = 8
DEPTH = 4
NEED_CTX = [True, True, False, False]
EPS = 1e-6
POOL_W = (2, 4, 8, 16)
NCH = 72
NDMA_SLOTS = 36


def _rect(ap):
    t = ap.tensor
    name = t.name
    dims = ap.ap
    off = int(ap.offset)
    if "DRam" in type(t).__name__:
        lo = off
        hi = off
        for s, c in dims:
            if s >= 0:
                hi += (c - 1) * s
            else:
                lo += (c - 1) * s
        cnt = 1
        for _s, c in dims:
            cnt *= c
        return (name, 0, 1, lo, hi + 1, cnt == hi + 1 - lo)
    rowsize = 1
    for d in list(t.shape)[1:]:
        rowsize *= d
    if "PSum" in type(t).__name__:
        return (name, 0, 128, 0, rowsize, True, True)
    p0 = off // rowsize
    f0 = off % rowsize
    s0, c0 = dims[0]
    if s0 == rowsize or c0 == 1:
        p1 = p0 + c0
        rest = dims[1:]
    else:
        p1 = p0 + 1
        rest = dims
    lo = f0
    hi = f0
    for s, c in rest:
        if s >= 0:
            hi += (c - 1) * s
        else:
            lo += (c - 1) * s
    cnt = 1
    for _s, c in rest:
        cnt *= c
    return (name, p0, p1, lo, hi + 1, cnt == hi + 1 - lo)


def _overlap(a, b):
    return a[1] < b[2] and b[1] < a[2] and a[3] < b[4] and b[3] < a[4]


def _contains(a, b):
    return a[1] <= b[1] and a[2] >= b[2] and a[3] <= b[3] and a[4] >= b[4]


class _Stop(Exception):
    pass


class Sched:
    ENG = ("pe", "act", "dve", "pool", "sp")

    def __init__(self, nc):
        self.nc = nc
        self.ops = []
        self.recs = {}

    limit = None

    def add(self, eng, fn, reads, writes, dma=False):
        oid = len(self.ops)
        if self.limit is not None and oid >= self.limit:
            raise _Stop()
        deps = {}
        rr = [_rect(a) for a in reads]
        wr = [_rect(a) for a in writes]
        psr = [r for r in rr if len(r) > 6]
        if psr:
            rr = [r for r in rr if len(r) <= 6]
            wr = wr + psr
        for r in rr:
            for (q, o, w) in self.recs.get(r[0], ()):
                if w and _overlap(r, q):
                    deps[o] = True
        for r in wr:
            for (q, o, w) in self.recs.get(r[0], ()):
                if _overlap(r, q):
                    deps.setdefault(o, False)
        ops = self.ops

        def same_eng(o):
            if o == oid:
                return True
            od = ops[o]
            return (not dma) and (not od["dma"]) and od["eng"] == eng

        for r in wr:
            lst = self.recs.setdefault(r[0], [])
            lst[:] = [x for x in lst if not (_contains(r, x[0]) and (r[5] or same_eng(x[1])))]
            lst.append((r, oid, True))
        for r in rr:
            lst = self.recs.setdefault(r[0], [])
            lst[:] = [x for x in lst if not ((not x[2]) and _contains(r, x[0]) and same_eng(x[1]))]
            lst.append((r, oid, False))
        deps.pop(oid, None)
        keep = []
        for o, raw in deps.items():
            od = self.ops[o]
            if (not dma) and (not od["dma"]) and od["eng"] == eng:
                if eng == "pe":
                    continue
            keep.append(o)
        self.ops.append(dict(eng=eng, fn=fn, dma=dma, deps=sorted(keep)))
        return oid

    def dma(self, out, in_, eng="sp"):
        return self.add(eng, lambda e: e.dma_start(out=out, in_=in_), [in_], [out], dma=True)

    def mm(self, out, lhsT, rhs, start=True, stop=True):
        reads = [lhsT, rhs]
        return self.add("pe", lambda e: e.matmul(out, lhsT, rhs, start=start, stop=stop), reads, [out])

    def tr(self, out, in_, ident):
        return self.add("pe", lambda e: e.transpose(out, in_, ident), [in_, ident], [out])

    def act(self, out, in_, func, bias=None, scale=None):
        reads = [in_]
        kw = {}
        if bias is not None:
            kw["bias"] = bias
            reads.append(bias)
        if scale is not None:
            kw["scale"] = scale
            if not isinstance(scale, (int, float)):
                reads.append(scale)
        return self.add("act", lambda e: e.activation(out, in_, func, **kw), reads, [out])

    def tt(self, eng, out, in0, in1, op):
        return self.add(eng, lambda e: e.tensor_tensor(out, in0, in1, op), [in0, in1], [out])

    def ts(self, eng, out, in0, s1, s2, op0, op1=None):
        reads = [in0] + [s for s in (s1, s2) if s is not None and not isinstance(s, (int, float))]
        if op1 is None:
            return self.add(eng, lambda e: e.tensor_scalar(out, in0, s1, None, op0), reads, [out])
        return self.add(eng, lambda e: e.tensor_scalar(out, in0, s1, s2, op0, op1), reads, [out])

    def stt(self, eng, out, in0, scalar, in1, op0, op1):
        reads = [in0, in1] + ([] if isinstance(scalar, (int, float)) else [scalar])
        return self.add(eng, lambda e: e.scalar_tensor_tensor(out, in0, scalar, in1, op0, op1), reads, [out])

    def copy(self, eng, out, in_):
        if eng == "act":
            return self.add("act", lambda e: e.activation(out, in_, AF.Copy), [in_], [out])
        return self.add(eng, lambda e: e.tensor_copy(out, in_), [in_], [out])

    def memset(self, eng, out, val):
        return self.add(eng, lambda e: e.memset(out, val), [], [out])

    def recip(self, out, in_):
        return self.add("dve", lambda e: e.reciprocal(out, in_), [in_], [out])

    def emit(self, sems, dsems):
        nc = self.nc
        ops = self.ops
        n = len(ops)
        per_eng = {e: [] for e in self.ENG}
        for i, o in enumerate(ops):
            per_eng[o["eng"]].append(i)
        slot_uses = [0] * len(dsems)
        dctr = 0
        sctr = 0
        NSW = 4
        for i, o in enumerate(ops):
            if o["dma"]:
                if o["eng"] == "pool":
                    s = len(dsems) - NSW + sctr % NSW
                    sctr += 1
                else:
                    s = dctr % (len(dsems) - NSW)
                    dctr += 1
                slot_uses[s] += 1
                o["tl"] = ("d", s)
                o["pos"] = slot_uses[s]
                o["val"] = 16 * slot_uses[s]
            else:
                o["tl"] = o["eng"]
        for e in self.ENG:
            k = 0
            for i in per_eng[e]:
                if not ops[i]["dma"]:
                    k += 1
                    ops[i]["pos"] = k
        know = {e: {} for e in self.ENG}
        snap = [None] * n
        waits = [None] * n
        signal = [False] * n
        last_slot_op = {}
        for i, o in enumerate(ops):
            e = o["eng"]
            k = know[e]
            w = []
            deps = list(o["deps"])
            if o["dma"]:
                s = o["tl"]
                if s in last_slot_op:
                    deps.append(last_slot_op[s])
                last_slot_op[s] = i
            for d in sorted(deps):
                od = ops[d]
                tl = od["tl"]
                if k.get(tl, 0) >= od["pos"]:
                    continue
                w.append(d)
                signal[d] = True
                for t2, p2 in snap[d].items():
                    if k.get(t2, 0) < p2:
                        k[t2] = p2
                if k.get(tl, 0) < od["pos"]:
                    k[tl] = od["pos"]
            waits[i] = w
            snap[i] = dict(k)
        for e in self.ENG:
            c = 0
            for i in per_eng[e]:
                o = ops[i]
                if o["dma"]:
                    continue
                if signal[i]:
                    c += 1
                    o["val"] = c
                else:
                    o["val"] = None
        self.n_waits = sum(len(w) for w in waits)
        self.n_signals = sum(1 for s_ in signal if s_)

        def semof(o):
            if o["dma"]:
                return dsems[o["tl"][1]]
            return sems[o["eng"]]

        def run_engine(e):
            def body(eng):
                for i in per_eng[e]:
                    o = ops[i]
                    best = {}
                    for d in waits[i]:
                        od = ops[d]
                        key = od["tl"]
                        if key not in best or best[key][1] < od["val"]:
                            best[key] = (semof(od), od["val"])
                    for key, (sm, v) in best.items():
                        eng.wait_ge(sm, v)
                    ins = o["fn"](eng)
                    if o["dma"]:
                        ins.then_inc(semof(o), 16)
                    elif signal[i]:
                        ins.then_inc(semof(o), 1)
                if e == "sp":
                    for s, sm in enumerate(dsems):
                        if slot_uses[s]:
                            eng.wait_ge(sm, 16 * slot_uses[s])
            return body

        with nc.Block() as block:
            block.sync(run_engine("sp"))
            block.tensor(run_engine("pe"))
            block.scalar(run_engine("act"))
            block.vector(run_engine("dve"))
            block.gpsimd(run_engine("pool"))


V_NORMG = 0
V_FINALG = 32
V_ADAB = 40
V_GLUB = 136
V_PSCALE = 148
V_CONVW = 156
V_CVEC = 180
V_DTAB = 196
V_CONSTS = 292
NVEC = 300
C_ID = 0
C_MF = 128
C_MB = 256
C_BDC = 384
C_BDS = 512
C_FIX = 640
C_PROW = 704
C_PCOL = 832
NCST = 1088

_CONST_CACHE = {}


def _host_consts():
    if _CONST_CACHE:
        return _CONST_CACHE
    f32 = np.float32
    rows = T // 64
    r = np.arange(rows, dtype=f32)
    col = np.arange(64, dtype=f32)
    rr, cc = np.meshgrid(r, col, indexing="ij")
    rr = rr.reshape(-1, 1)
    cc = cc.reshape(-1, 1)
    quarter = D // 4
    omega = np.power(f32(10000.0), -np.arange(quarter, dtype=f32) / f32(quarter)).astype(f32)
    pos = np.concatenate([np.sin(rr * omega), np.cos(rr * omega), np.sin(cc * omega), np.cos(cc * omega)],
                         axis=-1).astype(f32)
    posT = np.ascontiguousarray(pos.T)

    cst = np.zeros((128, NCST), f32)
    cst[:, C_ID:C_ID + 128] = np.eye(128, dtype=f32)
    jj = np.arange(128) // 16
    cst[:, C_MF:C_MF + 128] = (jj[:, None] <= jj[None, :]).astype(f32)
    cst[:, C_MB:C_MB + 128] = (jj[:, None] >= jj[None, :]).astype(f32)
    m = np.arange(64)
    ang = 2.0 * np.pi * ((m[:, None] * m[None, :]) % 64) / 64.0
    c64 = np.cos(ang)
    s64 = np.sin(ang)
    bdc = np.zeros((128, 128))
    bds = np.zeros((128, 128))
    for g in range(2):
        bdc[g * 64:(g + 1) * 64, g * 64:(g + 1) * 64] = c64
        bds[g * 64:(g + 1) * 64, g * 64:(g + 1) * 64] = s64
    cst[:, C_BDC:C_BDC + 128] = bdc
    cst[:, C_BDS:C_BDS + 128] = bds
    fix = np.zeros((4, 16))
    for ti, w in enumerate(POOL_W):
        for p in range(8):
            cnt = min(p + w // 2, 10 ** 6) - max(p - w // 2, 0)
            fix[ti, p] = w / cnt
            posr = 1000 - 8 + p
            cnt = min(posr + w // 2, 1000) - (posr - w // 2)
            fix[ti, 8 + p] = w / cnt
    cst[:, C_FIX:C_FIX + 64] = fix.reshape(1, 64)
    prow = pos[0::64, 0:512]
    pcol = pos[0:64, 512:1024]
    cst[:, C_PROW:C_PROW + 128] = prow.T.reshape(4, 128, 32).transpose(1, 0, 2).reshape(128, 128)
    cst[:, C_PCOL:C_PCOL + 256] = pcol.T.reshape(4, 128, 64).transpose(1, 0, 2).reshape(128, 256)

    def dft(n):
        t = np.arange(n)
        a = 2.0 * np.pi * ((t[:, None] * t[None, :]) % n) / n
        cs = np.cos(a)
        ns = -np.sin(a)
        kc = n // 128
        cs = cs.reshape(kc, 128, n).transpose(1, 0, 2)
        ns = ns.reshape(kc, 128, n).transpose(1, 0, 2)
        return (np.ascontiguousarray(cs).astype(ml_dtypes.bfloat16),
                np.ascontiguousarray(ns).astype(ml_dtypes.bfloat16))

    cosL, nsinL = dft(T)
    cosC, nsinC = dft(TC)
    _CONST_CACHE.update(posT=posT, cst=cst, cosL=cosL, nsinL=nsinL, cosC=cosC, nsinC=nsinC)
    return _CONST_CACHE


def _pk(v):
    v = np.asarray(v, np.float32)
    return np.ascontiguousarray(v.reshape(-1, 128).T)


def _prep_core_inputs(inp, b):
    f32 = np.float32
    vec = np.zeros((128, NVEC), f32)
    for l in range(DEPTH):
        vec[:, V_NORMG + l * 8:V_NORMG + (l + 1) * 8] = _pk(inp["norm_g"][l])
        vec[:, V_ADAB + l * 24:V_ADAB + (l + 1) * 24] = _pk(inp["ada_b"][l])
    vec[:, V_FINALG:V_FINALG + 8] = _pk(inp["final_g"])
    for i in range(2):
        vec[:, V_GLUB + i * 6:V_GLUB + (i + 1) * 6] = _pk(inp["s5_glu_b"][i])
        vec[:, V_PSCALE + i * 4:V_PSCALE + (i + 1) * 4] = _pk(inp["pool_scale"][i])
        for k in range(3):
            vec[:, V_CONVW + (i * 3 + k) * 4:V_CONVW + (i * 3 + k + 1) * 4] = _pk(inp["conv_w"][i, k])
        dt = np.asarray(inp["s5_d"][i], f32).T
        vec[:, V_DTAB + i * 48:V_DTAB + (i + 1) * 48] = np.tile(dt, (8, 1))
    cv = np.stack([_pk(inp["c"][b]), _pk(inp["c_ctx"])], axis=-1)
    vec[:, V_CVEC:V_CVEC + 16] = cv.reshape(128, 16)
    vec[:, V_CONSTS + 0] = EPS
    vec[:, V_CONSTS + 1] = 0.0
    return dict(
        xT=np.ascontiguousarray(np.asarray(inp["x"][b], f32).T),
        ctxT=np.ascontiguousarray(np.asarray(inp["ctx"][b], f32).T),
        vec=vec,
    )


def _prep_shared_inputs(inp):
    f32 = np.float32
    cs = _host_consts()
    sh = dict(cst=cs["cst"], cosL=cs["cosL"], nsinL=cs["nsinL"],
              cosC=cs["cosC"], nsinC=cs["nsinC"])
    for k in ("ada_w", "even_w_in", "even_w_out", "odd_w_in", "odd_w_out"):
        sh[k] = np.ascontiguousarray(np.asarray(inp[k], f32))
    sh["glu_w"] = np.ascontiguousarray(np.asarray(inp["s5_glu_w"], f32))
    sh["pool_w"] = np.ascontiguousarray(np.asarray(inp["pool_w"], f32).transpose(0, 2, 1, 3))
    fw = np.asarray(inp["fnet_w"], f32).reshape(2, 2, 2, 64, 64)
    sh["fnet_w"] = np.ascontiguousarray(fw.transpose(0, 2, 3, 1, 4).reshape(2, 128, 2, 64))
    def qmaj(a):
        return np.asarray(a, f32).reshape(96, 64).T
    s5s = np.zeros((2, 128, 3, 96), f32)
    s5bc = np.zeros((2, 128, 4, 96, 16), f32)
    for i in range(2):
        s5s[i, :, 0, :] = np.tile(qmaj(inp["s5_lam_re"][i]), (2, 1))
        s5s[i, :, 1, :] = np.tile(qmaj(inp["s5_lam_im"][i]), (2, 1))
        s5s[i, :, 2, :] = np.tile(np.asarray(inp["s5_log_step"][i], f32).reshape(1, 96), (128, 1))
        br = np.asarray(inp["s5_b_re"][i], f32).reshape(96, 64, 16).transpose(1, 0, 2)
        bi = np.asarray(inp["s5_b_im"][i], f32).reshape(96, 64, 16).transpose(1, 0, 2)
        cr = np.asarray(inp["s5_c_re"][i], f32).reshape(96, 16, 64).transpose(2, 0, 1)
        ci = np.asarray(inp["s5_c_im"][i], f32).reshape(96, 16, 64).transpose(2, 0, 1)
        for w, a in enumerate((br, bi, cr, ci)):
            s5bc[i, :, w] = np.tile(a, (2, 1, 1))
    sh["s5s"] = s5s
    sh["s5bc"] = s5bc
    return sh


def build_program(nlayers=DEPTH, final_norm=True, debug=False, stop=None):
    nc = bass.Bass("TRN2", target_bir_lowering=False)

    def din(name, shape, dt=F32):
        return nc.dram_tensor(name, list(shape), dt, kind="ExternalInput").ap()

    xT_d = din("xT", [D, T])
    ctxT_d = din("ctxT", [D, TC])
    vec_d = din("vec", [128, NVEC])
    cst_d = din("cst", [128, NCST])
    cosL_d = din("cosL", [128, 16, T], BF16)
    nsinL_d = din("nsinL", [128, 16, T], BF16)
    cosC_d = din("cosC", [128, 2, TC], BF16)
    nsinC_d = din("nsinC", [128, 2, TC], BF16)
    ada_w_d = din("ada_w", [DEPTH, D, 3 * D])
    ewin_d = din("even_w_in", [2, D, 2048])
    ewout_d = din("even_w_out", [2, D, D])
    owin_d = din("odd_w_in", [2, D, 3072])
    owout_d = din("odd_w_out", [2, D, D])
    glu_d = din("glu_w", [2, 768, 768])
    poolw_d = din("pool_w", [2, 128, 4, 128])
    fnetw_d = din("fnet_w", [2, 128, 2, 64])
    s5s_d = din("s5s", [2, 128, 3, 96])
    s5bc_d = din("s5bc", [2, 128, 4, 96, 16])
    outT_d = nc.dram_tensor("outT", [D, T], F32, kind="ExternalOutput").ap()
    skind = "ExternalOutput" if debug else "Internal"
    HLs = nc.dram_tensor("HLs", [128, KC, TT], BF16, kind=skind).ap()
    UGLs = nc.dram_tensor("UGLs", [128, 48, 288], BF16, kind=skind).ap()
    FNs = nc.dram_tensor("FNs", [128, 2, TT], BF16, kind=skind).ap()
    YGs = nc.dram_tensor("YGs", [128, 6, TT], BF16, kind=skind).ap()
    XLs = nc.dram_tensor("XLs", [128, KC, T], F32, kind="Internal").ap()

    with ExitStack() as st:
        E = st.enter_context
        XLA = E(nc.sbuf_tensor("XLA", [128, KC * T], F32))
        XC = E(nc.sbuf_tensor("XC", [128, KC, TC], F32))
        FA = E(nc.sbuf_tensor("FA", [128, 6144], F32))
        BA = E(nc.sbuf_tensor("BA", [128, 36864], BF16))
        VEC = E(nc.sbuf_tensor("VEC", [128, NVEC], F32))
        CST = E(nc.sbuf_tensor("CST", [128, NCST], F32))
        CB = E(nc.sbuf_tensor("CB", [128, 5, 128], BF16))
        ADA = E(nc.sbuf_tensor("ADA", [128, DEPTH, 24, 2], F32))
        AM = E(nc.sbuf_tensor("AM", [128, 2, 8], F32))
        SC = E(nc.sbuf_tensor("SC", [128, 8, 2], F32))
        IT = E(nc.sbuf_tensor("IT", [128, 96], I32))
        ADST = E(nc.sbuf_tensor("ADST", [128, 2, 1024], F32))
        WAB = E(nc.sbuf_tensor("WAB", [128, 2, 256], BF16))
        PWS = E(nc.sbuf_tensor("PWS", [128, 2, 4, 128], BF16))
        DG = E(nc.sbuf_tensor("DG", [128, 4, 3, 128], BF16))
        PS = [E(nc.psum_tensor("PS%d" % i, [128, 512], F32)) for i in range(6)]
        PSB = [E(nc.psum_tensor("PSB%d" % i, [128, 1024], BF16)) for i in range(2)]
        PSB_ADA = PS[3]
        sems = {e: E(nc.semaphore("s_" + e)) for e in Sched.ENG}
        dsems = [E(nc.semaphore("d%d" % i)) for i in range(NDMA_SLOTS)]

        S = Sched(nc)
        if isinstance(stop, int):
            S.limit = stop
        XL = XLA[:, :].rearrange("p (k t) -> p k t", k=KC)
        IDB = CB[:, 0, :]
        MSKF = CB[:, 1, :]
        MSKB = CB[:, 2, :]
        ONES = CB[:, 3, :]
        IDF = CST[:, C_ID:C_ID + 128]
        EPSC = VEC[:, V_CONSTS:V_CONSTS + 1]
        ZEROC = VEC[:, V_CONSTS + 1:V_CONSTS + 2]

        rr = {"ps": 0, "psb": 0, "w": 0, "cast": 0, "ev": 0}
        STQ = "act"
        LDQ = "pool"

        def chk(tag):
            if stop == tag:
                raise _Stop()

        def ps_next(lo=0, hi=None):
            if hi is None:
                hi = rr.get("hi", 4)
            i = lo + rr["ps"] % (hi - lo)
            rr["ps"] += 1
            return PS[i]

        def psb_next():
            i = rr["psb"] % 2
            rr["psb"] += 1
            return PSB[i]

        def ev_eng():
            return "dve"

        WST = [FA[:, i * 2048:(i + 1) * 2048] for i in range(2)]
        WBF = [BA[:, 34816 + i * 1024:34816 + (i + 1) * 1024] for i in range(2)]

        WBF = [BA[:, 36864 - 4096 + i * 2048:36864 - 4096 + (i + 1) * 2048] for i in range(2)]
        BA_FREE = 36864 - 4096

        class _W2:
            def __init__(self, halves):
                self.h = halves

            def __getitem__(self, key):
                p, k, c = key
                hidx = c.start // 128
                return self.h[hidx][p, k, c.start - 128 * hidx:c.stop - 128 * hidx]

        wsc = {}
        wseen = set()

        def _cache(key, kch, total_cols):
            if key not in wsc:
                wsc[key] = nc.dram_tensor("WSC_%s_%d" % key, [total_cols // 128, 128, kch * 128], BF16).ap()
            return wsc[key]

        def _load_half(wd, c0, ncols, kch, cast, hs, key=None):
            wb = WBF[hs // 2][:, (hs % 2) * 1024:(hs % 2) * 1024 + kch * ncols].rearrange("p (k c) -> p k c", k=kch)
            hidx = c0 // 128
            if cast and key is not None and (key, hidx) in wseen:
                cs = _cache(key, kch, wd.shape[1])
                S.dma(wb, cs[hidx].rearrange("p (k c) -> p k c", k=kch))
                return wb
            stg = WST[hs // 2][:, (hs % 2) * 1024:(hs % 2) * 1024 + kch * ncols].rearrange("p (k c) -> p k c", k=kch)
            S.dma(stg, wd[:, c0:c0 + ncols].rearrange("(k p) c -> p k c", p=128))
            if not cast:
                return stg
            rr["cast"] += 1
            ce = ("act", "act", "dve")[rr["cast"] % 3]
            S.copy(ce, wb, stg)
            if key is not None:
                cs = _cache(key, kch, wd.shape[1])
                S.dma(cs[hidx].rearrange("p (k c) -> p k c", k=kch), wb, eng=STQ)
                wseen.add((key, hidx))
            return wb

        def load_w(wd, c0, ncols=256, kch=KC, cast=True, split=True, key=None):
            if split and ncols == 256:
                hv = []
                for hh in range(2):
                    hs = rr["w"] % 4
                    rr["w"] += 1
                    hv.append(_load_half(wd, c0 + hh * 128, 128, kch, cast, hs, key))
                return _W2(hv)
            if rr["w"] % 2:
                rr["w"] += 1
            slot = (rr["w"] % 4) // 2
            rr["w"] += 2
            wb = WBF[slot][:, 0:kch * ncols].rearrange("p (k c) -> p k c", k=kch)
            h0 = c0 // 128
            if cast and key is not None and all((key, h0 + hh) in wseen for hh in range(ncols // 128)):
                cs = _cache(key, kch, wd.shape[1])
                for hh in range(ncols // 128):
                    S.dma(wb[:, :, hh * 128:(hh + 1) * 128], cs[h0 + hh].rearrange("p (k c) -> p k c", k=kch))
                return wb
            stg = WST[slot][:, 0:kch * ncols].rearrange("p (k c) -> p k c", k=kch)
            src = wd[:, c0:c0 + ncols].rearrange("(k p) c -> p k c", p=128)
            S.dma(stg, src)
            if not cast:
                return stg
            rr["cast"] += 1
            ce = ("act", "act", "dve")[rr["cast"] % 3]
            S.copy(ce, wb, stg)
            if key is not None:
                cs = _cache(key, kch, wd.shape[1])
                for hh in range(ncols // 128):
                    S.dma(cs[h0 + hh].rearrange("p (k c) -> p k c", k=kch), wb[:, :, hh * 128:(hh + 1) * 128], eng=STQ)
                    wseen.add((key, h0 + hh))
            return wb

        def nblocks(n, bs=512):
            out = []
            o = 0
            while o < n:
                out.append((o, min(bs, n - o)))
                o += bs
            return out

        FT = FA[:, 4096:6144]

        def make_hl(xsrc, hl_dst, n, am, sh, blocks, sqbuf):
            for (o, nb) in blocks:
                for k in range(KC):
                    S.act(sqbuf[:, k, 0:nb], xsrc(k, o, nb), AF.Square)
                pt = ps_next(4, 6)
                for k in range(KC):
                    S.mm(pt[:, 0:nb], ONES, sqbuf[:, k, 0:nb], start=(k == 0), stop=(k == KC - 1))
                rs = FT[:, 0:nb]
                S.act(rs, pt[:, 0:nb], AF.Sqrt, bias=EPSC, scale=1.0 / D)
                S.recip(rs, rs)
                for k in range(KC):
                    tmp = FT[:, 512 + (k % 2) * 512:512 + (k % 2) * 512 + nb]
                    S.tt("dve", tmp, xsrc(k, o, nb), rs, ALU.mult)
                    S.act(hl_dst(k, o, nb), tmp, AF.Identity, bias=sh[:, k:k + 1], scale=am[:, k:k + 1])

        RSL = ADST[:, :, :].rearrange("p a c -> p (a c)")
        RSC = FT[:, 1536:1792]

        def rs_of(kind, a, nb):
            return RSL[:, a:a + nb] if kind == "lat" else RSC[:, a:a + nb]

        def layer_stats(with_ctx, part=None):
            SQs = BA[:, 24576:24576 + 4096].rearrange("p (k c) -> p k c", k=KC)
            todo = [("lat", o, nb) for (o, nb) in nblocks(T)]
            if with_ctx:
                todo.append(("ctx", 0, TC))
            if part == 0:
                todo = todo[:2]
            elif part == 1:
                todo = todo[2:]
            for (kind, o, nb) in todo:
                xs = xview(kind)
                for k in range(KC):
                    S.act(SQs[:, k, 0:nb], xs(k, o, nb), AF.Square)
                pt = ps_next(4, 6)
                for k in range(KC):
                    S.mm(pt[:, 0:nb], ONES, SQs[:, k, 0:nb], start=(k == 0), stop=(k == KC - 1))
                rs = rs_of(kind, o, nb)
                S.act(rs, pt[:, 0:nb], AF.Sqrt, bias=EPSC, scale=1.0 / D)
                S.recip(rs, rs)

        def normalize(kind, hl_dst, am, sh, blocks):
            xs = xview(kind)
            for (o, nb) in blocks:
                for k in range(KC):
                    tmp = FT[:, 512 + (k % 2) * 512:512 + (k % 2) * 512 + nb]
                    S.tt("dve", tmp, xs(k, o, nb), rs_of(kind, o, nb), ALU.mult)
                    S.act(hl_dst(k, o, nb), tmp, AF.Identity, bias=sh[:, k:k + 1], scale=am[:, k:k + 1])

        ada_state = {}

        def ada_steps(l, n):
            if l >= nlayers:
                return
            st_ = ada_state.setdefault(l, 0)
            pt = PSB_ADA
            for jt in range(st_, min(24, st_ + n)):
                stg = ADST[:, jt % 2, :].rearrange("p (k c) -> p k c", k=KC)
                S.dma(stg, ada_w_d[l][:, jt * 128:(jt + 1) * 128].rearrange("(k p) c -> p k c", p=128))
                for k in range(KC):
                    S.mm(pt[:, l * 48 + jt * 2:l * 48 + jt * 2 + 2], stg[:, k, :], SC[:, k, :],
                         start=(k == 0), stop=(k == KC - 1))
            ada_state[l] = min(24, st_ + n)
            if ada_state[l] == 24 and st_ < 24:
                ab = VEC[:, V_ADAB + l * 24:V_ADAB + (l + 1) * 24]
                S.tt("dve", ADA[:, l, :, :], pt[:, l * 48:l * 48 + 48].rearrange("p (j w) -> p j w", w=2),
                     ab.unsqueeze(2).to_broadcast([128, 24, 2]), ALU.add)

        def ada_layer(l):
            pt = PS[5]
            for gidx in range(12):
                w = load_w(ada_w_d[l], gidx * 256, 256, KC, cast=False, split=False)
                for t2 in range(2):
                    jt = gidx * 2 + t2
                    for k in range(KC):
                        S.mm(pt[:, jt * 2:jt * 2 + 2], w[:, k, t2 * 128:(t2 + 1) * 128], SC[:, k, :],
                             start=(k == 0), stop=(k == KC - 1))
            ab = VEC[:, V_ADAB + l * 24:V_ADAB + (l + 1) * 24]
            S.tt("dve", ADA[:, l, :, :], pt[:, 0:48].rearrange("p (j w) -> p j w", w=2),
                 ab.unsqueeze(2).to_broadcast([128, 24, 2]), ALU.add)

        def layer_mod(l):
            ng = VEC[:, V_NORMG + l * 8:V_NORMG + (l + 1) * 8]
            for w in range(2):
                S.ts("dve", AM[:, w, :], ADA[:, l, 8:16, w], 1.0, None, ALU.add)
                S.tt("dve", AM[:, w, :], AM[:, w, :], ng, ALU.mult)

        pieces = [("lat", 0, 1024, 0), ("lat", 1024, 2048, 1024), ("ctx", 0, 256, 2048)]

        def xview(kind):
            if kind == "lat":
                return lambda k, a, n: XL[:, k, a:a + n]
            return lambda k, a, n: XC[:, k, a:a + n]

        def out_proj(wd, Yp, n, kind, s0, gate_w, l, wkey=None):
            for wg in range(4):
                w = load_w(wd, wg * 256, key=wkey)
                for t2 in range(2):
                    dt = wg * 2 + t2
                    for (o, nb) in nblocks(n):
                        pt = ps_next()
                        for k in range(KC):
                            S.mm(pt[:, 0:nb], w[:, k, t2 * 128:(t2 + 1) * 128], Yp[:, k, o:o + nb],
                                 start=(k == 0), stop=(k == KC - 1))
                        xv = xview(kind)(dt, s0 + o, nb)
                        S.stt("dve", xv, pt[:, 0:nb], ADA[:, l, 16 + dt, gate_w:gate_w + 1], xv, ALU.mult, ALU.add)

        S.dma(VEC[:, :], vec_d)
        S.dma(CST[:, :], cst_d)
        S.copy("dve", CB[:, 0, :], CST[:, C_ID:C_ID + 128])
        S.copy("dve", CB[:, 1, :], CST[:, C_MF:C_MF + 128])
        S.copy("dve", CB[:, 2, :], CST[:, C_MB:C_MB + 128])
        S.memset("dve", CB[:, 3, :], 1.0)
        S.act(SC[:, :, :], VEC[:, V_CVEC:V_CVEC + 16].rearrange("p (k w) -> p k w", w=2), AF.Silu)
        xv = xT_d.rearrange("(k p) t -> p k t", p=128)
        for half in range(2):
            hs_ = slice(half * 1024, (half + 1) * 1024)
            for k in range(KC):
                S.dma(XL[:, k, hs_], xv[:, k, hs_])
            for k in range(KC):
                xv3 = XL[:, k, hs_].rearrange("p (r c) -> p r c", c=64)
                if k < 4:
                    pr_ = CST[:, C_PROW + k * 32 + half * 16:C_PROW + k * 32 + half * 16 + 16]
                    pb = pr_.unsqueeze(2).to_broadcast([128, 16, 64])
                else:
                    pc_ = CST[:, C_PCOL + (k - 4) * 64:C_PCOL + (k - 4) * 64 + 64]
                    pb = pc_.unsqueeze(1).to_broadcast([128, 16, 64])
                S.tt("dve", xv3, xv3, pb, ALU.add)
            if half == 0:
                layer_stats(True, part=0)
                ada_layer(0)
        S.dma(XC[:, :, :], ctxT_d.rearrange("(k p) t -> p k t", p=128))

        def even_layer(l):
            i = l // 2
            need_ctx = NEED_CTX[l]
            layer_mod(l)
            fw = FT[:, 0:128].rearrange("p (t d) -> p t d", t=2)
            S.dma(fw, fnetw_d[i])
            S.memset("pool", WAB[:, :, :], 0.0)
            for t2 in range(2):
                for ab, cofs in ((0, C_BDC), (1, C_BDS)):
                    pt = ps_next(4, 6)
                    S.mm(pt[:, 0:64], CST[:, cofs:cofs + 128], fw[:, t2, :])
                    S.copy("act", WAB[0:64, t2, ab * 128:ab * 128 + 64], pt[0:64, 0:64])
                    S.copy("act", WAB[64:128, t2, ab * 128 + 64:ab * 128 + 128], pt[64:128, 0:64])
            ABtok = BA[:, 22528:22528 + 9216].rearrange("p (c f) -> p c f", f=512)
            assert 22528 + 9216 <= BA_FREE
            def e1_norm(pj):
                kind, s0, s1, goff = pieces[pj]
                n = s1 - s0
                w_ = 0 if kind == "lat" else 1
                HLp = BA[:, 0:KC * n].rearrange("p (k t) -> p k t", k=KC)
                normalize(kind, lambda k, a, nb, H=HLp, s0=s0: H[:, k, a - s0:a - s0 + nb],
                          AM[:, w_, :], ADA[:, l, 0:8, w_], [(s0 + o, nb) for (o, nb) in nblocks(n)])
                S.dma(HLs[:, :, goff:goff + n], HLp, eng=STQ)

            if l != 0:
                layer_stats(True, part=0)
            e1_norm(0)
            layer_stats(True, part=1)
            s5_main = None
            for pi, (kind, s0, s1, goff) in enumerate(pieces):
                last = (pi == len(pieces) - 1)
                if last:
                    S.dma(XLs, XL, eng=STQ)
                    s5_main = s5_phase(l, i)
                n = s1 - s0
                nk = n // 8
                w_ = 0 if kind == "lat" else 1
                HLp = BA[:, 0:KC * n].rearrange("p (k t) -> p k t", k=KC)
                PBLK = BA[:, 8192:8192 + 6144].rearrange("p (g j h) -> p g j h", g=48, j=8)
                UGp = BA[:, 14336:14336 + 48 * nk].rearrange("p (g c) -> p g c", g=48)
                UBp = BA[:, 20480:20480 + 2 * n].rearrange("p (t c) -> p t c", t=2)
                for wg in range(3):
                    w = load_w(ewin_d[i], wg * 256, split=False, key=("ewin", i))
                    for jp in range(4):
                        pt = ps_next()
                        for jj in range(2):
                            j = jp * 2 + jj
                            for k in range(KC):
                                S.mm(pt[0:nk, jj * 256:(jj + 1) * 256], HLp[:, k, j::8], w[:, k, :],
                                     start=(k == 0), stop=(k == KC - 1))
                        S.copy("act" if last else ("act", "dve")[jp % 2], PBLK[0:nk, wg * 16:(wg + 1) * 16, jp * 2:jp * 2 + 2, :],
                               pt[0:nk, 0:512].rearrange("p (j g h) -> p g j h", j=2, g=16))
                for g8 in range(6):
                    pb = psb_next()
                    for gi in range(8):
                        g = g8 * 8 + gi
                        S.tr(pb[:, gi * nk:(gi + 1) * nk], PBLK[0:nk, g, :, :].rearrange("p j h -> p (j h)"),
                             IDB[0:nk, 0:nk])
                    S.copy("act" if last else ("act", "dve")[g8 % 2], UGp[:, g8 * 8:(g8 + 1) * 8, :],
                           pb[:, 0:8 * nk].rearrange("p (g c) -> p g c", g=8))
                S.dma(UGLs[:, :, goff // 8:goff // 8 + nk], UGp, eng=STQ)
                if kind == "ctx" and not need_ctx:
                    continue
                w = load_w(ewin_d[i], 768, key=("ewin", i))
                for t2 in range(2):
                    for (o, nb) in nblocks(n):
                        pt = ps_next()
                        for k in range(KC):
                            S.mm(pt[:, 0:nb], w[:, k, t2 * 128:(t2 + 1) * 128], HLp[:, k, o:o + nb],
                                 start=(k == 0), stop=(k == KC - 1))
                        S.copy("act", UBp[:, t2, o:o + nb], pt[:, 0:nb])
                if pi + 1 < len(pieces):
                    e1_norm(pi + 1)
                for tc in range(n // 128):
                    pt = ps_next()
                    for t2 in range(2):
                        S.mm(pt[:, t2 * 256:(t2 + 1) * 256], UBp[:, t2, tc * 128:(tc + 1) * 128], WAB[:, t2, :])
                    S.copy("act" if last else ("dve", "act")[tc % 2], ABtok[:, goff // 128 + tc, :], pt[:, 0:512])
            chk("E1")
            FNb = BA[:, 0:2 * TT].rearrange("p (t c) -> p t c", t=2)
            DFTB = [BA[:, 4608 + s * 8192:4608 + (s + 1) * 8192] for s in range(2)]
            nrmL = 1.0 / math.sqrt(T * 64.0)
            nrmC = 1.0 / math.sqrt(TC * 64.0)
            for kb in range(8):
                db = DFTB[kb % 2]
                cb = db[:, 0:4096].rearrange("p (c k) -> p c k", c=16)
                sb = db[:, 4096:8192].rearrange("p (c k) -> p c k", c=16)
                S.dma(cb, cosL_d[:, :, kb * 256:(kb + 1) * 256])
                S.dma(sb, nsinL_d[:, :, kb * 256:(kb + 1) * 256])
                for t2 in range(2):
                    pt = ps_next()
                    for tc in range(16):
                        S.mm(pt[:, 0:256], ABtok[:, tc, t2 * 256:t2 * 256 + 128], cb[:, tc, :],
                             start=(tc == 0), stop=False)
                        S.mm(pt[:, 0:256], ABtok[:, tc, t2 * 256 + 128:t2 * 256 + 256], sb[:, tc, :],
                             start=False, stop=(tc == 15))
                    S.act(FNb[:, t2, kb * 256:(kb + 1) * 256], pt[:, 0:256], AF.Copy, scale=nrmL)
            if need_ctx:
                db = DFTB[0]
                cb = db[:, 0:512].rearrange("p (c k) -> p c k", c=2)
                sb = db[:, 512:1024].rearrange("p (c k) -> p c k", c=2)
                S.dma(cb, cosC_d)
                S.dma(sb, nsinC_d)
                for t2 in range(2):
                    pt = ps_next()
                    for tc in range(2):
                        S.mm(pt[:, 0:256], ABtok[:, 16 + tc, t2 * 256:t2 * 256 + 128], cb[:, tc, :],
                             start=(tc == 0), stop=False)
                        S.mm(pt[:, 0:256], ABtok[:, 16 + tc, t2 * 256 + 128:t2 * 256 + 256], sb[:, tc, :],
                             start=False, stop=(tc == 1))
                    S.act(FNb[:, t2, T:T + TC], pt[:, 0:256], AF.Copy, scale=nrmC)
                S.dma(FNs, FNb, eng=STQ)
            else:
                S.dma(FNs[:, :, 0:T], FNb[:, :, 0:T], eng=STQ)

            chk("E2")
            s5_main()
            rr["hi"] = 4
            chk("E3")

            for pi, (kind, s0, s1, goff) in enumerate(pieces):
                if kind == "ctx" and not need_ctx:
                    continue
                n = s1 - s0
                w_ = 0 if kind == "lat" else 1
                HLp = BA[:, 0:KC * n].rearrange("p (k t) -> p k t", k=KC)
                YGp = BA[:, 8192:8192 + 6 * n].rearrange("p (k t) -> p k t", k=6)
                FNp = BA[:, 14336:14336 + 2 * n].rearrange("p (k t) -> p k t", k=2)
                Yp = BA[:, 16384:16384 + KC * n].rearrange("p (k t) -> p k t", k=KC)
                TMP = [BA[:, 24576 + s * 512:24576 + (s + 1) * 512] for s in range(4)]
                if kind == "lat":
                    S.dma(XL[:, :, s0:s1], XLs[:, :, s0:s1], eng=LDQ)
                S.dma(YGp, YGs[:, :, goff:goff + n], eng=LDQ)
                S.dma(HLp, HLs[:, :, goff:goff + n], eng=LDQ)
                S.dma(FNp, FNs[:, :, goff:goff + n], eng=LDQ)
                tctr = 0
                for gw in range(3):
                    w = load_w(glu_d[i], gw * 256, 256, 6, key=("glu", i))
                    for t2 in range(2):
                        cp = gw * 2 + t2
                        for (o, nb) in nblocks(n):
                            pt = ps_next()
                            for k in range(6):
                                S.mm(pt[:, 0:nb], w[:, k, t2 * 128:(t2 + 1) * 128], YGp[:, k, o:o + nb],
                                     start=(k == 0), stop=(k == 5))
                            sg = TMP[tctr % 4][:, 0:nb]
                            tctr += 1
                            S.act(sg, pt[:, 0:nb], AF.Sigmoid, bias=VEC[:, V_GLUB + i * 6 + cp:V_GLUB + i * 6 + cp + 1])
                            S.tt(ev_eng(), Yp[:, cp, o:o + nb], YGp[:, cp, o:o + nb], sg, ALU.mult)
                for wg in range(4):
                    w = load_w(ewin_d[i], 1024 + wg * 256, key=("ewin", i))
                    for t2 in range(2):
                        zt = wg * 2 + t2
                        for (o, nb) in nblocks(n):
                            pt = ps_next()
                            for k in range(KC):
                                S.mm(pt[:, 0:nb], w[:, k, t2 * 128:(t2 + 1) * 128], HLp[:, k, o:o + nb],
                                     start=(k == 0), stop=(k == KC - 1))
                            zs = TMP[tctr % 4][:, 0:nb]
                            tctr += 1
                            S.act(zs, pt[:, 0:nb], AF.Silu)
                            src = Yp[:, zt, o:o + nb] if zt < 6 else FNp[:, zt - 6, o:o + nb]
                            S.tt(ev_eng(), Yp[:, zt, o:o + nb], src, zs, ALU.mult)
                out_proj(ewout_d[i], Yp, n, kind, s0, w_, l, wkey=("ewout", i))

        def s5_phase(l, i):
            Z = XLA[:, 0:6912].rearrange("p (q c) -> p q c", q=96)
            RAW = XLA[:, 0:6144].rearrange("p (w q h) -> p w q h", w=4, q=96)
            SS = XLA[:, 6144:6144 + 288].rearrange("p (w q) -> p w q", w=3)
            E1r = XLA[:, 6912:8448].rearrange("p (q k) -> p q k", q=96)
            E1i = XLA[:, 8448:9984].rearrange("p (q k) -> p q k", q=96)
            E8r = XLA[:, 9984:10464].rearrange("p (q k) -> p q k", q=96)
            E8i = XLA[:, 10464:10944].rearrange("p (q k) -> p q k", q=96)
            NAIs = XLA[:, 10944:11040]
            SM = [XLA[:, 11040 + s_ * 96:11040 + (s_ + 1) * 96] for s_ in range(16)]
            TQ = XLA[:, 12576:14112].rearrange("p (q h) -> p q h", q=96)
            tw = [XLA[:, 14112 + s_ * 384:14112 + (s_ + 1) * 384] for s_ in range(2)]
            X1b = FA[:, 0:1536].rearrange("p (q h) -> p q h", q=96)
            X2b = FA[:, 1536:3072].rearrange("p (q h) -> p q h", q=96)
            X1c = FA[:, 3072:4608].rearrange("p (q h) -> p q h", q=96)
            X2c = FA[:, 4608:6144].rearrange("p (q h) -> p q h", q=96)
            S.dma(RAW, s5bc_d[i], eng=LDQ)
            S.dma(SS, s5s_d[i], eng=LDQ)
            LR, LI, LS = SS[:, 0, :], SS[:, 1, :], SS[:, 2, :]
            (step, xr, xi, mag, sn, cs, t0, t1, t2, lbr, lbi, qr, qi, t3, t4, t5) = SM
            dv = "dve"
            S.act(step, LS, AF.Exp)
            S.tt(dv, xr, LR, step, ALU.mult)
            S.tt(dv, xi, LI, step, ALU.mult)
            S.act(mag, xr, AF.Exp)
            PI2 = 2.0 * math.pi

            def sin_of(dst, src, shift):
                S.ts(dv, t0, src, 1.0 / PI2, shift / PI2, ALU.mult, ALU.add)
                S.copy(dv, IT[:, :], t0)
                S.copy(dv, t1, IT[:, :])
                S.ts(dv, t2, src, shift, None, ALU.add)
                S.stt(dv, t2, t1, -PI2, t2, ALU.mult, ALU.add)
                S.ts(dv, t2, t2, -3.14159, 3.14159, ALU.max, ALU.min)
                S.act(dst, t2, AF.Sin)

            sin_of(sn, xi, 0.0)
            sin_of(cs, xi, math.pi / 2.0)
            S.tt(dv, lbr, mag, cs, ALU.mult)
            S.tt(dv, lbi, mag, sn, ALU.mult)
            S.ts(dv, t0, lbr, -1.0, None, ALU.add)
            S.tt(dv, t1, LR, LR, ALU.mult)
            S.tt(dv, t2, LI, LI, ALU.mult)
            S.tt(dv, t1, t1, t2, ALU.add)
            S.recip(t1, t1)
            S.tt(dv, t2, t0, LR, ALU.mult)
            S.tt(dv, t3, lbi, LI, ALU.mult)
            S.tt(dv, t2, t2, t3, ALU.add)
            S.tt(dv, qr, t2, t1, ALU.mult)
            S.tt(dv, t2, lbi, LR, ALU.mult)
            S.tt(dv, t3, t0, LI, ALU.mult)
            S.tt(dv, t2, t2, t3, ALU.subtract)
            S.tt(dv, qi, t2, t1, ALU.mult)
            BRr, BIr, CRr, CIr = RAW[:, 0], RAW[:, 1], RAW[:, 2], RAW[:, 3]
            lo, hi = slice(0, 64), slice(64, 128)

            def bq(a2, psl):
                return a2[psl].unsqueeze(2).to_broadcast([64, 96, 16])

            S.tt(dv, X1b[lo], BRr[lo], bq(qr, lo), ALU.mult)
            S.tt(dv, TQ[lo], BIr[lo], bq(qi, lo), ALU.mult)
            S.tt(dv, X1b[lo], X1b[lo], TQ[lo], ALU.subtract)
            S.tt(dv, X1b[hi], BIr[hi], bq(qr, hi), ALU.mult)
            S.tt(dv, TQ[hi], BRr[hi], bq(qi, hi), ALU.mult)
            S.tt(dv, X1b[hi], X1b[hi], TQ[hi], ALU.add)
            S.tt(dv, X2b[lo], BIr[lo], bq(qr, lo), ALU.mult)
            S.tt(dv, TQ[lo], BRr[lo], bq(qi, lo), ALU.mult)
            S.tt(dv, X2b[lo], X2b[lo], TQ[lo], ALU.add)
            S.ts(dv, X2b[lo], X2b[lo], -1.0, None, ALU.mult)
            S.tt(dv, X2b[hi], BRr[hi], bq(qr, hi), ALU.mult)
            S.tt(dv, TQ[hi], BIr[hi], bq(qi, hi), ALU.mult)
            S.tt(dv, X2b[hi], X2b[hi], TQ[hi], ALU.subtract)
            S.act(X1c[lo], CRr[lo], AF.Copy)
            S.act(X1c[hi], CIr[hi], AF.Copy, scale=-1.0)
            S.act(X2c[lo], CIr[lo], AF.Copy, scale=-1.0)
            S.act(X2c[hi], CRr[hi], AF.Copy, scale=-1.0)

            def cmul(eng, outr, outi, ar, ai, br, bi, ta, tb):
                S.tt(eng, ta, ar, br, ALU.mult)
                S.tt(eng, tb, ai, bi, ALU.mult)
                S.tt(eng, outr, ta, tb, ALU.subtract)
                S.tt(eng, ta, ar, bi, ALU.mult)
                S.tt(eng, tb, ai, br, ALU.mult)
                S.tt(eng, outi, ta, tb, ALU.add)

            def bc(ap2, n_):
                return ap2.unsqueeze(2).to_broadcast([128, 96, n_])

            def twv(s_, n_):
                return tw[s_][:, 0:96 * n_].rearrange("p (q k) -> p q k", q=96)

            S.memset(dv, E1r[:, :, 7:8], 1.0)
            S.memset(dv, E1i[:, :, 7:8], 0.0)
            S.copy(dv, E1r[:, :, 8], lbr)
            S.copy(dv, E1i[:, :, 8], lbi)
            cmul(dv, E1r[:, :, 9:10], E1i[:, :, 9:10], E1r[:, :, 8:9], E1i[:, :, 8:9], E1r[:, :, 8:9], E1i[:, :, 8:9],
                 twv(0, 1), twv(1, 1))
            cmul(dv, E1r[:, :, 10:12], E1i[:, :, 10:12], E1r[:, :, 8:10], E1i[:, :, 8:10],
                 bc(E1r[:, :, 9], 2), bc(E1i[:, :, 9], 2), twv(0, 2), twv(1, 2))
            cmul(dv, E1r[:, :, 12:16], E1i[:, :, 12:16], E1r[:, :, 8:12], E1i[:, :, 8:12],
                 bc(E1r[:, :, 11], 4), bc(E1i[:, :, 11], 4), twv(0, 4), twv(1, 4))
            S.tt(dv, t0, mag, mag, ALU.mult)
            S.recip(t0, t0)
            S.tt(dv, E1r[:, :, 6], lbr, t0, ALU.mult)
            S.tt(dv, t1, lbi, t0, ALU.mult)
            S.ts(dv, E1i[:, :, 6], t1, -1.0, None, ALU.mult)
            cmul(dv, E1r[:, :, 5:6], E1i[:, :, 5:6], E1r[:, :, 6:7], E1i[:, :, 6:7], E1r[:, :, 6:7], E1i[:, :, 6:7],
                 twv(0, 1), twv(1, 1))
            cmul(dv, E1r[:, :, 3:5], E1i[:, :, 3:5], E1r[:, :, 5:7], E1i[:, :, 5:7],
                 bc(E1r[:, :, 5], 2), bc(E1i[:, :, 5], 2), twv(0, 2), twv(1, 2))
            cmul(dv, E1r[:, :, 0:3], E1i[:, :, 0:3], E1r[:, :, 4:7], E1i[:, :, 4:7],
                 bc(E1r[:, :, 3], 3), bc(E1i[:, :, 3], 3), twv(0, 3), twv(1, 3))
            S.memset(dv, E8r[:, :, 0:1], 1.0)
            S.memset(dv, E8i[:, :, 0:1], 0.0)
            S.copy(dv, E8r[:, :, 1], E1r[:, :, 15])
            S.copy(dv, E8i[:, :, 1], E1i[:, :, 15])
            cmul(dv, E8r[:, :, 2:3], E8i[:, :, 2:3], E8r[:, :, 1:2], E8i[:, :, 1:2], E8r[:, :, 1:2], E8i[:, :, 1:2],
                 twv(0, 1), twv(1, 1))
            cmul(dv, E8r[:, :, 3:4], E8i[:, :, 3:4], E8r[:, :, 2:3], E8i[:, :, 2:3], E8r[:, :, 1:2], E8i[:, :, 1:2],
                 twv(0, 1), twv(1, 1))
            cmul(dv, E8r[:, :, 4:5], E8i[:, :, 4:5], E8r[:, :, 2:3], E8i[:, :, 2:3], E8r[:, :, 2:3], E8i[:, :, 2:3],
                 twv(0, 1), twv(1, 1))
            S.ts(dv, NAIs[0:64, :], E8i[0:64, :, 4], -1.0, None, ALU.mult)
            S.copy(dv, NAIs[64:128, :], E8i[64:128, :, 4])

            SPf = BA[:, 0:6912].rearrange("p (q c) -> p q c", q=96)
            UGb = [BA[:, 6912 + s_ * 1152:6912 + (s_ + 1) * 1152].rearrange("p (g c) -> p g c", g=4) for s_ in range(2)]
            o_ = 9216
            WZT = [BA[:, o_ + s_ * 2048:o_ + (s_ + 1) * 2048].rearrange("p (g f) -> p g f", g=4) for s_ in range(4)]
            WZ = [BA[:, o_ + 8192 + s_ * 512:o_ + 8192 + (s_ + 1) * 512].rearrange("p (a f) -> p a f", a=4)
                  for s_ in range(2)]
            TBA = [BA[:, o_ + 9216 + s_ * 2048:o_ + 9216 + (s_ + 1) * 2048] for s_ in range(4)]
            assert o_ + 9216 + 8192 <= 36864
            RB = [BA[:, o_ + s_ * 2048:o_ + (s_ + 1) * 2048].rearrange("p (g f) -> p g f", g=4) for s_ in range(4)]
            o_ += 8192
            QB = [BA[:, o_ + s_ * 512:o_ + (s_ + 1) * 512].rearrange("p (g f) -> p g f", g=4) for s_ in range(4)]
            o_ += 2048
            MB_ = [BA[:, o_ + s_ * 2048:o_ + (s_ + 1) * 2048].rearrange("p (g f) -> p g f", g=4) for s_ in range(2)]
            o_ += 4096
            PG = BA[:, o_:o_ + 4096].rearrange("p (t g h) -> p t g h", t=32, g=8)
            o_ += 4096
            YGT = [BA[:, o_:o_ + TT]]
            o_ += TT
            TBB = [BA[:, o_ + 1024 + s_ * 2048:o_ + 1024 + (s_ + 1) * 2048] for s_ in range(2)]
            assert o_ + 1024 + 4096 <= 36864, o_
            PTs = [XLA[:, 11040 + s_ * 128:11040 + (s_ + 1) * 128] for s_ in range(8)]
            BIG = [XLA[:, 12064 + s_ * 2048:12064 + (s_ + 1) * 2048] for s_ in range(2)]
            SCN = [XLA[:, 12064 + s_ * 48:12064 + (s_ + 1) * 48] for s_ in range(8)]
            QTMP = [XLA[:, s_ * 512:(s_ + 1) * 512] for s_ in range(2)]
            GB = 4

            def ptable(eng, slot, q0, e8sl, e1sl):
                def v4(t_):
                    return t_[:, 0:GB * 32].rearrange("p (g x y) -> p g x y", g=GB, x=4)
                pr = v4(PTs[slot * 2])
                pi_ = v4(PTs[slot * 2 + 1])
                a_r = E8r[:, q0:q0 + GB, e8sl].unsqueeze(3).to_broadcast([128, GB, 4, 8])
                a_i = E8i[:, q0:q0 + GB, e8sl].unsqueeze(3).to_broadcast([128, GB, 4, 8])
                b_r = E1r[:, q0:q0 + GB, e1sl].unsqueeze(2).to_broadcast([128, GB, 4, 8])
                b_i = E1i[:, q0:q0 + GB, e1sl].unsqueeze(2).to_broadcast([128, GB, 4, 8])
                cmul(eng, pr, pi_, a_r, a_i, b_r, b_i, v4(PTs[4 + slot * 2]), v4(PTs[5 + slot * 2]))
                return (PTs[slot * 2][:, 0:GB * 32].rearrange("p (g k) -> p g k", g=GB),
                        PTs[slot * 2 + 1][:, 0:GB * 32].rearrange("p (g k) -> p g k", g=GB))

            def bigtable(eng, out_bf, pr, pi_, X1, X2, q0, nk, ta_, tb_, eng2=None):
                n_ = GB * nk * 16
                ta = ta_[:, 0:n_].rearrange("p (g k h) -> p g k h", g=GB, k=nk)
                tb = tb_[:, 0:n_].rearrange("p (g k h) -> p g k h", g=GB, k=nk)
                prb = pr.unsqueeze(3).to_broadcast([128, GB, nk, 16])
                pib = pi_.unsqueeze(3).to_broadcast([128, GB, nk, 16])
                x1 = X1[:, q0:q0 + GB, :].unsqueeze(2).to_broadcast([128, GB, nk, 16])
                x2 = X2[:, q0:q0 + GB, :].unsqueeze(2).to_broadcast([128, GB, nk, 16])
                S.tt(eng, ta, prb, x1, ALU.mult)
                if eng2 is None or eng2 == eng:
                    S.tt(eng, tb, pib, x2, ALU.mult)
                else:
                    hg = GB // 2
                    S.tt(eng2, tb[:, 0:hg], pib[:, 0:hg], x2[:, 0:hg], ALU.mult)
                    S.tt(eng, tb[:, hg:GB], pib[:, hg:GB], x2[:, hg:GB], ALU.mult)
                S.tt(eng, out_bf[:, :, 0:nk * 16].rearrange("p g (k h) -> p g k h", k=nk), ta, tb, ALU.add)

            def tbl_mults(out_bf, pr, pi_, X1, X2, q0, tb_, eng_b):
                nk = 32
                o4 = out_bf[:, :, 0:512].rearrange("p g (k h) -> p g k h", k=nk)
                tb = tb_[:, 0:GB * 512].rearrange("p (g k h) -> p g k h", g=GB, k=nk)
                prb = pr.unsqueeze(3).to_broadcast([128, GB, nk, 16])
                pib = pi_.unsqueeze(3).to_broadcast([128, GB, nk, 16])
                x1 = X1[:, q0:q0 + GB, :].unsqueeze(2).to_broadcast([128, GB, nk, 16])
                x2 = X2[:, q0:q0 + GB, :].unsqueeze(2).to_broadcast([128, GB, nk, 16])
                S.tt("dve", o4, prb, x1, ALU.mult)
                S.tt(eng_b, tb, pib, x2, ALU.mult)

            def tbl_add(out_bf, tb_):
                o2 = out_bf[:, :, 0:512]
                S.tt("dve", o2, o2, tb_[:, 0:GB * 512].rearrange("p (g f) -> p g f", g=GB), ALU.add)

            DESC4 = slice(3, None, -1)

            def _main():
                rr["hi"] = 3
                chk("E3prep")
                for gb in range(12):
                    g0 = gb * GB
                    ug = UGb[gb % 2]
                    S.dma(ug, UGLs[:, g0:g0 + GB, :])
                    for dr in range(2):
                        q0 = dr * 48 + g0
                        if dr == 0:
                            pr, pi_ = ptable("dve", dr, q0, DESC4, slice(15, 7, -1))
                        else:
                            pr, pi_ = ptable("dve", dr, q0, slice(1, 5), slice(7, 15))
                        tbl_mults(WZT[(gb % 2) * 2 + dr], pr, pi_, X1b, X2b, q0, TBA[(gb % 2) * 2 + dr], "dve")
                    for dr in range(2):
                        q0 = dr * 48 + g0
                        wzt = WZT[(gb % 2) * 2 + dr]
                        tbl_add(wzt, TBA[(gb % 2) * 2 + dr])
                        pz = PS[4 + dr]
                        for gl in range(GB):
                            pb = psb_next()
                            for a in range(4):
                                S.tr(pb[:, a * 128:(a + 1) * 128], wzt[:, gl, a * 128:(a + 1) * 128], IDB)
                            wz = WZ[gl % 2]
                            S.copy("act", wz, pb[:, 0:512].rearrange("p (a f) -> p a f", a=4))
                            for a in range(4):
                                S.mm(pz[:, gl * 72:(gl + 1) * 72], wz[:, a, :], ug[:, gl, a::4],
                                     start=(a == 0), stop=(a == 3))
                        S.copy("act", Z[:, q0:q0 + GB, :], pz[:, 0:GB * 72].rearrange("p (g c) -> p g c", g=GB))
                    ada_steps(l + 2, 2)

                chk("E3A")
                orders = [list(range(64, 72)) + list(range(0, 64)),
                          list(range(71, 63, -1)) + list(range(63, -1, -1))]
                prevs = [None, None]
                for st_i in range(NCH):
                    for dr in range(2):
                        eng = ("dve", "pool")[dr]
                        qs = slice(dr * 48, (dr + 1) * 48)
                        AR = E8r[:, qs, 4]
                        AIs_ = NAIs[:, qs]
                        SBu = SCN[dr * 4 + 0]
                        tA = SCN[dr * 4 + 1]
                        tB = SCN[dr * 4 + 2]
                        c = orders[dr][st_i]
                        prev = prevs[dr]
                        if prev is not None:
                            P = Z[:, qs, prev]
                            ce_ = "act" if dr == 1 else eng
                            S.copy(ce_, SBu[0:64, :], Z[64:128, qs, prev])
                            S.copy(ce_, SBu[64:128, :], Z[0:64, qs, prev])
                            S.tt(eng, tA, AR, P, ALU.mult)
                            S.tt(eng, tB, AIs_, SBu, ALU.mult)
                            S.tt(eng, tA, tA, tB, ALU.add)
                            S.tt(eng, Z[:, qs, c], Z[:, qs, c], tA, ALU.add)
                        prevs[dr] = c
                S.memset("dve", SPf[:, 0:48, 64:65], 0.0)
                S.copy("dve", SPf[:, 0:48, 65:72], Z[:, 0:48, 64:71])
                S.copy("dve", SPf[:, 0:48, 0:1], Z[:, 0:48, 71:72])
                S.copy("dve", SPf[:, 0:48, 1:64], Z[:, 0:48, 0:63])
                S.memset("pool", SPf[:, 48:96, 71:72], 0.0)
                S.copy("act", SPf[:, 48:96, 64:71], Z[:, 48:96, 65:72])
                S.copy("act", SPf[:, 48:96, 0:64], Z[:, 48:96, 1:65])

                ada_steps(l + 1, 24)
                chk("E3S")
                dtab = VEC[:, V_DTAB + i * 48:V_DTAB + (i + 1) * 48]
                DGD = BA[:, o_:o_ + 1024].rearrange("p (g f) -> p g f", g=8)
                assert o_ + 1024 <= 36864

                def gen_tables(n_):
                    g0 = n_ * GB
                    sl = n_ % 2
                    S.dma(UGb[sl], UGLs[:, g0:g0 + GB, :])
                    for dr in range(2):
                        q0 = dr * 48 + g0
                        bs = sl * 2 + dr
                        rs_ = (n_ % 2) * 2 + dr
                        pr = PTRr[:, q0:q0 + GB, :]
                        pi_ = PTRi[:, q0:q0 + GB, :]
                        tbl_mults(RB[rs_], pr, pi_, X1c, X2c, q0, TBB[dr], "dve")
                    for dr in range(2):
                        q0 = dr * 48 + g0
                        bs = sl * 2 + dr
                        rs_ = (n_ % 2) * 2 + dr
                        e1 = slice(7, None, -1) if dr == 0 else slice(7, 15)
                        qe = "dve"
                        bigtable(qe, QB[bs], E1r[:, q0:q0 + GB, e1], E1i[:, q0:q0 + GB, e1], X1b, X2b, q0, 8,
                                 QTMP[dr * 2], QTMP[dr * 2 + 1])
                    for dr in range(2):
                        rs_ = (n_ % 2) * 2 + dr
                        tbl_add(RB[rs_], TBB[dr])

                def m_chain(n_):
                    sl = n_ % 2
                    for dr in range(2):
                        bs = sl * 2 + dr
                        rs_ = (n_ % 2) * 2 + dr
                        for gl in range(GB):
                            pm = ps_next()
                            S.mm(pm[:, 0:512], QB[bs][:, gl, :], RB[rs_][:, gl, :])
                            mdst = MB_[dr][:, gl, :]
                            S.copy("act", mdst[:, 0:512], pm[:, 0:512])
                            if dr == 0:
                                S.tt("dve", mdst[:, 0:128], mdst[:, 0:128], MSKF, ALU.mult)
                            else:
                                S.tt("dve", mdst[:, 384:512], mdst[:, 384:512], MSKB, ALU.mult)

                def y_part(n_):
                    g0 = n_ * GB
                    sl = n_ % 2
                    ug = UGb[sl]
                    hb = n_ % 2
                    for gl in range(GB):
                        g = g0 + gl
                        gi = hb * GB + gl
                        py = PS[4 + g % 2]
                        bf_, bb_ = 0, 1
                        rf_, rb_ = (n_ % 2) * 2 + 0, (n_ % 2) * 2 + 1
                        S.mm(py[0:NCH, 0:512], SPf[:, g, :], RB[rf_][:, gl, :], start=True, stop=False)
                        S.mm(py[0:NCH, 0:512], SPf[:, 48 + g, :], RB[rb_][:, gl, :], start=False, stop=False)
                        for a2 in range(4):
                            lh = ug[:, gl, a2::4]
                            S.mm(py[0:NCH, a2 * 128:(a2 + 1) * 128], lh, DGD[:, gi, :], start=False, stop=False)
                            S.mm(py[0:NCH, a2 * 128:512], lh, MB_[bf_][:, gl, 0:(4 - a2) * 128], start=False, stop=False)
                            S.mm(py[0:NCH, 0:(a2 + 1) * 128], lh, MB_[bb_][:, gl, (3 - a2) * 128:512],
                                 start=False, stop=(a2 == 3))
                        S.act(PG[0:NCH, :, gi, :], py[0:NCH, 0:512].rearrange("p (t h) -> p t h", t=32),
                              AF.Gelu_apprx_tanh)

                PTRr = XLA[:, 0:3072].rearrange("p (q k) -> p q k", q=96)
                PTRi = XLA[:, 3072:6144].rearrange("p (q k) -> p q k", q=96)
                ptmp = [XLA[:, 12064 + s_ * 1536:12064 + (s_ + 1) * 1536] for s_ in range(2)]

                def v4f(t_, q0_):
                    return t_[:, q0_:q0_ + 48, :].rearrange("p q (x y) -> p q x y", x=4)

                def pv(t_):
                    return t_[:, 0:1536].rearrange("p (q x y) -> p q x y", q=48, x=4)

                for dr, (e8sl, e1sl) in enumerate(((slice(0, 4), slice(7, 15)), (DESC4, slice(7, None, -1)))):
                    q0_ = dr * 48
                    a_r = E8r[:, q0_:q0_ + 48, e8sl].unsqueeze(3).to_broadcast([128, 48, 4, 8])
                    a_i = E8i[:, q0_:q0_ + 48, e8sl].unsqueeze(3).to_broadcast([128, 48, 4, 8])
                    b_r = E1r[:, q0_:q0_ + 48, e1sl].unsqueeze(2).to_broadcast([128, 48, 4, 8])
                    b_i = E1i[:, q0_:q0_ + 48, e1sl].unsqueeze(2).to_broadcast([128, 48, 4, 8])
                    cmul("dve", v4f(PTRr, q0_), v4f(PTRi, q0_), a_r, a_i, b_r, b_i, pv(ptmp[0]), pv(ptmp[1]))
                QTMP = [XLA[:, 12064 + s_ * 512:12064 + (s_ + 1) * 512] for s_ in range(4)]
                gen_tables(0)
                for n_ in range(12):
                    ct = n_ // 2
                    if n_ % 2 == 0:
                        for gi in range(8):
                            S.act(DGD[:, gi, :], IDF, AF.Copy, scale=dtab[:, ct * 8 + gi:ct * 8 + gi + 1])
                    m_chain(n_)
                    if n_ + 1 < 12:
                        gen_tables(n_ + 1)
                    chk("B_tab")
                    y_part(n_)
                    if n_ % 2 == 1:
                        ygt = YGT[0]
                        ygv = ygt.rearrange("p (c t) -> p t c", t=32)
                        for t0_, nt in ((0, 14), (14, 14), (28, 4)):
                            pb = psb_next()
                            for tk in range(nt):
                                S.tr(pb[:, tk * NCH:(tk + 1) * NCH],
                                     PG[0:NCH, t0_ + tk, :, :].rearrange("p g h -> p (g h)"), IDB[0:NCH, 0:NCH])
                            S.copy(("act", "dve")[(t0_ // 14) % 2], ygv[:, t0_:t0_ + nt, :],
                                   pb[:, 0:nt * NCH].rearrange("p (t c) -> p t c", t=nt))
                        S.dma(YGs[:, ct, :], ygt, eng=STQ)

            return _main

        def odd_layer(l):
            i = l // 2
            need_ctx = NEED_CTX[l]
            layer_mod(l)
            pw = FT[:, 0:512].rearrange("p (t d) -> p t d", t=4)
            S.dma(pw, poolw_d[i])
            for t_ in range(4):
                S.ts("dve", PWS[:, 0, t_, :], pw[:, t_, :], 1.0 / POOL_W[t_], None, ALU.mult)
                S.ts("dve", PWS[:, 1, t_, :], pw[:, t_, :], -1.0, None, ALU.mult)
                for k in range(3):
                    col = V_CONVW + (i * 3 + k) * 4 + t_
                    S.act(DG[:, t_, k, :], IDF, AF.Copy, scale=VEC[:, col:col + 1])
            act_pieces = [pj for pj, pc in enumerate(pieces) if not (pc[0] == "ctx" and not need_ctx)]

            def odd_norm(pj):
                kind, s0, s1, goff = pieces[pj]
                n = s1 - s0
                w_ = 0 if kind == "lat" else 1
                hl_ = 8 if (kind == "lat" and s0 > 0) else 0
                hr_ = 8 if (kind == "lat" and s1 < T) else 0
                ne = n + hl_ + hr_
                e0 = s0 - hl_
                HLe = BA[:, 0:KC * ne].rearrange("p (k t) -> p k t", k=KC)
                HALO = BA[:, 29184:29248].rearrange("p (k c) -> p k c", k=KC)
                hb0 = e0 + hl_
                normalize(kind, lambda k, a, nb, H=HLe, e0=e0: H[:, k, a - e0:a - e0 + nb],
                          AM[:, w_, :], ADA[:, l, 0:8, w_], [(hb0 + o, nb) for (o, nb) in nblocks(ne - hl_)])
                if kind == "lat" and s0 == 0:
                    S.copy("pool", HALO, HLe[:, :, 1016:1024])
                if hl_:
                    S.copy("pool", HLe[:, :, 0:hl_], HALO)

            layer_stats(need_ctx)
            for pi, (kind, s0, s1, goff) in enumerate(pieces):
                if kind == "ctx" and not need_ctx:
                    continue
                n = s1 - s0
                w_ = 0 if kind == "lat" else 1
                hl_ = 8 if (kind == "lat" and s0 > 0) else 0
                hr_ = 8 if (kind == "lat" and s1 < T) else 0
                ne = n + hl_ + hr_
                e0 = s0 - hl_
                left_edge = (hl_ == 0)
                right_edge = (hr_ == 0)
                HLe = BA[:, 0:KC * ne].rearrange("p (k t) -> p k t", k=KC)
                ZS = BA[:, 8320:8320 + KC * n].rearrange("p (k t) -> p k t", k=KC)
                wu = ne + 16
                UCb = BA[:, 16512:16512 + 4 * wu].rearrange("p (t c) -> p t c", t=4)
                wc_ = ne + 2
                CXb = BA[:, 20736:20736 + 4 * wc_].rearrange("p (t c) -> p t c", t=4)
                XPb = BA[:, 24896:24896 + 2 * ne].rearrange("p (t c) -> p t c", t=2)
                SA_ = BA[:, 26976:26976 + wu]
                SB_ = BA[:, 28032:28032 + wu]
                SQ = BA[:, 16512:16512 + 4096].rearrange("p (k c) -> p k c", k=KC)
                assert 28032 + wu <= BA_FREE
                HALO = BA[:, 29184:29248].rearrange("p (k c) -> p k c", k=KC)
                if pi == act_pieces[0]:
                    odd_norm(pi)
                core = lambda o, nb, hl_=hl_: slice(hl_ + o, hl_ + o + nb)

                def inproj(wv, t2, mov, nb):
                    pt = ps_next()
                    for k in range(KC):
                        S.mm(pt[:, 0:nb], wv[:, k, t2 * 128:(t2 + 1) * 128], mov(k), start=(k == 0), stop=(k == KC - 1))
                    return pt

                S.memset("pool", UCb[:, :, 0:8], 0.0)
                S.memset("pool", UCb[:, :, 8 + ne:16 + ne], 0.0)
                for wg in range(2):
                    w = load_w(owin_d[i], wg * 256, key=("owin", i))
                    for t2 in range(2):
                        t_ = wg * 2 + t2
                        for (o, nb) in nblocks(ne):
                            pt = inproj(w, t2, lambda k, o=o, nb=nb: HLe[:, k, o:o + nb], nb)
                            S.copy("act", UCb[:, t_, 8 + o:8 + o + nb], pt[:, 0:nb])
                c0 = 8 + hl_
                for t_ in range(4):
                    u = UCb[:, t_, :]
                    pe_ = "dve"
                    S.tt(pe_, SA_[:, 1:wu], u[:, 0:wu - 1], u[:, 1:wu], ALU.add)
                    cur, oth = SA_, SB_
                    if t_ >= 1:
                        S.tt(pe_, SB_[:, 2:wu - 1], SA_[:, 1:wu - 2], SA_[:, 3:wu], ALU.add)
                        cur, oth = SB_, SA_
                    if t_ >= 2:
                        S.tt(pe_, SA_[:, 4:wu - 3], SB_[:, 2:wu - 5], SB_[:, 6:wu - 1], ALU.add)
                        cur, oth = SA_, SB_
                    if t_ >= 3:
                        S.tt(pe_, SB_[:, 8:wu - 7], SA_[:, 4:wu - 11], SA_[:, 12:wu - 3], ALU.add)
                        cur, oth = SB_, SA_
                    fx = CST[:, C_FIX + t_ * 16:C_FIX + (t_ + 1) * 16]
                    if left_edge:
                        S.tt(pe_, cur[:, c0:c0 + 8], cur[:, c0:c0 + 8], fx[:, 0:8], ALU.mult)
                    if right_edge:
                        S.tt(pe_, cur[:, c0 + n - 8:c0 + n], cur[:, c0 + n - 8:c0 + n], fx[:, 8:16], ALU.mult)
                    S.ts(pe_, oth[:, c0:c0 + n], u[:, c0:c0 + n], -float(POOL_W[t_]), None, ALU.mult)
                    S.tt(pe_, u[:, c0:c0 + n], cur[:, c0:c0 + n], oth[:, c0:c0 + n], ALU.add)
                for wg in range(4):
                    w = load_w(owin_d[i], 2048 + wg * 256, key=("owin", i))
                    for t2 in range(2):
                        zt = wg * 2 + t2
                        for (o, nb) in nblocks(n):
                            pt = inproj(w, t2, lambda k, o=o, nb=nb: HLe[:, k, core(o, nb)], nb)
                            S.act(ZS[:, zt, o:o + nb], pt[:, 0:nb], AF.Silu)
                for wg in range(2):
                    w = load_w(owin_d[i], 1536 + wg * 256, key=("owin", i))
                    for t2 in range(2):
                        t_ = wg * 2 + t2
                        for (o, nb) in nblocks(n):
                            pt = inproj(w, t2, lambda k, o=o, nb=nb: HLe[:, k, core(o, nb)], nb)
                            S.tt("dve", ZS[:, 4 + t_, o:o + nb], pt[:, 0:nb], ZS[:, 4 + t_, o:o + nb], ALU.mult)
                for t_ in range(4):
                    for (o, nb) in nblocks(n):
                        pt = ps_next()
                        S.mm(pt[:, 0:nb], PWS[:, 0, t_, :], UCb[:, t_, c0 + o:c0 + o + nb])
                        S.stt("dve", ZS[:, t_, o:o + nb], pt[:, 0:nb],
                              VEC[:, V_PSCALE + i * 4 + t_:V_PSCALE + i * 4 + t_ + 1], ZS[:, t_, o:o + nb],
                              ALU.mult, ALU.mult)
                S.memset("pool", CXb[:, :, 0:1], 0.0)
                S.memset("pool", CXb[:, :, 1 + ne:2 + ne], 0.0)
                for wg in range(2):
                    wx = load_w(owin_d[i], 1024 + wg * 256, key=("owin", i))
                    for t2 in range(2):
                        for (o, nb) in nblocks(ne):
                            pt = inproj(wx, t2, lambda k, o=o, nb=nb: HLe[:, k, o:o + nb], nb)
                            S.copy("act", XPb[:, t2, o:o + nb], pt[:, 0:nb])
                    wc = load_w(owin_d[i], 512 + wg * 256, key=("owin", i))
                    for t2 in range(2):
                        t_ = wg * 2 + t2
                        for (o, nb) in nblocks(ne):
                            pt = inproj(wc, t2, lambda k, o=o, nb=nb: HLe[:, k, o:o + nb], nb)
                            S.tt("dve", CXb[:, t_, 1 + o:1 + o + nb], pt[:, 0:nb], XPb[:, t2, o:o + nb], ALU.mult)
                nxt = [pj for pj in act_pieces if pj > pi]
                if nxt:
                    odd_norm(nxt[0])
                for t_ in range(4):
                    for (o, nb) in nblocks(n):
                        pt = ps_next()
                        for k in range(3):
                            S.mm(pt[:, 0:nb], DG[:, t_, k, :], CXb[:, t_, hl_ + o + k:hl_ + o + k + nb],
                                 start=(k == 0), stop=(k == 2))
                        S.tt("dve", ZS[:, 4 + t_, o:o + nb], pt[:, 0:nb], ZS[:, 4 + t_, o:o + nb], ALU.mult)
                out_proj(owout_d[i], ZS, n, kind, s0, w_, l, wkey=("owout", i))

        try:
            chk("prologue")
            for l in range(nlayers):
                if l % 2 == 0:
                    even_layer(l)
                else:
                    odd_layer(l)
                if l % 2 == 0:
                    ada_steps(l + 1, 24)
                    ada_steps(l + 2, 24)
        except _Stop:
            print("stopped at op", len(S.ops))
            S.limit = None

        outv = outT_d.rearrange("(k p) t -> p k t", p=128)
        if final_norm:
            SQ = BA[:, 0:4096].rearrange("p (k c) -> p k c", k=KC)
            fg = VEC[:, V_FINALG:V_FINALG + 8]
            for bi, (o, nb) in enumerate(nblocks(T)):
                for k in range(KC):
                    S.act(SQ[:, k, 0:nb], XL[:, k, o:o + nb], AF.Square)
                pt = ps_next(4, 6)
                for k in range(KC):
                    S.mm(pt[:, 0:nb], ONES, SQ[:, k, 0:nb], start=(k == 0), stop=(k == KC - 1))
                rs = FT[:, 0:nb]
                S.act(rs, pt[:, 0:nb], AF.Sqrt, bias=EPSC, scale=1.0 / D)
                S.recip(rs, rs)
                ob = FA[:, 0:4096].rearrange("p (k c) -> p k c", k=KC)
                for k in range(KC):
                    S.stt("dve", ob[:, k, 0:nb], XL[:, k, o:o + nb], fg[:, k:k + 1], rs,
                          ALU.mult, ALU.mult)
                S.dma(outv[:, :, o:o + nb], ob[:, :, 0:nb], eng=STQ)
        else:
            S.dma(outv, XL)

        S.emit(sems, dsems)
        build_program.stats = (len(S.ops), S.n_waits, S.n_signals)
    return nc


_PROGRAM_CACHE = {}


def kernel(**inputs):
    inp = {k: np.asarray(v) for k, v in inputs.items()}
    nb = inp["x"].shape[0]
    shared = _prep_shared_inputs(inp)
    in_maps = []
    for b in range(nb):
        m = dict(shared)
        m.update(_prep_core_inputs(inp, b))
        in_maps.append(m)
    if "nc" not in _PROGRAM_CACHE:
        _PROGRAM_CACHE["nc"] = build_program()
    nc = _PROGRAM_CACHE["nc"]
    res = run_bass_kernel_spmd(nc, in_maps, core_ids=list(range(nb)))
    out = np.stack([np.ascontiguousarray(res.results[b]["outT"].T) for b in range(nb)], axis=0)
    return out.astype(np.float32)
```

```python
import math
from contextlib import ExitStack

import numpy as np
import ml_dtypes
import concourse.bass as bass
import concourse.mybir as mybir
from concourse.bass_utils import run_bass_kernel_spmd

F32 = mybir.dt.float32
BF16 = mybir.dt.bfloat16
I32 = mybir.dt.int32
AF = mybir.ActivationFunctionType
ALU = mybir.AluOpType

D = 1024
T = 2048
TC = 256
TT = T + TC
KC = 8
DEPTH = 4
NEED_CTX = [True, True, False, False]
EPS = 1e-6
POOL_W = (2, 4, 8, 16)
NCH = 72
NDMA_SLOTS = 18


def _rect(ap):
    t = ap.tensor
    name = t.name
    dims = ap.ap
    off = int(ap.offset)
    if "DRam" in type(t).__name__:
        lo = off
        hi = off
        for s, c in dims:
            if s >= 0:
                hi += (c - 1) * s
            else:
                lo += (c - 1) * s
        cnt = 1
        for _s, c in dims:
            cnt *= c
        return (name, 0, 1, lo, hi + 1, cnt == hi + 1 - lo)
    rowsize = 1
    for d in list(t.shape)[1:]:
        rowsize *= d
    if "PSum" in type(t).__name__:
        return (name, 0, 128, 0, rowsize, True, True)
    p0 = off // rowsize
    f0 = off % rowsize
    s0, c0 = dims[0]
    if s0 == rowsize or c0 == 1:
        p1 = p0 + c0
        rest = dims[1:]
    else:
        p1 = p0 + 1
        rest = dims
    lo = f0
    hi = f0
    for s, c in rest:
        if s >= 0:
            hi += (c - 1) * s
        else:
            lo += (c - 1) * s
    cnt = 1
    for _s, c in rest:
        cnt *= c
    return (name, p0, p1, lo, hi + 1, cnt == hi + 1 - lo)


def _overlap(a, b):
    return a[1] < b[2] and b[1] < a[2] and a[3] < b[4] and b[3] < a[4]


def _contains(a, b):
    return a[1] <= b[1] and a[2] >= b[2] and a[3] <= b[3] and a[4] >= b[4]


class _Stop(Exception):
    pass


class Sched:
    ENG = ("pe", "act", "dve", "pool", "sp")

    def __init__(self, nc):
        self.nc = nc
        self.ops = []
        self.recs = {}

    limit = None

    def add(self, eng, fn, reads, writes, dma=False):
        oid = len(self.ops)
        if self.limit is not None and oid >= self.limit:
            raise _Stop()
        deps = {}
        rr = [_rect(a) for a in reads]
        wr = [_rect(a) for a in writes]
        psr = [r for r in rr if len(r) > 6]
        if psr:
            rr = [r for r in rr if len(r) <= 6]
            wr = wr + psr
        for r in rr:
            for (q, o, w) in self.recs.get(r[0], ()):
                if w and _overlap(r, q):
                    deps[o] = True
        for r in wr:
            for (q, o, w) in self.recs.get(r[0], ()):
                if _overlap(r, q):
                    deps.setdefault(o, False)
        ops = self.ops

        def same_eng(o):
            if o == oid:
                return True
            od = ops[o]
            return (not dma) and (not od["dma"]) and od["eng"] == eng

        for r in wr:
            lst = self.recs.setdefault(r[0], [])
            lst[:] = [x for x in lst if not (_contains(r, x[0]) and (r[5] or same_eng(x[1])))]
            lst.append((r, oid, True))
        for r in rr:
            lst = self.recs.setdefault(r[0], [])
            lst[:] = [x for x in lst if not ((not x[2]) and _contains(r, x[0]) and same_eng(x[1]))]
            lst.append((r, oid, False))
        deps.pop(oid, None)
        keep = []
        for o, raw in deps.items():
            od = self.ops[o]
            if (not dma) and (not od["dma"]) and od["eng"] == eng:
                if eng == "pe":
                    continue
            keep.append(o)
        self.ops.append(dict(eng=eng, fn=fn, dma=dma, deps=sorted(keep)))
        return oid

    def dma(self, out, in_, eng="sp"):
        return self.add(eng, lambda e: e.dma_start(out=out, in_=in_), [in_], [out], dma=True)

    def mm(self, out, lhsT, rhs, start=True, stop=True):
        reads = [lhsT, rhs]
        return self.add("pe", lambda e: e.matmul(out, lhsT, rhs, start=start, stop=stop), reads, [out])

    def tr(self, out, in_, ident):
        return self.add("pe", lambda e: e.transpose(out, in_, ident), [in_, ident], [out])

    def act(self, out, in_, func, bias=None, scale=None):
        reads = [in_]
        kw = {}
        if bias is not None:
            kw["bias"] = bias
            reads.append(bias)
        if scale is not None:
            kw["scale"] = scale
            if not isinstance(scale, (int, float)):
                reads.append(scale)
        return self.add("act", lambda e: e.activation(out, in_, func, **kw), reads, [out])

    def tt(self, eng, out, in0, in1, op):
        return self.add(eng, lambda e: e.tensor_tensor(out, in0, in1, op), [in0, in1], [out])

    def ts(self, eng, out, in0, s1, s2, op0, op1=None):
        reads = [in0] + [s for s in (s1, s2) if s is not None and not isinstance(s, (int, float))]
        if op1 is None:
            return self.add(eng, lambda e: e.tensor_scalar(out, in0, s1, None, op0), reads, [out])
        return self.add(eng, lambda e: e.tensor_scalar(out, in0, s1, s2, op0, op1), reads, [out])

    def stt(self, eng, out, in0, scalar, in1, op0, op1):
        reads = [in0, in1] + ([] if isinstance(scalar, (int, float)) else [scalar])
        return self.add(eng, lambda e: e.scalar_tensor_tensor(out, in0, scalar, in1, op0, op1), reads, [out])

    def copy(self, eng, out, in_):
        if eng == "act":
            return self.add("act", lambda e: e.activation(out, in_, AF.Copy), [in_], [out])
        return self.add(eng, lambda e: e.tensor_copy(out, in_), [in_], [out])

    def memset(self, eng, out, val):
        return self.add(eng, lambda e: e.memset(out, val), [], [out])

    def recip(self, out, in_):
        return self.add("dve", lambda e: e.reciprocal(out, in_), [in_], [out])

    def emit(self, sems, dsems):
        nc = self.nc
        ops = self.ops
        n = len(ops)
        per_eng = {e: [] for e in self.ENG}
        for i, o in enumerate(ops):
            per_eng[o["eng"]].append(i)
        slot_uses = [0] * len(dsems)
        dctr = 0
        sctr = 0
        NSW = 4
        for i, o in enumerate(ops):
            if o["dma"]:
                if o["eng"] == "pool":
                    s = len(dsems) - NSW + sctr % NSW
                    sctr += 1
                else:
                    s = dctr % (len(dsems) - NSW)
                    dctr += 1
                slot_uses[s] += 1
                o["tl"] = ("d", s)
                o["pos"] = slot_uses[s]
                o["val"] = 16 * slot_uses[s]
            else:
                o["tl"] = o["eng"]
        for e in self.ENG:
            k = 0
            for i in per_eng[e]:
                if not ops[i]["dma"]:
                    k += 1
                    ops[i]["pos"] = k
        know = {e: {} for e in self.ENG}
        snap = [None] * n
        waits = [None] * n
        signal = [False] * n
        last_slot_op = {}
        for i, o in enumerate(ops):
            e = o["eng"]
            k = know[e]
            w = []
            deps = list(o["deps"])
            if o["dma"]:
                s = o["tl"]
                if s in last_slot_op:
                    deps.append(last_slot_op[s])
                last_slot_op[s] = i
            for d in sorted(deps):
                od = ops[d]
                tl = od["tl"]
                if k.get(tl, 0) >= od["pos"]:
                    continue
                w.append(d)
                signal[d] = True
                for t2, p2 in snap[d].items():
                    if k.get(t2, 0) < p2:
                        k[t2] = p2
                if k.get(tl, 0) < od["pos"]:
                    k[tl] = od["pos"]
            waits[i] = w
            snap[i] = dict(k)
        for e in self.ENG:
            c = 0
            for i in per_eng[e]:
                o = ops[i]
                if o["dma"]:
                    continue
                if signal[i]:
                    c += 1
                    o["val"] = c
                else:
                    o["val"] = None
        self.n_waits = sum(len(w) for w in waits)
        self.n_signals = sum(1 for s_ in signal if s_)

        def semof(o):
            if o["dma"]:
                return dsems[o["tl"][1]]
            return sems[o["eng"]]

        def run_engine(e):
            def body(eng):
                for i in per_eng[e]:
                    o = ops[i]
                    best = {}
                    for d in waits[i]:
                        od = ops[d]
                        key = od["tl"]
                        if key not in best or best[key][1] < od["val"]:
                            best[key] = (semof(od), od["val"])
                    for key, (sm, v) in best.items():
                        eng.wait_ge(sm, v)
                    ins = o["fn"](eng)
                    if o["dma"]:
                        ins.then_inc(semof(o), 16)
                    elif signal[i]:
                        ins.then_inc(semof(o), 1)
                if e == "sp":
                    for s, sm in enumerate(dsems):
                        if slot_uses[s]:
                            eng.wait_ge(sm, 16 * slot_uses[s])
            return body

        with nc.Block() as block:
            block.sync(run_engine("sp"))
            block.tensor(run_engine("pe"))
            block.scalar(run_engine("act"))
            block.vector(run_engine("dve"))
            block.gpsimd(run_engine("pool"))


V_NORMG = 0
V_FINALG = 32
V_ADAB = 40
V_GLUB = 136
V_PSCALE = 148
V_CONVW = 156
V_CVEC = 180
V_DTAB = 196
V_CONSTS = 292
NVEC = 300
C_ID = 0
C_MF = 128
C_MB = 256
C_BDC = 384
C_BDS = 512
C_FIX = 640
C_PROW = 704
C_PCOL = 832
NCST = 1088

_CONST_CACHE = {}


def _host_consts():
    if _CONST_CACHE:
        return _CONST_CACHE
    f32 = np.float32
    rows = T // 64
    r = np.arange(rows, dtype=f32)
    col = np.arange(64, dtype=f32)
    rr, cc = np.meshgrid(r, col, indexing="ij")
    rr = rr.reshape(-1, 1)
    cc = cc.reshape(-1, 1)
    quarter = D // 4
    omega = np.power(f32(10000.0), -np.arange(quarter, dtype=f32) / f32(quarter)).astype(f32)
    pos = np.concatenate([np.sin(rr * omega), np.cos(rr * omega), np.sin(cc * omega), np.cos(cc * omega)],
                         axis=-1).astype(f32)
    posT = np.ascontiguousarray(pos.T)

    cst = np.zeros((128, NCST), f32)
    cst[:, C_ID:C_ID + 128] = np.eye(128, dtype=f32)
    jj = np.arange(128) // 16
    cst[:, C_MF:C_MF + 128] = (jj[:, None] <= jj[None, :]).astype(f32)
    cst[:, C_MB:C_MB + 128] = (jj[:, None] >= jj[None, :]).astype(f32)
    m = np.arange(64)
    ang = 2.0 * np.pi * ((m[:, None] * m[None, :]) % 64) / 64.0
    c64 = np.cos(ang)
    s64 = np.sin(ang)
    bdc = np.zeros((128, 128))
    bds = np.zeros((128, 128))
    for g in range(2):
        bdc[g * 64:(g + 1) * 64, g * 64:(g + 1) * 64] = c64
        bds[g * 64:(g + 1) * 64, g * 64:(g + 1) * 64] = s64
    cst[:, C_BDC:C_BDC + 128] = bdc
    cst[:, C_BDS:C_BDS + 128] = bds
    fix = np.zeros((4, 16))
    for ti, w in enumerate(POOL_W):
        for p in range(8):
            cnt = min(p + w // 2, 10 ** 6) - max(p - w // 2, 0)
            fix[ti, p] = w / cnt
            posr = 1000 - 8 + p
            cnt = min(posr + w // 2, 1000) - (posr - w // 2)
            fix[ti, 8 + p] = w / cnt
    cst[:, C_FIX:C_FIX + 64] = fix.reshape(1, 64)
    prow = pos[0::64, 0:512]
    pcol = pos[0:64, 512:1024]
    cst[:, C_PROW:C_PROW + 128] = prow.T.reshape(4, 128, 32).transpose(1, 0, 2).reshape(128, 128)
    cst[:, C_PCOL:C_PCOL + 256] = pcol.T.reshape(4, 128, 64).transpose(1, 0, 2).reshape(128, 256)

    def dft(n):
        t = np.arange(n)
        a = 2.0 * np.pi * ((t[:, None] * t[None, :]) % n) / n
        cs = np.cos(a)
        ns = -np.sin(a)
        kc = n // 128
        cs = cs.reshape(kc, 128, n).transpose(1, 0, 2)
        ns = ns.reshape(kc, 128, n).transpose(1, 0, 2)
        return (np.ascontiguousarray(cs).astype(ml_dtypes.bfloat16),
                np.ascontiguousarray(ns).astype(ml_dtypes.bfloat16))

    cosL, nsinL = dft(T)
    cosC, nsinC = dft(TC)
    _CONST_CACHE.update(posT=posT, cst=cst, cosL=cosL, nsinL=nsinL, cosC=cosC, nsinC=nsinC)
    return _CONST_CACHE


def _pk(v):
    v = np.asarray(v, np.float32)
    return np.ascontiguousarray(v.reshape(-1, 128).T)


def _prep_core_inputs(inp, b):
    f32 = np.float32
    vec = np.zeros((128, NVEC), f32)
    for l in range(DEPTH):
        vec[:, V_NORMG + l * 8:V_NORMG + (l + 1) * 8] = _pk(inp["norm_g"][l])
        vec[:, V_ADAB + l * 24:V_ADAB + (l + 1) * 24] = _pk(inp["ada_b"][l])
    vec[:, V_FINALG:V_FINALG + 8] = _pk(inp["final_g"])
    for i in range(2):
        vec[:, V_GLUB + i * 6:V_GLUB + (i + 1) * 6] = _pk(inp["s5_glu_b"][i])
        vec[:, V_PSCALE + i * 4:V_PSCALE + (i + 1) * 4] = _pk(inp["pool_scale"][i])
        for k in range(3):
            vec[:, V_CONVW + (i * 3 + k) * 4:V_CONVW + (i * 3 + k + 1) * 4] = _pk(inp["conv_w"][i, k])
        dt = np.asarray(inp["s5_d"][i], f32).T
        vec[:, V_DTAB + i * 48:V_DTAB + (i + 1) * 48] = np.tile(dt, (8, 1))
    cv = np.stack([_pk(inp["c"][b]), _pk(inp["c_ctx"])], axis=-1)
    vec[:, V_CVEC:V_CVEC + 16] = cv.reshape(128, 16)
    vec[:, V_CONSTS + 0] = EPS
    vec[:, V_CONSTS + 1] = 0.0
    return dict(
        xT=np.ascontiguousarray(np.asarray(inp["x"][b], f32).T),
        ctxT=np.ascontiguousarray(np.asarray(inp["ctx"][b], f32).T),
        vec=vec,
    )


def _prep_shared_inputs(inp):
    f32 = np.float32
    cs = _host_consts()
    sh = dict(cst=cs["cst"], cosL=cs["cosL"], nsinL=cs["nsinL"],
              cosC=cs["cosC"], nsinC=cs["nsinC"])
    for k in ("ada_w", "even_w_in", "even_w_out", "odd_w_in", "odd_w_out"):
        sh[k] = np.ascontiguousarray(np.asarray(inp[k], f32))
    sh["glu_w"] = np.ascontiguousarray(np.asarray(inp["s5_glu_w"], f32))
    sh["pool_w"] = np.ascontiguousarray(np.asarray(inp["pool_w"], f32).transpose(0, 2, 1, 3))
    fw = np.asarray(inp["fnet_w"], f32).reshape(2, 2, 2, 64, 64)
    sh["fnet_w"] = np.ascontiguousarray(fw.transpose(0, 2, 3, 1, 4).reshape(2, 128, 2, 64))
    def qmaj(a):
        return np.asarray(a, f32).reshape(96, 64).T
    s5s = np.zeros((2, 128, 3, 96), f32)
    s5bc = np.zeros((2, 128, 4, 96, 16), f32)
    for i in range(2):
        s5s[i, :, 0, :] = np.tile(qmaj(inp["s5_lam_re"][i]), (2, 1))
        s5s[i, :, 1, :] = np.tile(qmaj(inp["s5_lam_im"][i]), (2, 1))
        s5s[i, :, 2, :] = np.tile(np.asarray(inp["s5_log_step"][i], f32).reshape(1, 96), (128, 1))
        br = np.asarray(inp["s5_b_re"][i], f32).reshape(96, 64, 16).transpose(1, 0, 2)
        bi = np.asarray(inp["s5_b_im"][i], f32).reshape(96, 64, 16).transpose(1, 0, 2)
        cr = np.asarray(inp["s5_c_re"][i], f32).reshape(96, 16, 64).transpose(2, 0, 1)
        ci = np.asarray(inp["s5_c_im"][i], f32).reshape(96, 16, 64).transpose(2, 0, 1)
        for w, a in enumerate((br, bi, cr, ci)):
            s5bc[i, :, w] = np.tile(a, (2, 1, 1))
    sh["s5s"] = s5s
    sh["s5bc"] = s5bc
    return sh


def build_program(nlayers=DEPTH, final_norm=True, debug=False, stop=None):
    nc = bass.Bass("TRN2", target_bir_lowering=False)

    def din(name, shape, dt=F32):
        return nc.dram_tensor(name, list(shape), dt, kind="ExternalInput").ap()

    xT_d = din("xT", [D, T])
    ctxT_d = din("ctxT", [D, TC])
    vec_d = din("vec", [128, NVEC])
    cst_d = din("cst", [128, NCST])
    cosL_d = din("cosL", [128, 16, T], BF16)
    nsinL_d = din("nsinL", [128, 16, T], BF16)
    cosC_d = din("cosC", [128, 2, TC], BF16)
    nsinC_d = din("nsinC", [128, 2, TC], BF16)
    ada_w_d = din("ada_w", [DEPTH, D, 3 * D])
    ewin_d = din("even_w_in", [2, D, 2048])
    ewout_d = din("even_w_out", [2, D, D])
    owin_d = din("odd_w_in", [2, D, 3072])
    owout_d = din("odd_w_out", [2, D, D])
    glu_d = din("glu_w", [2, 768, 768])
    poolw_d = din("pool_w", [2, 128, 4, 128])
    fnetw_d = din("fnet_w", [2, 128, 2, 64])
    s5s_d = din("s5s", [2, 128, 3, 96])
    s5bc_d = din("s5bc", [2, 128, 4, 96, 16])
    outT_d = nc.dram_tensor("outT", [D, T], F32, kind="ExternalOutput").ap()
    skind = "ExternalOutput" if debug else "Internal"
    HLs = nc.dram_tensor("HLs", [128, KC, TT], BF16, kind=skind).ap()
    UGLs = nc.dram_tensor("UGLs", [128, 48, 288], BF16, kind=skind).ap()
    FNs = nc.dram_tensor("FNs", [128, 2, TT], BF16, kind=skind).ap()
    YGs = nc.dram_tensor("YGs", [128, 6, TT], BF16, kind=skind).ap()
    XLs = nc.dram_tensor("XLs", [128, KC, T], F32, kind="Internal").ap()

    with ExitStack() as st:
        E = st.enter_context
        XLA = E(nc.sbuf_tensor("XLA", [128, KC * T], F32))
        XC = E(nc.sbuf_tensor("XC", [128, KC, TC], F32))
        FA = E(nc.sbuf_tensor("FA", [128, 6144], F32))
        BA = E(nc.sbuf_tensor("BA", [128, 36864], BF16))
        VEC = E(nc.sbuf_tensor("VEC", [128, NVEC], F32))
        CST = E(nc.sbuf_tensor("CST", [128, NCST], F32))
        CB = E(nc.sbuf_tensor("CB", [128, 5, 128], BF16))
        ADA = E(nc.sbuf_tensor("ADA", [128, DEPTH, 24, 2], F32))
        AM = E(nc.sbuf_tensor("AM", [128, 2, 8], F32))
        SC = E(nc.sbuf_tensor("SC", [128, 8, 2], F32))
        IT = E(nc.sbuf_tensor("IT", [128, 96], I32))
        ADST = E(nc.sbuf_tensor("ADST", [128, 2, 1024], F32))
        WAB = E(nc.sbuf_tensor("WAB", [128, 2, 256], BF16))
        PWS = E(nc.sbuf_tensor("PWS", [128, 2, 4, 128], BF16))
        DG = E(nc.sbuf_tensor("DG", [128, 4, 3, 128], BF16))
        PS = [E(nc.psum_tensor("PS%d" % i, [128, 512], F32)) for i in range(6)]
        PSB = [E(nc.psum_tensor("PSB%d" % i, [128, 1024], BF16)) for i in range(2)]
        PSB_ADA = PS[3]
        sems = {e: E(nc.semaphore("s_" + e)) for e in Sched.ENG}
        dsems = [E(nc.semaphore("d%d" % i)) for i in range(NDMA_SLOTS)]

        S = Sched(nc)
        if isinstance(stop, int):
            S.limit = stop
        XL = XLA[:, :].rearrange("p (k t) -> p k t", k=KC)
        IDB = CB[:, 0, :]
        MSKF = CB[:, 1, :]
        MSKB = CB[:, 2, :]
        ONES = CB[:, 3, :]
        IDF = CST[:, C_ID:C_ID + 128]
        EPSC = VEC[:, V_CONSTS:V_CONSTS + 1]
        ZEROC = VEC[:, V_CONSTS + 1:V_CONSTS + 2]

        rr = {"ps": 0, "psb": 0, "w": 0, "cast": 0, "ev": 0}
        STQ = "act"
        LDQ = "pool"

        def chk(tag):
            if stop == tag:
                raise _Stop()

        def ps_next(lo=0, hi=None):
            if hi is None:
                hi = rr.get("hi", 4)
            i = lo + rr["ps"] % (hi - lo)
            rr["ps"] += 1
            return PS[i]

        def psb_next():
            i = rr["psb"] % 2
            rr["psb"] += 1
            return PSB[i]

        def ev_eng():
            return "dve"

        WST = [FA[:, i * 2048:(i + 1) * 2048] for i in range(2)]
        WBF = [BA[:, 34816 + i * 1024:34816 + (i + 1) * 1024] for i in range(2)]

        WBF = [BA[:, 36864 - 4096 + i * 2048:36864 - 4096 + (i + 1) * 2048] for i in range(2)]
        BA_FREE = 36864 - 4096

        class _W2:
            def __init__(self, halves):
                self.h = halves

            def __getitem__(self, key):
                p, k, c = key
                hidx = c.start // 128
                return self.h[hidx][p, k, c.start - 128 * hidx:c.stop - 128 * hidx]

        wsc = {}
        wseen = set()

        def _cache(key, kch, total_cols):
            if key not in wsc:
                wsc[key] = nc.dram_tensor("WSC_%s_%d" % key, [total_cols // 128, 128, kch * 128], BF16).ap()
            return wsc[key]

        def _load_half(wd, c0, ncols, kch, cast, hs, key=None):
            wb = WBF[hs // 2][:, (hs % 2) * 1024:(hs % 2) * 1024 + kch * ncols].rearrange("p (k c) -> p k c", k=kch)
            hidx = c0 // 128
            if cast and key is not None and (key, hidx) in wseen:
                cs = _cache(key, kch, wd.shape[1])
                S.dma(wb, cs[hidx].rearrange("p (k c) -> p k c", k=kch))
                return wb
            stg = WST[hs // 2][:, (hs % 2) * 1024:(hs % 2) * 1024 + kch * ncols].rearrange("p (k c) -> p k c", k=kch)
            S.dma(stg, wd[:, c0:c0 + ncols].rearrange("(k p) c -> p k c", p=128))
            if not cast:
                return stg
            rr["cast"] += 1
            ce = ("act", "act", "dve")[rr["cast"] % 3]
            S.copy(ce, wb, stg)
            if key is not None:
                cs = _cache(key, kch, wd.shape[1])
                S.dma(cs[hidx].rearrange("p (k c) -> p k c", k=kch), wb, eng=STQ)
                wseen.add((key, hidx))
            return wb

        def load_w(wd, c0, ncols=256, kch=KC, cast=True, split=True, key=None):
            if split and ncols == 256:
                hv = []
                for hh in range(2):
                    hs = rr["w"] % 4
                    rr["w"] += 1
                    hv.append(_load_half(wd, c0 + hh * 128, 128, kch, cast, hs, key))
                return _W2(hv)
            if rr["w"] % 2:
                rr["w"] += 1
            slot = (rr["w"] % 4) // 2
            rr["w"] += 2
            wb = WBF[slot][:, 0:kch * ncols].rearrange("p (k c) -> p k c", k=kch)
            h0 = c0 // 128
            if cast and key is not None and all((key, h0 + hh) in wseen for hh in range(ncols // 128)):
                cs = _cache(key, kch, wd.shape[1])
                for hh in range(ncols // 128):
                    S.dma(wb[:, :, hh * 128:(hh + 1) * 128], cs[h0 + hh].rearrange("p (k c) -> p k c", k=kch))
                return wb
            stg = WST[slot][:, 0:kch * ncols].rearrange("p (k c) -> p k c", k=kch)
            src = wd[:, c0:c0 + ncols].rearrange("(k p) c -> p k c", p=128)
            S.dma(stg, src)
            if not cast:
                return stg
            rr["cast"] += 1
            ce = ("act", "act", "dve")[rr["cast"] % 3]
            S.copy(ce, wb, stg)
            if key is not None:
                cs = _cache(key, kch, wd.shape[1])
                for hh in range(ncols // 128):
                    S.dma(cs[h0 + hh].rearrange("p (k c) -> p k c", k=kch), wb[:, :, hh * 128:(hh + 1) * 128], eng=STQ)
                    wseen.add((key, h0 + hh))
            return wb

        def nblocks(n, bs=512):
            out = []
            o = 0
            while o < n:
                out.append((o, min(bs, n - o)))
                o += bs
            return out

        FT = FA[:, 4096:6144]

        def make_hl(xsrc, hl_dst, n, am, sh, blocks, sqbuf):
            for (o, nb) in blocks:
                for k in range(KC):
                    S.act(sqbuf[:, k, 0:nb], xsrc(k, o, nb), AF.Square)
                pt = ps_next(4, 6)
                for k in range(KC):
                    S.mm(pt[:, 0:nb], ONES, sqbuf[:, k, 0:nb], start=(k == 0), stop=(k == KC - 1))
                rs = FT[:, 0:nb]
                S.act(rs, pt[:, 0:nb], AF.Sqrt, bias=EPSC, scale=1.0 / D)
                S.recip(rs, rs)
                for k in range(KC):
                    tmp = FT[:, 512 + (k % 2) * 512:512 + (k % 2) * 512 + nb]
                    S.tt("dve", tmp, xsrc(k, o, nb), rs, ALU.mult)
                    S.act(hl_dst(k, o, nb), tmp, AF.Identity, bias=sh[:, k:k + 1], scale=am[:, k:k + 1])

        RSL = ADST[:, :, :].rearrange("p a c -> p (a c)")
        RSC = FT[:, 1536:1792]

        def rs_of(kind, a, nb):
            return RSL[:, a:a + nb] if kind == "lat" else RSC[:, a:a + nb]

        def layer_stats(with_ctx, part=None):
            SQ2 = [BA[:, 24576 + b_ * 4096:24576 + (b_ + 1) * 4096].rearrange("p (k c) -> p k c", k=KC) for b_ in range(2)]
            todo = [("lat", o, nb) for (o, nb) in nblocks(T)]
            if with_ctx:
                todo.append(("ctx", 0, TC))
            if part == 0:
                todo = todo[:2]
            elif part == 1:
                todo = todo[2:]
            for bi_, (kind, o, nb) in enumerate(todo):
                SQs = SQ2[bi_ % 2]
                xs = xview(kind)
                for k in range(KC):
                    S.act(SQs[:, k, 0:nb], xs(k, o, nb), AF.Square)
                pt = ps_next(4, 6)
                for k in range(KC):
                    S.mm(pt[:, 0:nb], ONES, SQs[:, k, 0:nb], start=(k == 0), stop=(k == KC - 1))
                rs = rs_of(kind, o, nb)
                S.act(rs, pt[:, 0:nb], AF.Sqrt, bias=EPSC, scale=1.0 / D)
                S.recip(rs, rs)

        def normalize(kind, hl_dst, am, sh, blocks):
            xs = xview(kind)
            for (o, nb) in blocks:
                for k in range(KC):
                    tmp = FT[:, 512 + (k % 2) * 512:512 + (k % 2) * 512 + nb]
                    S.tt("dve", tmp, xs(k, o, nb), rs_of(kind, o, nb), ALU.mult)
                    S.act(hl_dst(k, o, nb), tmp, AF.Identity, bias=sh[:, k:k + 1], scale=am[:, k:k + 1])

        ada_state = {}

        def ada_steps(l, n):
            if l >= nlayers:
                return
            st_ = ada_state.setdefault(l, 0)
            pt = PSB_ADA
            for jt in range(st_, min(24, st_ + n)):
                stg = ADST[:, jt % 2, :].rearrange("p (k c) -> p k c", k=KC)
                S.dma(stg, ada_w_d[l][:, jt * 128:(jt + 1) * 128].rearrange("(k p) c -> p k c", p=128))
                for k in range(KC):
                    S.mm(pt[:, l * 48 + jt * 2:l * 48 + jt * 2 + 2], stg[:, k, :], SC[:, k, :],
                         start=(k == 0), stop=(k == KC - 1))
            ada_state[l] = min(24, st_ + n)
            if ada_state[l] == 24 and st_ < 24:
                ab = VEC[:, V_ADAB + l * 24:V_ADAB + (l + 1) * 24]
                S.tt("dve", ADA[:, l, :, :], pt[:, l * 48:l * 48 + 48].rearrange("p (j w) -> p j w", w=2),
                     ab.unsqueeze(2).to_broadcast([128, 24, 2]), ALU.add)

        def ada_layer(l):
            pt = PS[5]
            for gidx in range(12):
                w = load_w(ada_w_d[l], gidx * 256, 256, KC, cast=False, split=False)
                for t2 in range(2):
                    jt = gidx * 2 + t2
                    for k in range(KC):
                        S.mm(pt[:, jt * 2:jt * 2 + 2], w[:, k, t2 * 128:(t2 + 1) * 128], SC[:, k, :],
                             start=(k == 0), stop=(k == KC - 1))
            ab = VEC[:, V_ADAB + l * 24:V_ADAB + (l + 1) * 24]
            S.tt("dve", ADA[:, l, :, :], pt[:, 0:48].rearrange("p (j w) -> p j w", w=2),
                 ab.unsqueeze(2).to_broadcast([128, 24, 2]), ALU.add)

        def layer_mod(l):
            ng = VEC[:, V_NORMG + l * 8:V_NORMG + (l + 1) * 8]
            for w in range(2):
                S.ts("dve", AM[:, w, :], ADA[:, l, 8:16, w], 1.0, None, ALU.add)
                S.tt("dve", AM[:, w, :], AM[:, w, :], ng, ALU.mult)

        pieces = [("lat", 0, 1024, 0), ("lat", 1024, 2048, 1024), ("ctx", 0, 256, 2048)]

        def xview(kind):
            if kind == "lat":
                return lambda k, a, n: XL[:, k, a:a + n]
            return lambda k, a, n: XC[:, k, a:a + n]

        def out_proj(wd, Yp, n, kind, s0, gate_w, l, wkey=None):
            for wg in range(4):
                w = load_w(wd, wg * 256, key=wkey)
                for t2 in range(2):
                    dt = wg * 2 + t2
                    for (o, nb) in nblocks(n):
                        pt = ps_next()
                        for k in range(KC):
                            S.mm(pt[:, 0:nb], w[:, k, t2 * 128:(t2 + 1) * 128], Yp[:, k, o:o + nb],
                                 start=(k == 0), stop=(k == KC - 1))
                        xv = xview(kind)(dt, s0 + o, nb)
                        S.stt("dve", xv, pt[:, 0:nb], ADA[:, l, 16 + dt, gate_w:gate_w + 1], xv, ALU.mult, ALU.add)

        S.dma(VEC[:, :], vec_d)
        S.dma(CST[:, :], cst_d)
        S.copy("dve", CB[:, 0, :], CST[:, C_ID:C_ID + 128])
        S.copy("dve", CB[:, 1, :], CST[:, C_MF:C_MF + 128])
        S.copy("dve", CB[:, 2, :], CST[:, C_MB:C_MB + 128])
        S.memset("dve", CB[:, 3, :], 1.0)
        S.act(SC[:, :, :], VEC[:, V_CVEC:V_CVEC + 16].rearrange("p (k w) -> p k w", w=2), AF.Silu)
        xv = xT_d.rearrange("(k p) t -> p k t", p=128)
        for half in range(2):
            hs_ = slice(half * 1024, (half + 1) * 1024)
            for k in range(KC):
                S.dma(XL[:, k, hs_], xv[:, k, hs_])
            for k in range(KC):
                xv3 = XL[:, k, hs_].rearrange("p (r c) -> p r c", c=64)
                if k < 4:
                    pr_ = CST[:, C_PROW + k * 32 + half * 16:C_PROW + k * 32 + half * 16 + 16]
                    pb = pr_.unsqueeze(2).to_broadcast([128, 16, 64])
                else:
                    pc_ = CST[:, C_PCOL + (k - 4) * 64:C_PCOL + (k - 4) * 64 + 64]
                    pb = pc_.unsqueeze(1).to_broadcast([128, 16, 64])
                S.tt("dve", xv3, xv3, pb, ALU.add)
            if half == 0:
                layer_stats(True, part=0)
                ada_layer(0)
        S.dma(XC[:, :, :], ctxT_d.rearrange("(k p) t -> p k t", p=128))

        def even_layer(l):
            i = l // 2
            need_ctx = NEED_CTX[l]
            layer_mod(l)
            fw = FT[:, 0:128].rearrange("p (t d) -> p t d", t=2)
            S.dma(fw, fnetw_d[i])
            S.memset("pool", WAB[:, :, :], 0.0)
            for t2 in range(2):
                for ab, cofs in ((0, C_BDC), (1, C_BDS)):
                    pt = ps_next(4, 6)
                    S.mm(pt[:, 0:64], CST[:, cofs:cofs + 128], fw[:, t2, :])
                    S.copy("act", WAB[0:64, t2, ab * 128:ab * 128 + 64], pt[0:64, 0:64])
                    S.copy("act", WAB[64:128, t2, ab * 128 + 64:ab * 128 + 128], pt[64:128, 0:64])
            ABtok = BA[:, 22528:22528 + 9216].rearrange("p (c f) -> p c f", f=512)
            assert 22528 + 9216 <= BA_FREE
            def e1_norm(pj):
                kind, s0, s1, goff = pieces[pj]
                n = s1 - s0
                w_ = 0 if kind == "lat" else 1
                HLp = BA[:, 0:KC * n].rearrange("p (k t) -> p k t", k=KC)
                normalize(kind, lambda k, a, nb, H=HLp, s0=s0: H[:, k, a - s0:a - s0 + nb],
                          AM[:, w_, :], ADA[:, l, 0:8, w_], [(s0 + o, nb) for (o, nb) in nblocks(n)])
                S.dma(HLs[:, :, goff:goff + n], HLp, eng=STQ)

            if l != 0:
                layer_stats(True, part=0)
            e1_norm(0)
            layer_stats(True, part=1)
            s5_main = None
            for pi, (kind, s0, s1, goff) in enumerate(pieces):
                last = (pi == len(pieces) - 1)
                if last:
                    S.dma(XLs, XL, eng=STQ)
                    s5_main = s5_phase(l, i)
                n = s1 - s0
                nk = n // 8
                w_ = 0 if kind == "lat" else 1
                HLp = BA[:, 0:KC * n].rearrange("p (k t) -> p k t", k=KC)
                PBLK = BA[:, 8192:8192 + 6144].rearrange("p (g j h) -> p g j h", g=48, j=8)
                UGp = BA[:, 14336:14336 + 48 * nk].rearrange("p (g c) -> p g c", g=48)
                UBp = BA[:, 20480:20480 + 2 * n].rearrange("p (t c) -> p t c", t=2)
                for wg in range(3):
                    w = load_w(ewin_d[i], wg * 256, split=False, key=("ewin", i))
                    for jp in range(4):
                        pt = ps_next()
                        for jj in range(2):
                            j = jp * 2 + jj
                            for k in range(KC):
                                S.mm(pt[0:nk, jj * 256:(jj + 1) * 256], HLp[:, k, j::8], w[:, k, :],
                                     start=(k == 0), stop=(k == KC - 1))
                        S.copy("act" if last else ("act", "dve")[jp % 2], PBLK[0:nk, wg * 16:(wg + 1) * 16, jp * 2:jp * 2 + 2, :],
                               pt[0:nk, 0:512].rearrange("p (j g h) -> p g j h", j=2, g=16))
                for g8 in range(6):
                    pb = psb_next()
                    for gi in range(8):
                        g = g8 * 8 + gi
                        S.tr(pb[:, gi * nk:(gi + 1) * nk], PBLK[0:nk, g, :, :].rearrange("p j h -> p (j h)"),
                             IDB[0:nk, 0:nk])
                    S.copy("act" if last else ("act", "dve")[g8 % 2], UGp[:, g8 * 8:(g8 + 1) * 8, :],
                           pb[:, 0:8 * nk].rearrange("p (g c) -> p g c", g=8))
                S.dma(UGLs[:, :, goff // 8:goff // 8 + nk], UGp, eng=STQ)
                if kind == "ctx" and not need_ctx:
                    continue
                w = load_w(ewin_d[i], 768, key=("ewin", i))
                for t2 in range(2):
                    for (o, nb) in nblocks(n):
                        pt = ps_next()
                        for k in range(KC):
                            S.mm(pt[:, 0:nb], w[:, k, t2 * 128:(t2 + 1) * 128], HLp[:, k, o:o + nb],
                                 start=(k == 0), stop=(k == KC - 1))
                        S.copy("act", UBp[:, t2, o:o + nb], pt[:, 0:nb])
                if pi + 1 < len(pieces):
                    e1_norm(pi + 1)
                for tc in range(n // 128):
                    pt = ps_next()
                    for t2 in range(2):
                        S.mm(pt[:, t2 * 256:(t2 + 1) * 256], UBp[:, t2, tc * 128:(tc + 1) * 128], WAB[:, t2, :])
                    S.copy("act" if last else ("dve", "act")[tc % 2], ABtok[:, goff // 128 + tc, :], pt[:, 0:512])
            chk("E1")
            FNb = BA[:, 0:2 * TT].rearrange("p (t c) -> p t c", t=2)
            DFTB = [BA[:, 4608 + s * 8192:4608 + (s + 1) * 8192] for s in range(2)]
            nrmL = 1.0 / math.sqrt(T * 64.0)
            nrmC = 1.0 / math.sqrt(TC * 64.0)
            for kb in range(8):
                db = DFTB[kb % 2]
                cb = db[:, 0:4096].rearrange("p (c k) -> p c k", c=16)
                sb = db[:, 4096:8192].rearrange("p (c k) -> p c k", c=16)
                S.dma(cb, cosL_d[:, :, kb * 256:(kb + 1) * 256])
                S.dma(sb, nsinL_d[:, :, kb * 256:(kb + 1) * 256])
                for t2 in range(2):
                    pt = ps_next()
                    for tc in range(16):
                        S.mm(pt[:, 0:256], ABtok[:, tc, t2 * 256:t2 * 256 + 128], cb[:, tc, :],
                             start=(tc == 0), stop=False)
                        S.mm(pt[:, 0:256], ABtok[:, tc, t2 * 256 + 128:t2 * 256 + 256], sb[:, tc, :],
                             start=False, stop=(tc == 15))
                    S.act(FNb[:, t2, kb * 256:(kb + 1) * 256], pt[:, 0:256], AF.Copy, scale=nrmL)
            if need_ctx:
                db = DFTB[0]
                cb = db[:, 0:512].rearrange("p (c k) -> p c k", c=2)
                sb = db[:, 512:1024].rearrange("p (c k) -> p c k", c=2)
                S.dma(cb, cosC_d)
                S.dma(sb, nsinC_d)
                for t2 in range(2):
                    pt = ps_next()
                    for tc in range(2):
                        S.mm(pt[:, 0:256], ABtok[:, 16 + tc, t2 * 256:t2 * 256 + 128], cb[:, tc, :],
                             start=(tc == 0), stop=False)
                        S.mm(pt[:, 0:256], ABtok[:, 16 + tc, t2 * 256 + 128:t2 * 256 + 256], sb[:, tc, :],
                             start=False, stop=(tc == 1))
                    S.act(FNb[:, t2, T:T + TC], pt[:, 0:256], AF.Copy, scale=nrmC)
                S.dma(FNs, FNb, eng=STQ)
            else:
                S.dma(FNs[:, :, 0:T], FNb[:, :, 0:T], eng=STQ)

            chk("E2")
            s5_main()
            rr["hi"] = 4
            chk("E3")

            for pi, (kind, s0, s1, goff) in enumerate(pieces):
                if kind == "ctx" and not need_ctx:
                    continue
                n = s1 - s0
                w_ = 0 if kind == "lat" else 1
                HLp = BA[:, 0:KC * n].rearrange("p (k t) -> p k t", k=KC)
                YGp = BA[:, 8192:8192 + 6 * n].rearrange("p (k t) -> p k t", k=6)
                FNp = BA[:, 14336:14336 + 2 * n].rearrange("p (k t) -> p k t", k=2)
                Yp = BA[:, 16384:16384 + KC * n].rearrange("p (k t) -> p k t", k=KC)
                TMP = [BA[:, 24576 + s * 512:24576 + (s + 1) * 512] for s in range(4)]
                if kind == "lat":
                    S.dma(XL[:, :, s0:s1], XLs[:, :, s0:s1], eng=LDQ)
                S.dma(YGp, YGs[:, :, goff:goff + n], eng=LDQ)
                S.dma(HLp, HLs[:, :, goff:goff + n], eng=LDQ)
                S.dma(FNp, FNs[:, :, goff:goff + n], eng=LDQ)
                tctr = 0
                for gw in range(3):
                    w = load_w(glu_d[i], gw * 256, 256, 6, key=("glu", i))
                    for t2 in range(2):
                        cp = gw * 2 + t2
                        for (o, nb) in nblocks(n):
                            pt = ps_next()
                            for k in range(6):
                                S.mm(pt[:, 0:nb], w[:, k, t2 * 128:(t2 + 1) * 128], YGp[:, k, o:o + nb],
                                     start=(k == 0), stop=(k == 5))
                            sg = TMP[tctr % 4][:, 0:nb]
                            tctr += 1
                            S.act(sg, pt[:, 0:nb], AF.Sigmoid, bias=VEC[:, V_GLUB + i * 6 + cp:V_GLUB + i * 6 + cp + 1])
                            S.tt(ev_eng(), Yp[:, cp, o:o + nb], YGp[:, cp, o:o + nb], sg, ALU.mult)
                for wg in range(4):
                    w = load_w(ewin_d[i], 1024 + wg * 256, key=("ewin", i))
                    for t2 in range(2):
                        zt = wg * 2 + t2
                        for (o, nb) in nblocks(n):
                            pt = ps_next()
                            for k in range(KC):
                                S.mm(pt[:, 0:nb], w[:, k, t2 * 128:(t2 + 1) * 128], HLp[:, k, o:o + nb],
                                     start=(k == 0), stop=(k == KC - 1))
                            zs = TMP[tctr % 4][:, 0:nb]
                            tctr += 1
                            S.act(zs, pt[:, 0:nb], AF.Silu)
                            src = Yp[:, zt, o:o + nb] if zt < 6 else FNp[:, zt - 6, o:o + nb]
                            S.tt(ev_eng(), Yp[:, zt, o:o + nb], src, zs, ALU.mult)
                out_proj(ewout_d[i], Yp, n, kind, s0, w_, l, wkey=("ewout", i))

        def s5_phase(l, i):
            Z = XLA[:, 0:6912].rearrange("p (q c) -> p q c", q=96)
            RAW = XLA[:, 0:6144].rearrange("p (w q h) -> p w q h", w=4, q=96)
            SS = XLA[:, 6144:6144 + 288].rearrange("p (w q) -> p w q", w=3)
            E1r = XLA[:, 6912:8448].rearrange("p (q k) -> p q k", q=96)
            E1i = XLA[:, 8448:9984].rearrange("p (q k) -> p q k", q=96)
            E8r = XLA[:, 9984:10464].rearrange("p (q k) -> p q k", q=96)
            E8i = XLA[:, 10464:10944].rearrange("p (q k) -> p q k", q=96)
            NAIs = XLA[:, 10944:11040]
            SM = [XLA[:, 11040 + s_ * 96:11040 + (s_ + 1) * 96] for s_ in range(16)]
            TQ = XLA[:, 12576:14112].rearrange("p (q h) -> p q h", q=96)
            tw = [XLA[:, 14112 + s_ * 384:14112 + (s_ + 1) * 384] for s_ in range(2)]
            X1b = FA[:, 0:1536].rearrange("p (q h) -> p q h", q=96)
            X2b = FA[:, 1536:3072].rearrange("p (q h) -> p q h", q=96)
            X1c = FA[:, 3072:4608].rearrange("p (q h) -> p q h", q=96)
            X2c = FA[:, 4608:6144].rearrange("p (q h) -> p q h", q=96)
            S.dma(RAW, s5bc_d[i], eng=LDQ)
            S.dma(SS, s5s_d[i], eng=LDQ)
            LR, LI, LS = SS[:, 0, :], SS[:, 1, :], SS[:, 2, :]
            (step, xr, xi, mag, sn, cs, t0, t1, t2, lbr, lbi, qr, qi, t3, t4, t5) = SM
            dv = "dve"
            S.act(step, LS, AF.Exp)
            S.tt(dv, xr, LR, step, ALU.mult)
            S.tt(dv, xi, LI, step, ALU.mult)
            S.act(mag, xr, AF.Exp)
            PI2 = 2.0 * math.pi

            def sin_of(dst, src, shift):
                S.ts(dv, t0, src, 1.0 / PI2, shift / PI2, ALU.mult, ALU.add)
                S.copy(dv, IT[:, :], t0)
                S.copy(dv, t1, IT[:, :])
                S.ts(dv, t2, src, shift, None, ALU.add)
                S.stt(dv, t2, t1, -PI2, t2, ALU.mult, ALU.add)
                S.ts(dv, t2, t2, -3.14159, 3.14159, ALU.max, ALU.min)
                S.act(dst, t2, AF.Sin)

            sin_of(sn, xi, 0.0)
            sin_of(cs, xi, math.pi / 2.0)
            S.tt(dv, lbr, mag, cs, ALU.mult)
            S.tt(dv, lbi, mag, sn, ALU.mult)
            S.ts(dv, t0, lbr, -1.0, None, ALU.add)
            S.tt(dv, t1, LR, LR, ALU.mult)
            S.tt(dv, t2, LI, LI, ALU.mult)
            S.tt(dv, t1, t1, t2, ALU.add)
            S.recip(t1, t1)
            S.tt(dv, t2, t0, LR, ALU.mult)
            S.tt(dv, t3, lbi, LI, ALU.mult)
            S.tt(dv, t2, t2, t3, ALU.add)
            S.tt(dv, qr, t2, t1, ALU.mult)
            S.tt(dv, t2, lbi, LR, ALU.mult)
            S.tt(dv, t3, t0, LI, ALU.mult)
            S.tt(dv, t2, t2, t3, ALU.subtract)
            S.tt(dv, qi, t2, t1, ALU.mult)
            BRr, BIr, CRr, CIr = RAW[:, 0], RAW[:, 1], RAW[:, 2], RAW[:, 3]
            lo, hi = slice(0, 64), slice(64, 128)

            def bq(a2, psl):
                return a2[psl].unsqueeze(2).to_broadcast([64, 96, 16])

            S.tt(dv, X1b[lo], BRr[lo], bq(qr, lo), ALU.mult)
            S.tt(dv, TQ[lo], BIr[lo], bq(qi, lo), ALU.mult)
            S.tt(dv, X1b[lo], X1b[lo], TQ[lo], ALU.subtract)
            S.tt(dv, X1b[hi], BIr[hi], bq(qr, hi), ALU.mult)
            S.tt(dv, TQ[hi], BRr[hi], bq(qi, hi), ALU.mult)
            S.tt(dv, X1b[hi], X1b[hi], TQ[hi], ALU.add)
            S.tt(dv, X2b[lo], BIr[lo], bq(qr, lo), ALU.mult)
            S.tt(dv, TQ[lo], BRr[lo], bq(qi, lo), ALU.mult)
            S.tt(dv, X2b[lo], X2b[lo], TQ[lo], ALU.add)
            S.ts(dv, X2b[lo], X2b[lo], -1.0, None, ALU.mult)
            S.tt(dv, X2b[hi], BRr[hi], bq(qr, hi), ALU.mult)
            S.tt(dv, TQ[hi], BIr[hi], bq(qi, hi), ALU.mult)
            S.tt(dv, X2b[hi], X2b[hi], TQ[hi], ALU.subtract)
            S.act(X1c[lo], CRr[lo], AF.Copy)
            S.act(X1c[hi], CIr[hi], AF.Copy, scale=-1.0)
            S.act(X2c[lo], CIr[lo], AF.Copy, scale=-1.0)
            S.act(X2c[hi], CRr[hi], AF.Copy, scale=-1.0)

            def cmul(eng, outr, outi, ar, ai, br, bi, ta, tb):
                S.tt(eng, ta, ar, br, ALU.mult)
                S.tt(eng, tb, ai, bi, ALU.mult)
                S.tt(eng, outr, ta, tb, ALU.subtract)
                S.tt(eng, ta, ar, bi, ALU.mult)
                S.tt(eng, tb, ai, br, ALU.mult)
                S.tt(eng, outi, ta, tb, ALU.add)

            def bc(ap2, n_):
                return ap2.unsqueeze(2).to_broadcast([128, 96, n_])

            def twv(s_, n_):
                return tw[s_][:, 0:96 * n_].rearrange("p (q k) -> p q k", q=96)

            S.memset(dv, E1r[:, :, 7:8], 1.0)
            S.memset(dv, E1i[:, :, 7:8], 0.0)
            S.copy(dv, E1r[:, :, 8], lbr)
            S.copy(dv, E1i[:, :, 8], lbi)
            cmul(dv, E1r[:, :, 9:10], E1i[:, :, 9:10], E1r[:, :, 8:9], E1i[:, :, 8:9], E1r[:, :, 8:9], E1i[:, :, 8:9],
                 twv(0, 1), twv(1, 1))
            cmul(dv, E1r[:, :, 10:12], E1i[:, :, 10:12], E1r[:, :, 8:10], E1i[:, :, 8:10],
                 bc(E1r[:, :, 9], 2), bc(E1i[:, :, 9], 2), twv(0, 2), twv(1, 2))
            cmul(dv, E1r[:, :, 12:16], E1i[:, :, 12:16], E1r[:, :, 8:12], E1i[:, :, 8:12],
                 bc(E1r[:, :, 11], 4), bc(E1i[:, :, 11], 4), twv(0, 4), twv(1, 4))
            S.tt(dv, t0, mag, mag, ALU.mult)
            S.recip(t0, t0)
            S.tt(dv, E1r[:, :, 6], lbr, t0, ALU.mult)
            S.tt(dv, t1, lbi, t0, ALU.mult)
            S.ts(dv, E1i[:, :, 6], t1, -1.0, None, ALU.mult)
            cmul(dv, E1r[:, :, 5:6], E1i[:, :, 5:6], E1r[:, :, 6:7], E1i[:, :, 6:7], E1r[:, :, 6:7], E1i[:, :, 6:7],
                 twv(0, 1), twv(1, 1))
            cmul(dv, E1r[:, :, 3:5], E1i[:, :, 3:5], E1r[:, :, 5:7], E1i[:, :, 5:7],
                 bc(E1r[:, :, 5], 2), bc(E1i[:, :, 5], 2), twv(0, 2), twv(1, 2))
            cmul(dv, E1r[:, :, 0:3], E1i[:, :, 0:3], E1r[:, :, 4:7], E1i[:, :, 4:7],
                 bc(E1r[:, :, 3], 3), bc(E1i[:, :, 3], 3), twv(0, 3), twv(1, 3))
            S.memset(dv, E8r[:, :, 0:1], 1.0)
            S.memset(dv, E8i[:, :, 0:1], 0.0)
            S.copy(dv, E8r[:, :, 1], E1r[:, :, 15])
            S.copy(dv, E8i[:, :, 1], E1i[:, :, 15])
            cmul(dv, E8r[:, :, 2:3], E8i[:, :, 2:3], E8r[:, :, 1:2], E8i[:, :, 1:2], E8r[:, :, 1:2], E8i[:, :, 1:2],
                 twv(0, 1), twv(1, 1))
            cmul(dv, E8r[:, :, 3:4], E8i[:, :, 3:4], E8r[:, :, 2:3], E8i[:, :, 2:3], E8r[:, :, 1:2], E8i[:, :, 1:2],
                 twv(0, 1), twv(1, 1))
            cmul(dv, E8r[:, :, 4:5], E8i[:, :, 4:5], E8r[:, :, 2:3], E8i[:, :, 2:3], E8r[:, :, 2:3], E8i[:, :, 2:3],
                 twv(0, 1), twv(1, 1))
            S.ts(dv, NAIs[0:64, :], E8i[0:64, :, 4], -1.0, None, ALU.mult)
            S.copy(dv, NAIs[64:128, :], E8i[64:128, :, 4])

            SPf = BA[:, 0:6912].rearrange("p (q c) -> p q c", q=96)
            UGb = [BA[:, 6912 + s_ * 1152:6912 + (s_ + 1) * 1152].rearrange("p (g c) -> p g c", g=4) for s_ in range(2)]
            o_ = 9216
            WZT = [BA[:, o_ + s_ * 2048:o_ + (s_ + 1) * 2048].rearrange("p (g f) -> p g f", g=4) for s_ in range(4)]
            WZ = [BA[:, o_ + 8192 + s_ * 512:o_ + 8192 + (s_ + 1) * 512].rearrange("p (a f) -> p a f", a=4)
                  for s_ in range(2)]
            TBA = [BA[:, o_ + 9216 + s_ * 2048:o_ + 9216 + (s_ + 1) * 2048] for s_ in range(4)]
            assert o_ + 9216 + 8192 <= 36864
            RB = [BA[:, o_ + s_ * 2048:o_ + (s_ + 1) * 2048].rearrange("p (g f) -> p g f", g=4) for s_ in range(4)]
            o_ += 8192
            QB = [BA[:, o_ + s_ * 512:o_ + (s_ + 1) * 512].rearrange("p (g f) -> p g f", g=4) for s_ in range(4)]
            o_ += 2048
            MB_ = [BA[:, o_ + s_ * 2048:o_ + (s_ + 1) * 2048].rearrange("p (g f) -> p g f", g=4) for s_ in range(2)]
            o_ += 4096
            PG = BA[:, o_:o_ + 4096].rearrange("p (t g h) -> p t g h", t=32, g=8)
            o_ += 4096
            YGT = [BA[:, o_:o_ + TT]]
            o_ += TT
            TBB = [BA[:, o_ + 1024 + s_ * 2048:o_ + 1024 + (s_ + 1) * 2048] for s_ in range(2)]
            assert o_ + 1024 + 4096 <= 36864, o_
            PTs = [XLA[:, 11040 + s_ * 128:11040 + (s_ + 1) * 128] for s_ in range(8)]
            BIG = [XLA[:, 12064 + s_ * 2048:12064 + (s_ + 1) * 2048] for s_ in range(2)]
            SCN = [XLA[:, 12064 + s_ * 48:12064 + (s_ + 1) * 48] for s_ in range(8)]
            QTMP = [XLA[:, s_ * 512:(s_ + 1) * 512] for s_ in range(2)]
            GB = 4

            def ptable(eng, slot, q0, e8sl, e1sl):
                def v4(t_):
                    return t_[:, 0:GB * 32].rearrange("p (g x y) -> p g x y", g=GB, x=4)
                pr = v4(PTs[slot * 2])
                pi_ = v4(PTs[slot * 2 + 1])
                a_r = E8r[:, q0:q0 + GB, e8sl].unsqueeze(3).to_broadcast([128, GB, 4, 8])
                a_i = E8i[:, q0:q0 + GB, e8sl].unsqueeze(3).to_broadcast([128, GB, 4, 8])
                b_r = E1r[:, q0:q0 + GB, e1sl].unsqueeze(2).to_broadcast([128, GB, 4, 8])
                b_i = E1i[:, q0:q0 + GB, e1sl].unsqueeze(2).to_broadcast([128, GB, 4, 8])
                cmul(eng, pr, pi_, a_r, a_i, b_r, b_i, v4(PTs[4 + slot * 2]), v4(PTs[5 + slot * 2]))
                return (PTs[slot * 2][:, 0:GB * 32].rearrange("p (g k) -> p g k", g=GB),
                        PTs[slot * 2 + 1][:, 0:GB * 32].rearrange("p (g k) -> p g k", g=GB))

            def bigtable(eng, out_bf, pr, pi_, X1, X2, q0, nk, ta_, tb_, eng2=None):
                n_ = GB * nk * 16
                ta = ta_[:, 0:n_].rearrange("p (g k h) -> p g k h", g=GB, k=nk)
                tb = tb_[:, 0:n_].rearrange("p (g k h) -> p g k h", g=GB, k=nk)
                prb = pr.unsqueeze(3).to_broadcast([128, GB, nk, 16])
                pib = pi_.unsqueeze(3).to_broadcast([128, GB, nk, 16])
                x1 = X1[:, q0:q0 + GB, :].unsqueeze(2).to_broadcast([128, GB, nk, 16])
                x2 = X2[:, q0:q0 + GB, :].unsqueeze(2).to_broadcast([128, GB, nk, 16])
                S.tt(eng, ta, prb, x1, ALU.mult)
                if eng2 is None or eng2 == eng:
                    S.tt(eng, tb, pib, x2, ALU.mult)
                else:
                    hg = GB // 2
                    S.tt(eng2, tb[:, 0:hg], pib[:, 0:hg], x2[:, 0:hg], ALU.mult)
                    S.tt(eng, tb[:, hg:GB], pib[:, hg:GB], x2[:, hg:GB], ALU.mult)
                S.tt(eng, out_bf[:, :, 0:nk * 16].rearrange("p g (k h) -> p g k h", k=nk), ta, tb, ALU.add)

            def tbl_mults(out_bf, pr, pi_, X1, X2, q0, tb_, eng_b):
                nk = 32
                o4 = out_bf[:, :, 0:512].rearrange("p g (k h) -> p g k h", k=nk)
                tb = tb_[:, 0:GB * 512].rearrange("p (g k h) -> p g k h", g=GB, k=nk)
                prb = pr.unsqueeze(3).to_broadcast([128, GB, nk, 16])
                pib = pi_.unsqueeze(3).to_broadcast([128, GB, nk, 16])
                x1 = X1[:, q0:q0 + GB, :].unsqueeze(2).to_broadcast([128, GB, nk, 16])
                x2 = X2[:, q0:q0 + GB, :].unsqueeze(2).to_broadcast([128, GB, nk, 16])
                S.tt("dve", o4, prb, x1, ALU.mult)
                S.tt(eng_b, tb, pib, x2, ALU.mult)

            def tbl_add(out_bf, tb_):
                o2 = out_bf[:, :, 0:512]
                S.tt("dve", o2, o2, tb_[:, 0:GB * 512].rearrange("p (g f) -> p g f", g=GB), ALU.add)

            DESC4 = slice(3, None, -1)

            def _main():
                rr["hi"] = 3
                chk("E3prep")
                for gb in range(12):
                    g0 = gb * GB
                    ug = UGb[gb % 2]
                    S.dma(ug, UGLs[:, g0:g0 + GB, :])
                    for dr in range(2):
                        q0 = dr * 48 + g0
                        if dr == 0:
                            pr, pi_ = ptable("dve", dr, q0, DESC4, slice(15, 7, -1))
                        else:
                            pr, pi_ = ptable("dve", dr, q0, slice(1, 5), slice(7, 15))
                        tbl_mults(WZT[(gb % 2) * 2 + dr], pr, pi_, X1b, X2b, q0, TBA[(gb % 2) * 2 + dr], "dve")
                    for dr in range(2):
                        q0 = dr * 48 + g0
                        wzt = WZT[(gb % 2) * 2 + dr]
                        tbl_add(wzt, TBA[(gb % 2) * 2 + dr])
                        pz = PS[4 + dr]
                        for gl in range(GB):
                            pb = psb_next()
                            for a in range(4):
                                S.tr(pb[:, a * 128:(a + 1) * 128], wzt[:, gl, a * 128:(a + 1) * 128], IDB)
                            wz = WZ[gl % 2]
                            S.copy("act", wz, pb[:, 0:512].rearrange("p (a f) -> p a f", a=4))
                            for a in range(4):
                                S.mm(pz[:, gl * 72:(gl + 1) * 72], wz[:, a, :], ug[:, gl, a::4],
                                     start=(a == 0), stop=(a == 3))
                        S.copy("act", Z[:, q0:q0 + GB, :], pz[:, 0:GB * 72].rearrange("p (g c) -> p g c", g=GB))
                    ada_steps(l + 2, 2)

                chk("E3A")
                orders = [list(range(64, 72)) + list(range(0, 64)),
                          list(range(71, 63, -1)) + list(range(63, -1, -1))]
                prevs = [None, None]
                for st_i in range(NCH):
                    for dr in range(2):
                        eng = ("dve", "pool")[dr]
                        qs = slice(dr * 48, (dr + 1) * 48)
                        AR = E8r[:, qs, 4]
                        AIs_ = NAIs[:, qs]
                        SBu = SCN[dr * 4 + 0]
                        tA = SCN[dr * 4 + 1]
                        tB = SCN[dr * 4 + 2]
                        c = orders[dr][st_i]
                        prev = prevs[dr]
                        if prev is not None:
                            P = Z[:, qs, prev]
                            ce_ = "act" if dr == 1 else eng
                            S.copy(ce_, SBu[0:64, :], Z[64:128, qs, prev])
                            S.copy(ce_, SBu[64:128, :], Z[0:64, qs, prev])
                            S.tt(eng, tA, AR, P, ALU.mult)
                            S.tt(eng, tB, AIs_, SBu, ALU.mult)
                            S.tt(eng, tA, tA, tB, ALU.add)
                            S.tt(eng, Z[:, qs, c], Z[:, qs, c], tA, ALU.add)
                        prevs[dr] = c
                S.memset("dve", SPf[:, 0:48, 64:65], 0.0)
                S.copy("dve", SPf[:, 0:48, 65:72], Z[:, 0:48, 64:71])
                S.copy("dve", SPf[:, 0:48, 0:1], Z[:, 0:48, 71:72])
                S.copy("dve", SPf[:, 0:48, 1:64], Z[:, 0:48, 0:63])
                S.memset("pool", SPf[:, 48:96, 71:72], 0.0)
                S.copy("act", SPf[:, 48:96, 64:71], Z[:, 48:96, 65:72])
                S.copy("act", SPf[:, 48:96, 0:64], Z[:, 48:96, 1:65])

                ada_steps(l + 1, 24)
                chk("E3S")
                dtab = VEC[:, V_DTAB + i * 48:V_DTAB + (i + 1) * 48]
                DGD = BA[:, o_:o_ + 1024].rearrange("p (g f) -> p g f", g=8)
                assert o_ + 1024 <= 36864

                def gen_tables(n_):
                    g0 = n_ * GB
                    sl = n_ % 2
                    S.dma(UGb[sl], UGLs[:, g0:g0 + GB, :])
                    for dr in range(2):
                        q0 = dr * 48 + g0
                        bs = sl * 2 + dr
                        rs_ = (n_ % 2) * 2 + dr
                        pr = PTRr[:, q0:q0 + GB, :]
                        pi_ = PTRi[:, q0:q0 + GB, :]
                        tbl_mults(RB[rs_], pr, pi_, X1c, X2c, q0, TBB[dr], "dve")
                    for dr in range(2):
                        q0 = dr * 48 + g0
                        bs = sl * 2 + dr
                        rs_ = (n_ % 2) * 2 + dr
                        e1 = slice(7, None, -1) if dr == 0 else slice(7, 15)
                        qe = "dve"
                        bigtable(qe, QB[bs], E1r[:, q0:q0 + GB, e1], E1i[:, q0:q0 + GB, e1], X1b, X2b, q0, 8,
                                 QTMP[dr * 2], QTMP[dr * 2 + 1])
                    for dr in range(2):
                        rs_ = (n_ % 2) * 2 + dr
                        tbl_add(RB[rs_], TBB[dr])

                def m_chain(n_):
                    sl = n_ % 2
                    for dr in range(2):
                        bs = sl * 2 + dr
                        rs_ = (n_ % 2) * 2 + dr
                        for gl in range(GB):
                            pm = ps_next()
                            S.mm(pm[:, 0:512], QB[bs][:, gl, :], RB[rs_][:, gl, :])
                            mdst = MB_[dr][:, gl, :]
                            S.copy("act", mdst[:, 0:512], pm[:, 0:512])
                            if dr == 0:
                                S.tt("dve", mdst[:, 0:128], mdst[:, 0:128], MSKF, ALU.mult)
                            else:
                                S.tt("dve", mdst[:, 384:512], mdst[:, 384:512], MSKB, ALU.mult)

                def y_part(n_):
                    g0 = n_ * GB
                    sl = n_ % 2
                    ug = UGb[sl]
                    hb = n_ % 2
                    for gl in range(GB):
                        g = g0 + gl
                        gi = hb * GB + gl
                        py = PS[4 + g % 2]
                        bf_, bb_ = 0, 1
                        rf_, rb_ = (n_ % 2) * 2 + 0, (n_ % 2) * 2 + 1
                        S.mm(py[0:NCH, 0:512], SPf[:, g, :], RB[rf_][:, gl, :], start=True, stop=False)
                        S.mm(py[0:NCH, 0:512], SPf[:, 48 + g, :], RB[rb_][:, gl, :], start=False, stop=False)
                        for a2 in range(4):
                            lh = ug[:, gl, a2::4]
                            S.mm(py[0:NCH, a2 * 128:(a2 + 1) * 128], lh, DGD[:, gi, :], start=False, stop=False)
                            S.mm(py[0:NCH, a2 * 128:512], lh, MB_[bf_][:, gl, 0:(4 - a2) * 128], start=False, stop=False)
                            S.mm(py[0:NCH, 0:(a2 + 1) * 128], lh, MB_[bb_][:, gl, (3 - a2) * 128:512],
                                 start=False, stop=(a2 == 3))
                        S.act(PG[0:NCH, :, gi, :], py[0:NCH, 0:512].rearrange("p (t h) -> p t h", t=32),
                              AF.Gelu_apprx_tanh)

                PTRr = XLA[:, 0:3072].rearrange("p (q k) -> p q k", q=96)
                PTRi = XLA[:, 3072:6144].rearrange("p (q k) -> p q k", q=96)
                ptmp = [XLA[:, 12064 + s_ * 1536:12064 + (s_ + 1) * 1536] for s_ in range(2)]

                def v4f(t_, q0_):
                    return t_[:, q0_:q0_ + 48, :].rearrange("p q (x y) -> p q x y", x=4)

                def pv(t_):
                    return t_[:, 0:1536].rearrange("p (q x y) -> p q x y", q=48, x=4)

                for dr, (e8sl, e1sl) in enumerate(((slice(0, 4), slice(7, 15)), (DESC4, slice(7, None, -1)))):
                    q0_ = dr * 48
                    a_r = E8r[:, q0_:q0_ + 48, e8sl].unsqueeze(3).to_broadcast([128, 48, 4, 8])
                    a_i = E8i[:, q0_:q0_ + 48, e8sl].unsqueeze(3).to_broadcast([128, 48, 4, 8])
                    b_r = E1r[:, q0_:q0_ + 48, e1sl].unsqueeze(2).to_broadcast([128, 48, 4, 8])
                    b_i = E1i[:, q0_:q0_ + 48, e1sl].unsqueeze(2).to_broadcast([128, 48, 4, 8])
                    cmul("dve", v4f(PTRr, q0_), v4f(PTRi, q0_), a_r, a_i, b_r, b_i, pv(ptmp[0]), pv(ptmp[1]))
                QTMP = [XLA[:, 12064 + s_ * 512:12064 + (s_ + 1) * 512] for s_ in range(4)]
                gen_tables(0)
                for n_ in range(12):
                    ct = n_ // 2
                    if n_ % 2 == 0:
                        for gi in range(8):
                            S.act(DGD[:, gi, :], IDF, AF.Copy, scale=dtab[:, ct * 8 + gi:ct * 8 + gi + 1])
                    m_chain(n_)
                    if n_ + 1 < 12:
                        gen_tables(n_ + 1)
                    chk("B_tab")
                    y_part(n_)
                    if n_ % 2 == 1:
                        ygt = YGT[0]
                        ygv = ygt.rearrange("p (c t) -> p t c", t=32)
                        for t0_, nt in ((0, 14), (14, 14), (28, 4)):
                            pb = psb_next()
                            for tk in range(nt):
                                S.tr(pb[:, tk * NCH:(tk + 1) * NCH],
                                     PG[0:NCH, t0_ + tk, :, :].rearrange("p g h -> p (g h)"), IDB[0:NCH, 0:NCH])
                            S.copy(("act", "dve")[(t0_ // 14) % 2], ygv[:, t0_:t0_ + nt, :],
                                   pb[:, 0:nt * NCH].rearrange("p (t c) -> p t c", t=nt))
                        S.dma(YGs[:, ct, :], ygt, eng=STQ)

            return _main

        def odd_layer(l):
            i = l // 2
            need_ctx = NEED_CTX[l]
            layer_mod(l)
            pw = FT[:, 0:512].rearrange("p (t d) -> p t d", t=4)
            S.dma(pw, poolw_d[i])
            for t_ in range(4):
                S.ts("dve", PWS[:, 0, t_, :], pw[:, t_, :], 1.0 / POOL_W[t_], None, ALU.mult)
                S.ts("dve", PWS[:, 1, t_, :], pw[:, t_, :], -1.0, None, ALU.mult)
                for k in range(3):
                    col = V_CONVW + (i * 3 + k) * 4 + t_
                    S.act(DG[:, t_, k, :], IDF, AF.Copy, scale=VEC[:, col:col + 1])
            act_pieces = [pj for pj, pc in enumerate(pieces) if not (pc[0] == "ctx" and not need_ctx)]

            def odd_norm(pj):
                kind, s0, s1, goff = pieces[pj]
                n = s1 - s0
                w_ = 0 if kind == "lat" else 1
                hl_ = 8 if (kind == "lat" and s0 > 0) else 0
                hr_ = 8 if (kind == "lat" and s1 < T) else 0
                ne = n + hl_ + hr_
                e0 = s0 - hl_
                HLe = BA[:, 0:KC * ne].rearrange("p (k t) -> p k t", k=KC)
                HALO = BA[:, 29184:29248].rearrange("p (k c) -> p k c", k=KC)
                hb0 = e0 + hl_
                normalize(kind, lambda k, a, nb, H=HLe, e0=e0: H[:, k, a - e0:a - e0 + nb],
                          AM[:, w_, :], ADA[:, l, 0:8, w_], [(hb0 + o, nb) for (o, nb) in nblocks(ne - hl_)])
                if kind == "lat" and s0 == 0:
                    S.copy("pool", HALO, HLe[:, :, 1016:1024])
                if hl_:
                    S.copy("pool", HLe[:, :, 0:hl_], HALO)

            layer_stats(need_ctx)
            for pi, (kind, s0, s1, goff) in enumerate(pieces):
                if kind == "ctx" and not need_ctx:
                    continue
                n = s1 - s0
                w_ = 0 if kind == "lat" else 1
                hl_ = 8 if (kind == "lat" and s0 > 0) else 0
                hr_ = 8 if (kind == "lat" and s1 < T) else 0
                ne = n + hl_ + hr_
                e0 = s0 - hl_
                left_edge = (hl_ == 0)
                right_edge = (hr_ == 0)
                HLe = BA[:, 0:KC * ne].rearrange("p (k t) -> p k t", k=KC)
                ZS = BA[:, 8320:8320 + KC * n].rearrange("p (k t) -> p k t", k=KC)
                wu = ne + 16
                UCb = BA[:, 16512:16512 + 4 * wu].rearrange("p (t c) -> p t c", t=4)
                wc_ = ne + 2
                CXb = BA[:, 20736:20736 + 4 * wc_].rearrange("p (t c) -> p t c", t=4)
                XPb = BA[:, 24896:24896 + 2 * ne].rearrange("p (t c) -> p t c", t=2)
                SA_ = BA[:, 26976:26976 + wu]
                SB_ = BA[:, 28032:28032 + wu]
                SQ = BA[:, 16512:16512 + 4096].rearrange("p (k c) -> p k c", k=KC)
                assert 28032 + wu <= BA_FREE
                HALO = BA[:, 29184:29248].rearrange("p (k c) -> p k c", k=KC)
                if pi == act_pieces[0]:
                    odd_norm(pi)
                core = lambda o, nb, hl_=hl_: slice(hl_ + o, hl_ + o + nb)

                def inproj(wv, t2, mov, nb):
                    pt = ps_next()
                    for k in range(KC):
                        S.mm(pt[:, 0:nb], wv[:, k, t2 * 128:(t2 + 1) * 128], mov(k), start=(k == 0), stop=(k == KC - 1))
                    return pt

                S.memset("pool", UCb[:, :, 0:8], 0.0)
                S.memset("pool", UCb[:, :, 8 + ne:16 + ne], 0.0)
                for wg in range(2):
                    w = load_w(owin_d[i], wg * 256, key=("owin", i))
                    for t2 in range(2):
                        t_ = wg * 2 + t2
                        for (o, nb) in nblocks(ne):
                            pt = inproj(w, t2, lambda k, o=o, nb=nb: HLe[:, k, o:o + nb], nb)
                            S.copy("act", UCb[:, t_, 8 + o:8 + o + nb], pt[:, 0:nb])
                c0 = 8 + hl_
                for t_ in range(4):
                    u = UCb[:, t_, :]
                    pe_ = "dve"
                    S.tt(pe_, SA_[:, 1:wu], u[:, 0:wu - 1], u[:, 1:wu], ALU.add)
                    cur, oth = SA_, SB_
                    if t_ >= 1:
                        S.tt(pe_, SB_[:, 2:wu - 1], SA_[:, 1:wu - 2], SA_[:, 3:wu], ALU.add)
                        cur, oth = SB_, SA_
                    if t_ >= 2:
                        S.tt(pe_, SA_[:, 4:wu - 3], SB_[:, 2:wu - 5], SB_[:, 6:wu - 1], ALU.add)
                        cur, oth = SA_, SB_
                    if t_ >= 3:
                        S.tt(pe_, SB_[:, 8:wu - 7], SA_[:, 4:wu - 11], SA_[:, 12:wu - 3], ALU.add)
                        cur, oth = SB_, SA_
                    fx = CST[:, C_FIX + t_ * 16:C_FIX + (t_ + 1) * 16]
                    if left_edge:
                        S.tt(pe_, cur[:, c0:c0 + 8], cur[:, c0:c0 + 8], fx[:, 0:8], ALU.mult)
                    if right_edge:
                        S.tt(pe_, cur[:, c0 + n - 8:c0 + n], cur[:, c0 + n - 8:c0 + n], fx[:, 8:16], ALU.mult)
                    S.ts(pe_, oth[:, c0:c0 + n], u[:, c0:c0 + n], -float(POOL_W[t_]), None, ALU.mult)
                    S.tt(pe_, u[:, c0:c0 + n], cur[:, c0:c0 + n], oth[:, c0:c0 + n], ALU.add)
                for wg in range(4):
                    w = load_w(owin_d[i], 2048 + wg * 256, key=("owin", i))
                    for t2 in range(2):
                        zt = wg * 2 + t2
                        for (o, nb) in nblocks(n):
                            pt = inproj(w, t2, lambda k, o=o, nb=nb: HLe[:, k, core(o, nb)], nb)
                            S.act(ZS[:, zt, o:o + nb], pt[:, 0:nb], AF.Silu)
                for wg in range(2):
                    w = load_w(owin_d[i], 1536 + wg * 256, key=("owin", i))
                    for t2 in range(2):
                        t_ = wg * 2 + t2
                        for (o, nb) in nblocks(n):
                            pt = inproj(w, t2, lambda k, o=o, nb=nb: HLe[:, k, core(o, nb)], nb)
                            S.tt("dve", ZS[:, 4 + t_, o:o + nb], pt[:, 0:nb], ZS[:, 4 + t_, o:o + nb], ALU.mult)
                for t_ in range(4):
                    for (o, nb) in nblocks(n):
                        pt = ps_next()
                        S.mm(pt[:, 0:nb], PWS[:, 0, t_, :], UCb[:, t_, c0 + o:c0 + o + nb])
                        S.stt("dve", ZS[:, t_, o:o + nb], pt[:, 0:nb],
                              VEC[:, V_PSCALE + i * 4 + t_:V_PSCALE + i * 4 + t_ + 1], ZS[:, t_, o:o + nb],
                              ALU.mult, ALU.mult)
                S.memset("pool", CXb[:, :, 0:1], 0.0)
                S.memset("pool", CXb[:, :, 1 + ne:2 + ne], 0.0)
                for wg in range(2):
                    wx = load_w(owin_d[i], 1024 + wg * 256, key=("owin", i))
                    for t2 in range(2):
                        for (o, nb) in nblocks(ne):
                            pt = inproj(wx, t2, lambda k, o=o, nb=nb: HLe[:, k, o:o + nb], nb)
                            S.copy("act", XPb[:, t2, o:o + nb], pt[:, 0:nb])
                    wc = load_w(owin_d[i], 512 + wg * 256, key=("owin", i))
                    for t2 in range(2):
                        t_ = wg * 2 + t2
                        for (o, nb) in nblocks(ne):
                            pt = inproj(wc, t2, lambda k, o=o, nb=nb: HLe[:, k, o:o + nb], nb)
                            S.tt("dve", CXb[:, t_, 1 + o:1 + o + nb], pt[:, 0:nb], XPb[:, t2, o:o + nb], ALU.mult)
                nxt = [pj for pj in act_pieces if pj > pi]
                if nxt:
                    odd_norm(nxt[0])
                for t_ in range(4):
                    for (o, nb) in nblocks(n):
                        pt = ps_next()
                        for k in range(3):
                            S.mm(pt[:, 0:nb], DG[:, t_, k, :], CXb[:, t_, hl_ + o + k:hl_ + o + k + nb],
                                 start=(k == 0), stop=(k == 2))
                        S.tt("dve", ZS[:, 4 + t_, o:o + nb], pt[:, 0:nb], ZS[:, 4 + t_, o:o + nb], ALU.mult)
                out_proj(owout_d[i], ZS, n, kind, s0, w_, l, wkey=("owout", i))

        try:
            chk("prologue")
            for l in range(nlayers):
                if l % 2 == 0:
                    even_layer(l)
                else:
                    odd_layer(l)
                if l % 2 == 0:
                    ada_steps(l + 1, 24)
                    ada_steps(l + 2, 24)
        except _Stop:
            print("stopped at op", len(S.ops))
            S.limit = None

        outv = outT_d.rearrange("(k p) t -> p k t", p=128)
        if final_norm:
            SQ = BA[:, 0:4096].rearrange("p (k c) -> p k c", k=KC)
            fg = VEC[:, V_FINALG:V_FINALG + 8]
            for bi, (o, nb) in enumerate(nblocks(T)):
                for k in range(KC):
                    S.act(SQ[:, k, 0:nb], XL[:, k, o:o + nb], AF.Square)
                pt = ps_next(4, 6)
                for k in range(KC):
                    S.mm(pt[:, 0:nb], ONES, SQ[:, k, 0:nb], start=(k == 0), stop=(k == KC - 1))
                rs = FT[:, 0:nb]
                S.act(rs, pt[:, 0:nb], AF.Sqrt, bias=EPSC, scale=1.0 / D)
                S.recip(rs, rs)
                ob = FA[:, 0:4096].rearrange("p (k c) -> p k c", k=KC)
                for k in range(KC):
                    S.stt("dve", ob[:, k, 0:nb], XL[:, k, o:o + nb], fg[:, k:k + 1], rs,
                          ALU.mult, ALU.mult)
                S.dma(outv[:, :, o:o + nb], ob[:, :, 0:nb], eng=STQ)
        else:
            S.dma(outv, XL)

        S.emit(sems, dsems)
        build_program.stats = (len(S.ops), S.n_waits, S.n_signals)
    return nc


_PROGRAM_CACHE = {}


def kernel(**inputs):
    inp = {k: np.asarray(v) for k, v in inputs.items()}
    nb = inp["x"].shape[0]
    shared = _prep_shared_inputs(inp)
    in_maps = []
    for b in range(nb):
        m = dict(shared)
        m.update(_prep_core_inputs(inp, b))
        in_maps.append(m)
    if "nc" not in _PROGRAM_CACHE:
        _PROGRAM_CACHE["nc"] = build_program()
    nc = _PROGRAM_CACHE["nc"]
    res = run_bass_kernel_spmd(nc, in_maps, core_ids=list(range(nb)))
    out = np.stack([np.ascontiguousarray(res.results[b]["outT"].T) for b in range(nb)], axis=0)
    return out.astype(np.float32)
```

```python
import math
from contextlib import ExitStack

import numpy as np
import ml_dtypes
import concourse.bass as bass
import concourse.mybir as mybir
from concourse.bass_utils import run_bass_kernel_spmd

F32 = mybir.dt.float32
BF16 = mybir.dt.bfloat16
I32 = mybir.dt.int32
AF = mybir.ActivationFunctionType
ALU = mybir.AluOpType

D = 1024
T = 2048
TC = 256
TT = T + TC
KC = 8
DEPTH = 4
NEED_CTX = [True, True, False, False]
EPS = 1e-6
POOL_W = (2, 4, 8, 16)
NCH = 72
NDMA_SLOTS = 18


def _rect(ap):
    t = ap.tensor
    name = t.name
    dims = ap.ap
    off = int(ap.offset)
    if "DRam" in type(t).__name__:
        lo = off
        hi = off
        for s, c in dims:
            if s >= 0:
                hi += (c - 1) * s
            else:
                lo += (c - 1) * s
        cnt = 1
        for _s, c in dims:
            cnt *= c
        return (name, 0, 1, lo, hi + 1, cnt == hi + 1 - lo)
    rowsize = 1
    for d in list(t.shape)[1:]:
        rowsize *= d
    if "PSum" in type(t).__name__:
        return (name, 0, 128, 0, rowsize, True, True)
    p0 = off // rowsize
    f0 = off % rowsize
    s0, c0 = dims[0]
    if s0 == rowsize or c0 == 1:
        p1 = p0 + c0
        rest = dims[1:]
    else:
        p1 = p0 + 1
        rest = dims
    lo = f0
    hi = f0
    for s, c in rest:
        if s >= 0:
            hi += (c - 1) * s
        else:
            lo += (c - 1) * s
    cnt = 1
    for _s, c in rest:
        cnt *= c
    return (name, p0, p1, lo, hi + 1, cnt == hi + 1 - lo)


def _overlap(a, b):
    return a[1] < b[2] and b[1] < a[2] and a[3] < b[4] and b[3] < a[4]


def _contains(a, b):
    return a[1] <= b[1] and a[2] >= b[2] and a[3] <= b[3] and a[4] >= b[4]


class _Stop(Exception):
    pass


class Sched:
    ENG = ("pe", "act", "dve", "pool", "sp")

    def __init__(self, nc):
        self.nc = nc
        self.ops = []
        self.recs = {}

    limit = None

    def add(self, eng, fn, reads, writes, dma=False):
        oid = len(self.ops)
        if self.limit is not None and oid >= self.limit:
            raise _Stop()
        deps = {}
        rr = [_rect(a) for a in reads]
        wr = [_rect(a) for a in writes]
        psr = [r for r in rr if len(r) > 6]
        if psr:
            rr = [r for r in rr if len(r) <= 6]
            wr = wr + psr
        for r in rr:
            for (q, o, w) in self.recs.get(r[0], ()):
                if w and _overlap(r, q):
                    deps[o] = True
        for r in wr:
            for (q, o, w) in self.recs.get(r[0], ()):
                if _overlap(r, q):
                    deps.setdefault(o, False)
        ops = self.ops

        def same_eng(o):
            if o == oid:
                return True
            od = ops[o]
            return (not dma) and (not od["dma"]) and od["eng"] == eng

        for r in wr:
            lst = self.recs.setdefault(r[0], [])
            lst[:] = [x for x in lst if not (_contains(r, x[0]) and (r[5] or same_eng(x[1])))]
            lst.append((r, oid, True))
        for r in rr:
            lst = self.recs.setdefault(r[0], [])
            lst[:] = [x for x in lst if not ((not x[2]) and _contains(r, x[0]) and same_eng(x[1]))]
            lst.append((r, oid, False))
        deps.pop(oid, None)
        keep = []
        for o, raw in deps.items():
            od = self.ops[o]
            if (not dma) and (not od["dma"]) and od["eng"] == eng:
                if eng == "pe":
                    continue
            keep.append(o)
        self.ops.append(dict(eng=eng, fn=fn, dma=dma, deps=sorted(keep)))
        return oid

    def dma(self, out, in_, eng="sp"):
        return self.add(eng, lambda e: e.dma_start(out=out, in_=in_), [in_], [out], dma=True)

    def mm(self, out, lhsT, rhs, start=True, stop=True):
        reads = [lhsT, rhs]
        return self.add("pe", lambda e: e.matmul(out, lhsT, rhs, start=start, stop=stop), reads, [out])

    def tr(self, out, in_, ident):
        return self.add("pe", lambda e: e.transpose(out, in_, ident), [in_, ident], [out])

    def act(self, out, in_, func, bias=None, scale=None):
        reads = [in_]
        kw = {}
        if bias is not None:
            kw["bias"] = bias
            reads.append(bias)
        if scale is not None:
            kw["scale"] = scale
            if not isinstance(scale, (int, float)):
                reads.append(scale)
        return self.add("act", lambda e: e.activation(out, in_, func, **kw), reads, [out])

    def tt(self, eng, out, in0, in1, op):
        return self.add(eng, lambda e: e.tensor_tensor(out, in0, in1, op), [in0, in1], [out])

    def ts(self, eng, out, in0, s1, s2, op0, op1=None):
        reads = [in0] + [s for s in (s1, s2) if s is not None and not isinstance(s, (int, float))]
        if op1 is None:
            return self.add(eng, lambda e: e.tensor_scalar(out, in0, s1, None, op0), reads, [out])
        return self.add(eng, lambda e: e.tensor_scalar(out, in0, s1, s2, op0, op1), reads, [out])

    def stt(self, eng, out, in0, scalar, in1, op0, op1):
        reads = [in0, in1] + ([] if isinstance(scalar, (int, float)) else [scalar])
        return self.add(eng, lambda e: e.scalar_tensor_tensor(out, in0, scalar, in1, op0, op1), reads, [out])

    def copy(self, eng, out, in_):
        if eng == "act":
            return self.add("act", lambda e: e.activation(out, in_, AF.Copy), [in_], [out])
        return self.add(eng, lambda e: e.tensor_copy(out, in_), [in_], [out])

    def memset(self, eng, out, val):
        return self.add(eng, lambda e: e.memset(out, val), [], [out])

    def recip(self, out, in_):
        return self.add("dve", lambda e: e.reciprocal(out, in_), [in_], [out])

    def emit(self, sems, dsems):
        nc = self.nc
        ops = self.ops
        n = len(ops)
        per_eng = {e: [] for e in self.ENG}
        for i, o in enumerate(ops):
            per_eng[o["eng"]].append(i)
        slot_uses = [0] * len(dsems)
        dctr = 0
        sctr = 0
        NSW = 4
        for i, o in enumerate(ops):
            if o["dma"]:
                if o["eng"] == "pool":
                    s = len(dsems) - NSW + sctr % NSW
                    sctr += 1
                else:
                    s = dctr % (len(dsems) - NSW)
                    dctr += 1
                slot_uses[s] += 1
                o["tl"] = ("d", s)
                o["pos"] = slot_uses[s]
                o["val"] = 16 * slot_uses[s]
            else:
                o["tl"] = o["eng"]
        for e in self.ENG:
            k = 0
            for i in per_eng[e]:
                if not ops[i]["dma"]:
                    k += 1
                    ops[i]["pos"] = k
        know = {e: {} for e in self.ENG}
        snap = [None] * n
        waits = [None] * n
        signal = [False] * n
        last_slot_op = {}
        for i, o in enumerate(ops):
            e = o["eng"]
            k = know[e]
            w = []
            deps = list(o["deps"])
            if o["dma"]:
                s = o["tl"]
                if s in last_slot_op:
                    deps.append(last_slot_op[s])
                last_slot_op[s] = i
            for d in sorted(deps):
                od = ops[d]
                tl = od["tl"]
                if k.get(tl, 0) >= od["pos"]:
                    continue
                w.append(d)
                signal[d] = True
                for t2, p2 in snap[d].items():
                    if k.get(t2, 0) < p2:
                        k[t2] = p2
                if k.get(tl, 0) < od["pos"]:
                    k[tl] = od["pos"]
            waits[i] = w
            snap[i] = dict(k)
        for e in self.ENG:
            c = 0
            for i in per_eng[e]:
                o = ops[i]
                if o["dma"]:
                    continue
                if signal[i]:
                    c += 1
                    o["val"] = c
                else:
                    o["val"] = None
        self.n_waits = sum(len(w) for w in waits)
        self.n_signals = sum(1 for s_ in signal if s_)

        def semof(o):
            if o["dma"]:
                return dsems[o["tl"][1]]
            return sems[o["eng"]]

        def run_engine(e):
            def body(eng):
                for i in per_eng[e]:
                    o = ops[i]
                    best = {}
                    for d in waits[i]:
                        od = ops[d]
                        key = od["tl"]
                        if key not in best or best[key][1] < od["val"]:
                            best[key] = (semof(od), od["val"])
                    for key, (sm, v) in best.items():
                        eng.wait_ge(sm, v)
                    ins = o["fn"](eng)
                    if o["dma"]:
                        ins.then_inc(semof(o), 16)
                    elif signal[i]:
                        ins.then_inc(semof(o), 1)
                if e == "sp":
                    for s, sm in enumerate(dsems):
                        if slot_uses[s]:
                            eng.wait_ge(sm, 16 * slot_uses[s])
            return body

        with nc.Block() as block:
            block.sync(run_engine("sp"))
            block.tensor(run_engine("pe"))
            block.scalar(run_engine("act"))
            block.vector(run_engine("dve"))
            block.gpsimd(run_engine("pool"))


V_NORMG = 0
V_FINALG = 32
V_ADAB = 40
V_GLUB = 136
V_PSCALE = 148
V_CONVW = 156
V_CVEC = 180
V_DTAB = 196
V_CONSTS = 292
NVEC = 300
C_ID = 0
C_MF = 128
C_MB = 256
C_BDC = 384
C_BDS = 512
C_FIX = 640
C_PROW = 704
C_PCOL = 832
NCST = 1088

_CONST_CACHE = {}


def _host_consts():
    if _CONST_CACHE:
        return _CONST_CACHE
    f32 = np.float32
    rows = T // 64
    r = np.arange(rows, dtype=f32)
    col = np.arange(64, dtype=f32)
    rr, cc = np.meshgrid(r, col, indexing="ij")
    rr = rr.reshape(-1, 1)
    cc = cc.reshape(-1, 1)
    quarter = D // 4
    omega = np.power(f32(10000.0), -np.arange(quarter, dtype=f32) / f32(quarter)).astype(f32)
    pos = np.concatenate([np.sin(rr * omega), np.cos(rr * omega), np.sin(cc * omega), np.cos(cc * omega)],
                         axis=-1).astype(f32)
    posT = np.ascontiguousarray(pos.T)

    cst = np.zeros((128, NCST), f32)
    cst[:, C_ID:C_ID + 128] = np.eye(128, dtype=f32)
    jj = np.arange(128) // 16
    cst[:, C_MF:C_MF + 128] = (jj[:, None] <= jj[None, :]).astype(f32)
    cst[:, C_MB:C_MB + 128] = (jj[:, None] >= jj[None, :]).astype(f32)
    m = np.arange(64)
    ang = 2.0 * np.pi * ((m[:, None] * m[None, :]) % 64) / 64.0
    c64 = np.cos(ang)
    s64 = np.sin(ang)
    bdc = np.zeros((128, 128))
    bds = np.zeros((128, 128))
    for g in range(2):
        bdc[g * 64:(g + 1) * 64, g * 64:(g + 1) * 64] = c64
        bds[g * 64:(g + 1) * 64, g * 64:(g + 1) * 64] = s64
    cst[:, C_BDC:C_BDC + 128] = bdc
    cst[:, C_BDS:C_BDS + 128] = bds
    fix = np.zeros((4, 16))
    for ti, w in enumerate(POOL_W):
        for p in range(8):
            cnt = min(p + w // 2, 10 ** 6) - max(p - w // 2, 0)
            fix[ti, p] = w / cnt
            posr = 1000 - 8 + p
            cnt = min(posr + w // 2, 1000) - (posr - w // 2)
            fix[ti, 8 + p] = w / cnt
    cst[:, C_FIX:C_FIX + 64] = fix.reshape(1, 64)
    prow = pos[0::64, 0:512]
    pcol = pos[0:64, 512:1024]
    cst[:, C_PROW:C_PROW + 128] = prow.T.reshape(4, 128, 32).transpose(1, 0, 2).reshape(128, 128)
    cst[:, C_PCOL:C_PCOL + 256] = pcol.T.reshape(4, 128, 64).transpose(1, 0, 2).reshape(128, 256)

    def dft(n):
        t = np.arange(n)
        a = 2.0 * np.pi * ((t[:, None] * t[None, :]) % n) / n
        cs = np.cos(a)
        ns = -np.sin(a)
        kc = n // 128
        cs = cs.reshape(kc, 128, n).transpose(1, 0, 2)
        ns = ns.reshape(kc, 128, n).transpose(1, 0, 2)
        return (np.ascontiguousarray(cs).astype(ml_dtypes.bfloat16),
                np.ascontiguousarray(ns).astype(ml_dtypes.bfloat16))

    cosL, nsinL = dft(T)
    cosC, nsinC = dft(TC)
    _CONST_CACHE.update(posT=posT, cst=cst, cosL=cosL, nsinL=nsinL, cosC=cosC, nsinC=nsinC)
    return _CONST_CACHE


def _pk(v):
    v = np.asarray(v, np.float32)
    return np.ascontiguousarray(v.reshape(-1, 128).T)


def _prep_core_inputs(inp, b):
    f32 = np.float32
    vec = np.zeros((128, NVEC), f32)
    for l in range(DEPTH):
        vec[:, V_NORMG + l * 8:V_NORMG + (l + 1) * 8] = _pk(inp["norm_g"][l])
        vec[:, V_ADAB + l * 24:V_ADAB + (l + 1) * 24] = _pk(inp["ada_b"][l])
    vec[:, V_FINALG:V_FINALG + 8] = _pk(inp["final_g"])
    for i in range(2):
        vec[:, V_GLUB + i * 6:V_GLUB + (i + 1) * 6] = _pk(inp["s5_glu_b"][i])
        vec[:, V_PSCALE + i * 4:V_PSCALE + (i + 1) * 4] = _pk(inp["pool_scale"][i])
        for k in range(3):
            vec[:, V_CONVW + (i * 3 + k) * 4:V_CONVW + (i * 3 + k + 1) * 4] = _pk(inp["conv_w"][i, k])
        dt = np.asarray(inp["s5_d"][i], f32).T
        vec[:, V_DTAB + i * 48:V_DTAB + (i + 1) * 48] = np.tile(dt, (8, 1))
    cv = np.stack([_pk(inp["c"][b]), _pk(inp["c_ctx"])], axis=-1)
    vec[:, V_CVEC:V_CVEC + 16] = cv.reshape(128, 16)
    vec[:, V_CONSTS + 0] = EPS
    vec[:, V_CONSTS + 1] = 0.0
    return dict(
        xT=np.ascontiguousarray(np.asarray(inp["x"][b], f32).T),
        ctxT=np.ascontiguousarray(np.asarray(inp["ctx"][b], f32).T),
        vec=vec,
    )


def _prep_shared_inputs(inp):
    f32 = np.float32
    cs = _host_consts()
    sh = dict(cst=cs["cst"], cosL=cs["cosL"], nsinL=cs["nsinL"],
              cosC=cs["cosC"], nsinC=cs["nsinC"])
    for k in ("ada_w", "even_w_in", "even_w_out", "odd_w_in", "odd_w_out"):
        sh[k] = np.ascontiguousarray(np.asarray(inp[k], f32))
    sh["glu_w"] = np.ascontiguousarray(np.asarray(inp["s5_glu_w"], f32))
    sh["pool_w"] = np.ascontiguousarray(np.asarray(inp["pool_w"], f32).transpose(0, 2, 1, 3))
    fw = np.asarray(inp["fnet_w"], f32).reshape(2, 2, 2, 64, 64)
    sh["fnet_w"] = np.ascontiguousarray(fw.transpose(0, 2, 3, 1, 4).reshape(2, 128, 2, 64))
    def qmaj(a):
        return np.asarray(a, f32).reshape(96, 64).T
    s5s = np.zeros((2, 128, 3, 96), f32)
    s5bc = np.zeros((2, 128, 4, 96, 16), f32)
    for i in range(2):
        s5s[i, :, 0, :] = np.tile(qmaj(inp["s5_lam_re"][i]), (2, 1))
        s5s[i, :, 1, :] = np.tile(qmaj(inp["s5_lam_im"][i]), (2, 1))
        s5s[i, :, 2, :] = np.tile(np.asarray(inp["s5_log_step"][i], f32).reshape(1, 96), (128, 1))
        br = np.asarray(inp["s5_b_re"][i], f32).reshape(96, 64, 16).transpose(1, 0, 2)
        bi = np.asarray(inp["s5_b_im"][i], f32).reshape(96, 64, 16).transpose(1, 0, 2)
        cr = np.asarray(inp["s5_c_re"][i], f32).reshape(96, 16, 64).transpose(2, 0, 1)
        ci = np.asarray(inp["s5_c_im"][i], f32).reshape(96, 16, 64).transpose(2, 0, 1)
        for w, a in enumerate((br, bi, cr, ci)):
            s5bc[i, :, w] = np.tile(a, (2, 1, 1))
    sh["s5s"] = s5s
    sh["s5bc"] = s5bc
    return sh


def build_program(nlayers=DEPTH, final_norm=True, debug=False, stop=None):
    nc = bass.Bass("TRN2", target_bir_lowering=False)

    def din(name, shape, dt=F32):
        return nc.dram_tensor(name, list(shape), dt, kind="ExternalInput").ap()

    xT_d = din("xT", [D, T])
    ctxT_d = din("ctxT", [D, TC])
    vec_d = din("vec", [128, NVEC])
    cst_d = din("cst", [128, NCST])
    cosL_d = din("cosL", [128, 16, T], BF16)
    nsinL_d = din("nsinL", [128, 16, T], BF16)
    cosC_d = din("cosC", [128, 2, TC], BF16)
    nsinC_d = din("nsinC", [128, 2, TC], BF16)
    ada_w_d = din("ada_w", [DEPTH, D, 3 * D])
    ewin_d = din("even_w_in", [2, D, 2048])
    ewout_d = din("even_w_out", [2, D, D])
    owin_d = din("odd_w_in", [2, D, 3072])
    owout_d = din("odd_w_out", [2, D, D])
    glu_d = din("glu_w", [2, 768, 768])
    poolw_d = din("pool_w", [2, 128, 4, 128])
    fnetw_d = din("fnet_w", [2, 128, 2, 64])
    s5s_d = din("s5s", [2, 128, 3, 96])
    s5bc_d = din("s5bc", [2, 128, 4, 96, 16])
    outT_d = nc.dram_tensor("outT", [D, T], F32, kind="ExternalOutput").ap()
    skind = "ExternalOutput" if debug else "Internal"
    HLs = nc.dram_tensor("HLs", [128, KC, TT], BF16, kind=skind).ap()
    UGLs = nc.dram_tensor("UGLs", [128, 48, 288], BF16, kind=skind).ap()
    FNs = nc.dram_tensor("FNs", [128, 2, TT], BF16, kind=skind).ap()
    YGs = nc.dram_tensor("YGs", [128, 6, TT], BF16, kind=skind).ap()
    XLs = nc.dram_tensor("XLs", [128, KC, T], F32, kind="Internal").ap()

    with ExitStack() as st:
        E = st.enter_context
        XLA = E(nc.sbuf_tensor("XLA", [128, KC * T], F32))
        XC = E(nc.sbuf_tensor("XC", [128, KC, TC], F32))
        FA = E(nc.sbuf_tensor("FA", [128, 6144], F32))
        BA = E(nc.sbuf_tensor("BA", [128, 36864], BF16))
        VEC = E(nc.sbuf_tensor("VEC", [128, NVEC], F32))
        CST = E(nc.sbuf_tensor("CST", [128, NCST], F32))
        CB = E(nc.sbuf_tensor("CB", [128, 5, 128], BF16))
        ADA = E(nc.sbuf_tensor("ADA", [128, DEPTH, 24, 2], F32))
        AM = E(nc.sbuf_tensor("AM", [128, 2, 8], F32))
        SC = E(nc.sbuf_tensor("SC", [128, 8, 2], F32))
        IT = E(nc.sbuf_tensor("IT", [128, 96], I32))
        ADST = E(nc.sbuf_tensor("ADST", [128, 2, 1024], F32))
        WAB = E(nc.sbuf_tensor("WAB", [128, 2, 256], BF16))
        PWS = E(nc.sbuf_tensor("PWS", [128, 2, 4, 128], BF16))
        DG = E(nc.sbuf_tensor("DG", [128, 4, 3, 128], BF16))
        PS = [E(nc.psum_tensor("PS%d" % i, [128, 512], F32)) for i in range(6)]
        PSB = [E(nc.psum_tensor("PSB%d" % i, [128, 1024], BF16)) for i in range(2)]
        PSB_ADA = PS[3]
        sems = {e: E(nc.semaphore("s_" + e)) for e in Sched.ENG}
        dsems = [E(nc.semaphore("d%d" % i)) for i in range(NDMA_SLOTS)]

        S = Sched(nc)
        if isinstance(stop, int):
            S.limit = stop
        XL = XLA[:, :].rearrange("p (k t) -> p k t", k=KC)
        IDB = CB[:, 0, :]
        MSKF = CB[:, 1, :]
        MSKB = CB[:, 2, :]
        ONES = CB[:, 3, :]
        IDF = CST[:, C_ID:C_ID + 128]
        EPSC = VEC[:, V_CONSTS:V_CONSTS + 1]
        ZEROC = VEC[:, V_CONSTS + 1:V_CONSTS + 2]

        rr = {"ps": 0, "psb": 0, "w": 0, "cast": 0, "ev": 0}
        STQ = "act"
        LDQ = "pool"

        def chk(tag):
            if stop == tag:
                raise _Stop()

        def ps_next(lo=0, hi=None):
            if hi is None:
                hi = rr.get("hi", 4)
            i = lo + rr["ps"] % (hi - lo)
            rr["ps"] += 1
            return PS[i]

        def psb_next():
            i = rr["psb"] % 2
            rr["psb"] += 1
            return PSB[i]

        def ev_eng():
            return "dve"

        WST = [FA[:, i * 2048:(i + 1) * 2048] for i in range(2)]
        WBF = [BA[:, 34816 + i * 1024:34816 + (i + 1) * 1024] for i in range(2)]

        WBF = [BA[:, 36864 - 4096 + i * 2048:36864 - 4096 + (i + 1) * 2048] for i in range(2)]
        BA_FREE = 36864 - 4096

        class _W2:
            def __init__(self, halves):
                self.h = halves

            def __getitem__(self, key):
                p, k, c = key
                hidx = c.start // 128
                return self.h[hidx][p, k, c.start - 128 * hidx:c.stop - 128 * hidx]

        wsc = {}
        wseen = set()

        def _cache(key, kch, total_cols):
            if key not in wsc:
                wsc[key] = nc.dram_tensor("WSC_%s_%d" % key, [total_cols // 128, 128, kch * 128], BF16).ap()
            return wsc[key]

        def _load_half(wd, c0, ncols, kch, cast, hs, key=None):
            wb = WBF[hs // 2][:, (hs % 2) * 1024:(hs % 2) * 1024 + kch * ncols].rearrange("p (k c) -> p k c", k=kch)
            hidx = c0 // 128
            if cast and key is not None and (key, hidx) in wseen:
                cs = _cache(key, kch, wd.shape[1])
                S.dma(wb, cs[hidx].rearrange("p (k c) -> p k c", k=kch))
                return wb
            stg = WST[hs // 2][:, (hs % 2) * 1024:(hs % 2) * 1024 + kch * ncols].rearrange("p (k c) -> p k c", k=kch)
            S.dma(stg, wd[:, c0:c0 + ncols].rearrange("(k p) c -> p k c", p=128))
            if not cast:
                return stg
            rr["cast"] += 1
            ce = ("act", "act", "dve")[rr["cast"] % 3]
            S.copy(ce, wb, stg)
            if key is not None:
                cs = _cache(key, kch, wd.shape[1])
                S.dma(cs[hidx].rearrange("p (k c) -> p k c", k=kch), wb, eng=STQ)
                wseen.add((key, hidx))
            return wb

        def load_w(wd, c0, ncols=256, kch=KC, cast=True, split=True, key=None):
            if split and ncols == 256:
                hv = []
                for hh in range(2):
                    hs = rr["w"] % 4
                    rr["w"] += 1
                    hv.append(_load_half(wd, c0 + hh * 128, 128, kch, cast, hs, key))
                return _W2(hv)
            if rr["w"] % 2:
                rr["w"] += 1
            slot = (rr["w"] % 4) // 2
            rr["w"] += 2
            wb = WBF[slot][:, 0:kch * ncols].rearrange("p (k c) -> p k c", k=kch)
            h0 = c0 // 128
            if cast and key is not None and all((key, h0 + hh) in wseen for hh in range(ncols // 128)):
                cs = _cache(key, kch, wd.shape[1])
                for hh in range(ncols // 128):
                    S.dma(wb[:, :, hh * 128:(hh + 1) * 128], cs[h0 + hh].rearrange("p (k c) -> p k c", k=kch))
                return wb
            stg = WST[slot][:, 0:kch * ncols].rearrange("p (k c) -> p k c", k=kch)
            src = wd[:, c0:c0 + ncols].rearrange("(k p) c -> p k c", p=128)
            S.dma(stg, src)
            if not cast:
                return stg
            rr["cast"] += 1
            ce = ("act", "act", "dve")[rr["cast"] % 3]
            S.copy(ce, wb, stg)
            if key is not None:
                cs = _cache(key, kch, wd.shape[1])
                for hh in range(ncols // 128):
                    S.dma(cs[h0 + hh].rearrange("p (k c) -> p k c", k=kch), wb[:, :, hh * 128:(hh + 1) * 128], eng=STQ)
                    wseen.add((key, h0 + hh))
            return wb

        def nblocks(n, bs=512):
            out = []
            o = 0
            while o < n:
                out.append((o, min(bs, n - o)))
                o += bs
            return out

        FT = FA[:, 4096:6144]

        def make_hl(xsrc, hl_dst, n, am, sh, blocks, sqbuf):
            for (o, nb) in blocks:
                for k in range(KC):
                    S.act(sqbuf[:, k, 0:nb], xsrc(k, o, nb), AF.Square)
                pt = ps_next(4, 6)
                for k in range(KC):
                    S.mm(pt[:, 0:nb], ONES, sqbuf[:, k, 0:nb], start=(k == 0), stop=(k == KC - 1))
                rs = FT[:, 0:nb]
                S.act(rs, pt[:, 0:nb], AF.Sqrt, bias=EPSC, scale=1.0 / D)
                S.recip(rs, rs)
                for k in range(KC):
                    tmp = FT[:, 512 + (k % 2) * 512:512 + (k % 2) * 512 + nb]
                    S.tt("dve", tmp, xsrc(k, o, nb), rs, ALU.mult)
                    S.act(hl_dst(k, o, nb), tmp, AF.Identity, bias=sh[:, k:k + 1], scale=am[:, k:k + 1])

        RSL = ADST[:, :, :].rearrange("p a c -> p (a c)")
        RSC = FT[:, 1536:1792]

        def rs_of(kind, a, nb):
            return RSL[:, a:a + nb] if kind == "lat" else RSC[:, a:a + nb]

        def layer_stats(with_ctx, part=None):
            SQs = BA[:, 24576:24576 + 4096].rearrange("p (k c) -> p k c", k=KC)
            todo = [("lat", o, nb) for (o, nb) in nblocks(T)]
            if with_ctx:
                todo.append(("ctx", 0, TC))
            if part == 0:
                todo = todo[:2]
            elif part == 1:
                todo = todo[2:]
            for (kind, o, nb) in todo:
                xs = xview(kind)
                for k in range(KC):
                    S.act(SQs[:, k, 0:nb], xs(k, o, nb), AF.Square)
                pt = ps_next(4, 6)
                for k in range(KC):
                    S.mm(pt[:, 0:nb], ONES, SQs[:, k, 0:nb], start=(k == 0), stop=(k == KC - 1))
                rs = rs_of(kind, o, nb)
                S.act(rs, pt[:, 0:nb], AF.Sqrt, bias=EPSC, scale=1.0 / D)
                S.recip(rs, rs)

        def normalize(kind, hl_dst, am, sh, blocks):
            xs = xview(kind)
            for (o, nb) in blocks:
                for k in range(KC):
                    tmp = FT[:, 512 + (k % 2) * 512:512 + (k % 2) * 512 + nb]
                    S.tt("dve", tmp, xs(k, o, nb), rs_of(kind, o, nb), ALU.mult)
                    S.act(hl_dst(k, o, nb), tmp, AF.Identity, bias=sh[:, k:k + 1], scale=am[:, k:k + 1])

        ada_state = {}

        def ada_steps(l, n):
            if l >= nlayers:
                return
            st_ = ada_state.setdefault(l, 0)
            pt = PSB_ADA
            for jt in range(st_, min(24, st_ + n)):
                stg = ADST[:, jt % 2, :].rearrange("p (k c) -> p k c", k=KC)
                S.dma(stg, ada_w_d[l][:, jt * 128:(jt + 1) * 128].rearrange("(k p) c -> p k c", p=128))
                for k in range(KC):
                    S.mm(pt[:, l * 48 + jt * 2:l * 48 + jt * 2 + 2], stg[:, k, :], SC[:, k, :],
                         start=(k == 0), stop=(k == KC - 1))
            ada_state[l] = min(24, st_ + n)
            if ada_state[l] == 24 and st_ < 24:
                ab = VEC[:, V_ADAB + l * 24:V_ADAB + (l + 1) * 24]
                S.tt("dve", ADA[:, l, :, :], pt[:, l * 48:l * 48 + 48].rearrange("p (j w) -> p j w", w=2),
                     ab.unsqueeze(2).to_broadcast([128, 24, 2]), ALU.add)

        def ada_layer(l):
            pt = PS[5]
            for gidx in range(12):
                w = load_w(ada_w_d[l], gidx * 256, 256, KC, cast=False, split=False)
                for t2 in range(2):
                    jt = gidx * 2 + t2
                    for k in range(KC):
                        S.mm(pt[:, jt * 2:jt * 2 + 2], w[:, k, t2 * 128:(t2 + 1) * 128], SC[:, k, :],
                             start=(k == 0), stop=(k == KC - 1))
            ab = VEC[:, V_ADAB + l * 24:V_ADAB + (l + 1) * 24]
            S.tt("dve", ADA[:, l, :, :], pt[:, 0:48].rearrange("p (j w) -> p j w", w=2),
                 ab.unsqueeze(2).to_broadcast([128, 24, 2]), ALU.add)

        def layer_mod(l):
            ng = VEC[:, V_NORMG + l * 8:V_NORMG + (l + 1) * 8]
            for w in range(2):
                S.ts("dve", AM[:, w, :], ADA[:, l, 8:16, w], 1.0, None, ALU.add)
                S.tt("dve", AM[:, w, :], AM[:, w, :], ng, ALU.mult)

        pieces = [("lat", 0, 1024, 0), ("lat", 1024, 2048, 1024), ("ctx", 0, 256, 2048)]

        def xview(kind):
            if kind == "lat":
                return lambda k, a, n: XL[:, k, a:a + n]
            return lambda k, a, n: XC[:, k, a:a + n]

        def out_proj(wd, Yp, n, kind, s0, gate_w, l, wkey=None):
            for wg in range(4):
                w = load_w(wd, wg * 256, key=wkey)
                for t2 in range(2):
                    dt = wg * 2 + t2
                    for (o, nb) in nblocks(n):
                        pt = ps_next()
                        for k in range(KC):
                            S.mm(pt[:, 0:nb], w[:, k, t2 * 128:(t2 + 1) * 128], Yp[:, k, o:o + nb],
                                 start=(k == 0), stop=(k == KC - 1))
                        xv = xview(kind)(dt, s0 + o, nb)
                        S.stt("dve", xv, pt[:, 0:nb], ADA[:, l, 16 + dt, gate_w:gate_w + 1], xv, ALU.mult, ALU.add)

        S.dma(VEC[:, :], vec_d)
        S.dma(CST[:, :], cst_d)
        S.copy("dve", CB[:, 0, :], CST[:, C_ID:C_ID + 128])
        S.copy("dve", CB[:, 1, :], CST[:, C_MF:C_MF + 128])
        S.copy("dve", CB[:, 2, :], CST[:, C_MB:C_MB + 128])
        S.memset("dve", CB[:, 3, :], 1.0)
        S.act(SC[:, :, :], VEC[:, V_CVEC:V_CVEC + 16].rearrange("p (k w) -> p k w", w=2), AF.Silu)
        xv = xT_d.rearrange("(k p) t -> p k t", p=128)
        for half in range(2):
            hs_ = slice(half * 1024, (half + 1) * 1024)
            for k in range(KC):
                S.dma(XL[:, k, hs_], xv[:, k, hs_])
            for k in range(KC):
                xv3 = XL[:, k, hs_].rearrange("p (r c) -> p r c", c=64)
                if k < 4:
                    pr_ = CST[:, C_PROW + k * 32 + half * 16:C_PROW + k * 32 + half * 16 + 16]
                    pb = pr_.unsqueeze(2).to_broadcast([128, 16, 64])
                else:
                    pc_ = CST[:, C_PCOL + (k - 4) * 64:C_PCOL + (k - 4) * 64 + 64]
                    pb = pc_.unsqueeze(1).to_broadcast([128, 16, 64])
                S.tt("dve", xv3, xv3, pb, ALU.add)
            if half == 0:
                layer_stats(True, part=0)
                ada_layer(0)
        S.dma(XC[:, :, :], ctxT_d.rearrange("(k p) t -> p k t", p=128))

        def even_layer(l):
            i = l // 2
            need_ctx = NEED_CTX[l]
            layer_mod(l)
            fw = FT[:, 0:128].rearrange("p (t d) -> p t d", t=2)
            S.dma(fw, fnetw_d[i])
            S.memset("pool", WAB[:, :, :], 0.0)
            for t2 in range(2):
                for ab, cofs in ((0, C_BDC), (1, C_BDS)):
                    pt = ps_next(4, 6)
                    S.mm(pt[:, 0:64], CST[:, cofs:cofs + 128], fw[:, t2, :])
                    S.copy("act", WAB[0:64, t2, ab * 128:ab * 128 + 64], pt[0:64, 0:64])
                    S.copy("act", WAB[64:128, t2, ab * 128 + 64:ab * 128 + 128], pt[64:128, 0:64])
            ABtok = BA[:, 22528:22528 + 9216].rearrange("p (c f) -> p c f", f=512)
            assert 22528 + 9216 <= BA_FREE
            def e1_norm(pj):
                kind, s0, s1, goff = pieces[pj]
                n = s1 - s0
                w_ = 0 if kind == "lat" else 1
                HLp = BA[:, 0:KC * n].rearrange("p (k t) -> p k t", k=KC)
                normalize(kind, lambda k, a, nb, H=HLp, s0=s0: H[:, k, a - s0:a - s0 + nb],
                          AM[:, w_, :], ADA[:, l, 0:8, w_], [(s0 + o, nb) for (o, nb) in nblocks(n)])
                S.dma(HLs[:, :, goff:goff + n], HLp, eng=STQ)

            if l != 0:
                layer_stats(True, part=0)
            e1_norm(0)
            layer_stats(True, part=1)
            s5_main = None
            for pi, (kind, s0, s1, goff) in enumerate(pieces):
                last = (pi == len(pieces) - 1)
                if last:
                    S.dma(XLs, XL, eng=STQ)
                    s5_main = s5_phase(l, i)
                n = s1 - s0
                nk = n // 8
                w_ = 0 if kind == "lat" else 1
                HLp = BA[:, 0:KC * n].rearrange("p (k t) -> p k t", k=KC)
                PBLK = BA[:, 8192:8192 + 6144].rearrange("p (g j h) -> p g j h", g=48, j=8)
                UGp = BA[:, 14336:14336 + 48 * nk].rearrange("p (g c) -> p g c", g=48)
                UBp = BA[:, 20480:20480 + 2 * n].rearrange("p (t c) -> p t c", t=2)
                for wg in range(3):
                    w = load_w(ewin_d[i], wg * 256, split=False, key=("ewin", i))
                    for jp in range(4):
                        pt = ps_next()
                        for jj in range(2):
                            j = jp * 2 + jj
                            for k in range(KC):
                                S.mm(pt[0:nk, jj * 256:(jj + 1) * 256], HLp[:, k, j::8], w[:, k, :],
                                     start=(k == 0), stop=(k == KC - 1))
                        S.copy("act" if last else ("act", "dve")[jp % 2], PBLK[0:nk, wg * 16:(wg + 1) * 16, jp * 2:jp * 2 + 2, :],
                               pt[0:nk, 0:512].rearrange("p (j g h) -> p g j h", j=2, g=16))
                for g8 in range(6):
                    pb = psb_next()
                    for gi in range(8):
                        g = g8 * 8 + gi
                        S.tr(pb[:, gi * nk:(gi + 1) * nk], PBLK[0:nk, g, :, :].rearrange("p j h -> p (j h)"),
                             IDB[0:nk, 0:nk])
                    S.copy("act" if last else ("act", "dve")[g8 % 2], UGp[:, g8 * 8:(g8 + 1) * 8, :],
                           pb[:, 0:8 * nk].rearrange("p (g c) -> p g c", g=8))
                S.dma(UGLs[:, :, goff // 8:goff // 8 + nk], UGp, eng=STQ)
                if kind == "ctx" and not need_ctx:
                    continue
                w = load_w(ewin_d[i], 768, key=("ewin", i))
                for t2 in range(2):
                    for (o, nb) in nblocks(n):
                        pt = ps_next()
                        for k in range(KC):
                            S.mm(pt[:, 0:nb], w[:, k, t2 * 128:(t2 + 1) * 128], HLp[:, k, o:o + nb],
                                 start=(k == 0), stop=(k == KC - 1))
                        S.copy("act", UBp[:, t2, o:o + nb], pt[:, 0:nb])
                if pi + 1 < len(pieces):
                    e1_norm(pi + 1)
                for tc in range(n // 128):
                    pt = ps_next()
                    for t2 in range(2):
                        S.mm(pt[:, t2 * 256:(t2 + 1) * 256], UBp[:, t2, tc * 128:(tc + 1) * 128], WAB[:, t2, :])
                    S.copy("act" if last else ("dve", "act")[tc % 2], ABtok[:, goff // 128 + tc, :], pt[:, 0:512])
            chk("E1")
            FNb = BA[:, 0:2 * TT].rearrange("p (t c) -> p t c", t=2)
            DFTB = [BA[:, 4608 + s * 8192:4608 + (s + 1) * 8192] for s in range(2)]
            nrmL = 1.0 / math.sqrt(T * 64.0)
            nrmC = 1.0 / math.sqrt(TC * 64.0)
            for kb in range(8):
                db = DFTB[kb % 2]
                cb = db[:, 0:4096].rearrange("p (c k) -> p c k", c=16)
                sb = db[:, 4096:8192].rearrange("p (c k) -> p c k", c=16)
                S.dma(cb, cosL_d[:, :, kb * 256:(kb + 1) * 256])
                S.dma(sb, nsinL_d[:, :, kb * 256:(kb + 1) * 256])
                for t2 in range(2):
                    pt = ps_next()
                    for tc in range(16):
                        S.mm(pt[:, 0:256], ABtok[:, tc, t2 * 256:t2 * 256 + 128], cb[:, tc, :],
                             start=(tc == 0), stop=False)
                        S.mm(pt[:, 0:256], ABtok[:, tc, t2 * 256 + 128:t2 * 256 + 256], sb[:, tc, :],
                             start=False, stop=(tc == 15))
                    S.act(FNb[:, t2, kb * 256:(kb + 1) * 256], pt[:, 0:256], AF.Copy, scale=nrmL)
            if need_ctx:
                db = DFTB[0]
                cb = db[:, 0:512].rearrange("p (c k) -> p c k", c=2)
                sb = db[:, 512:1024].rearrange("p (c k) -> p c k", c=2)
                S.dma(cb, cosC_d)
                S.dma(sb, nsinC_d)
                for t2 in range(2):
                    pt = ps_next()
                    for tc in range(2):
                        S.mm(pt[:, 0:256], ABtok[:, 16 + tc, t2 * 256:t2 * 256 + 128], cb[:, tc, :],
                             start=(tc == 0), stop=False)
                        S.mm(pt[:, 0:256], ABtok[:, 16 + tc, t2 * 256 + 128:t2 * 256 + 256], sb[:, tc, :],
                             start=False, stop=(tc == 1))
                    S.act(FNb[:, t2, T:T + TC], pt[:, 0:256], AF.Copy, scale=nrmC)
                S.dma(FNs, FNb, eng=STQ)
            else:
                S.dma(FNs[:, :, 0:T], FNb[:, :, 0:T], eng=STQ)

            chk("E2")
            s5_main()
            rr["hi"] = 4
            chk("E3")

            for pi, (kind, s0, s1, goff) in enumerate(pieces):
                if kind == "ctx" and not need_ctx:
                    continue
                n = s1 - s0
                w_ = 0 if kind == "lat" else 1
                HLp = BA[:, 0:KC * n].rearrange("p (k t) -> p k t", k=KC)
                YGp = BA[:, 8192:8192 + 6 * n].rearrange("p (k t) -> p k t", k=6)
                FNp = BA[:, 14336:14336 + 2 * n].rearrange("p (k t) -> p k t", k=2)
                Yp = BA[:, 16384:16384 + KC * n].rearrange("p (k t) -> p k t", k=KC)
                TMP = [BA[:, 24576 + s * 512:24576 + (s + 1) * 512] for s in range(4)]
                if kind == "lat":
                    S.dma(XL[:, :, s0:s1], XLs[:, :, s0:s1], eng=LDQ)
                S.dma(YGp, YGs[:, :, goff:goff + n], eng=LDQ)
                S.dma(HLp, HLs[:, :, goff:goff + n], eng=LDQ)
                S.dma(FNp, FNs[:, :, goff:goff + n], eng=LDQ)
                tctr = 0
                for gw in range(3):
                    w = load_w(glu_d[i], gw * 256, 256, 6, key=("glu", i))
                    for t2 in range(2):
                        cp = gw * 2 + t2
                        for (o, nb) in nblocks(n):
                            pt = ps_next()
                            for k in range(6):
                                S.mm(pt[:, 0:nb], w[:, k, t2 * 128:(t2 + 1) * 128], YGp[:, k, o:o + nb],
                                     start=(k == 0), stop=(k == 5))
                            sg = TMP[tctr % 4][:, 0:nb]
                            tctr += 1
                            S.act(sg, pt[:, 0:nb], AF.Sigmoid, bias=VEC[:, V_GLUB + i * 6 + cp:V_GLUB + i * 6 + cp + 1])
                            S.tt(ev_eng(), Yp[:, cp, o:o + nb], YGp[:, cp, o:o + nb], sg, ALU.mult)
                for wg in range(4):
                    w = load_w(ewin_d[i], 1024 + wg * 256, key=("ewin", i))
                    for t2 in range(2):
                        zt = wg * 2 + t2
                        for (o, nb) in nblocks(n):
                            pt = ps_next()
                            for k in range(KC):
                                S.mm(pt[:, 0:nb], w[:, k, t2 * 128:(t2 + 1) * 128], HLp[:, k, o:o + nb],
                                     start=(k == 0), stop=(k == KC - 1))
                            zs = TMP[tctr % 4][:, 0:nb]
                            tctr += 1
                            S.act(zs, pt[:, 0:nb], AF.Silu)
                            src = Yp[:, zt, o:o + nb] if zt < 6 else FNp[:, zt - 6, o:o + nb]
                            S.tt(ev_eng(), Yp[:, zt, o:o + nb], src, zs, ALU.mult)
                out_proj(ewout_d[i], Yp, n, kind, s0, w_, l, wkey=("ewout", i))

        def s5_phase(l, i):
            Z = XLA[:, 0:6912].rearrange("p (q c) -> p q c", q=96)
            RAW = XLA[:, 0:6144].rearrange("p (w q h) -> p w q h", w=4, q=96)
            SS = XLA[:, 6144:6144 + 288].rearrange("p (w q) -> p w q", w=3)
            E1r = XLA[:, 6912:8448].rearrange("p (q k) -> p q k", q=96)
            E1i = XLA[:, 8448:9984].rearrange("p (q k) -> p q k", q=96)
            E8r = XLA[:, 9984:10464].rearrange("p (q k) -> p q k", q=96)
            E8i = XLA[:, 10464:10944].rearrange("p (q k) -> p q k", q=96)
            NAIs = XLA[:, 10944:11040]
            SM = [XLA[:, 11040 + s_ * 96:11040 + (s_ + 1) * 96] for s_ in range(16)]
            TQ = XLA[:, 12576:14112].rearrange("p (q h) -> p q h", q=96)
            tw = [XLA[:, 14112 + s_ * 384:14112 + (s_ + 1) * 384] for s_ in range(2)]
            X1b = FA[:, 0:1536].rearrange("p (q h) -> p q h", q=96)
            X2b = FA[:, 1536:3072].rearrange("p (q h) -> p q h", q=96)
            X1c = FA[:, 3072:4608].rearrange("p (q h) -> p q h", q=96)
            X2c = FA[:, 4608:6144].rearrange("p (q h) -> p q h", q=96)
            S.dma(RAW, s5bc_d[i], eng=LDQ)
            S.dma(SS, s5s_d[i], eng=LDQ)
            LR, LI, LS = SS[:, 0, :], SS[:, 1, :], SS[:, 2, :]
            (step, xr, xi, mag, sn, cs, t0, t1, t2, lbr, lbi, qr, qi, t3, t4, t5) = SM
            dv = "dve"
            S.act(step, LS, AF.Exp)
            S.tt(dv, xr, LR, step, ALU.mult)
            S.tt(dv, xi, LI, step, ALU.mult)
            S.act(mag, xr, AF.Exp)
            PI2 = 2.0 * math.pi

            def sin_of(dst, src, shift):
                S.ts(dv, t0, src, 1.0 / PI2, shift / PI2, ALU.mult, ALU.add)
                S.copy(dv, IT[:, :], t0)
                S.copy(dv, t1, IT[:, :])
                S.ts(dv, t2, src, shift, None, ALU.add)
                S.stt(dv, t2, t1, -PI2, t2, ALU.mult, ALU.add)
                S.ts(dv, t2, t2, -3.14159, 3.14159, ALU.max, ALU.min)
                S.act(dst, t2, AF.Sin)

            sin_of(sn, xi, 0.0)
            sin_of(cs, xi, math.pi / 2.0)
            S.tt(dv, lbr, mag, cs, ALU.mult)
            S.tt(dv, lbi, mag, sn, ALU.mult)
            S.ts(dv, t0, lbr, -1.0, None, ALU.add)
            S.tt(dv, t1, LR, LR, ALU.mult)
            S.tt(dv, t2, LI, LI, ALU.mult)
            S.tt(dv, t1, t1, t2, ALU.add)
            S.recip(t1, t1)
            S.tt(dv, t2, t0, LR, ALU.mult)
            S.tt(dv, t3, lbi, LI, ALU.mult)
            S.tt(dv, t2, t2, t3, ALU.add)
            S.tt(dv, qr, t2, t1, ALU.mult)
            S.tt(dv, t2, lbi, LR, ALU.mult)
            S.tt(dv, t3, t0, LI, ALU.mult)
            S.tt(dv, t2, t2, t3, ALU.subtract)
            S.tt(dv, qi, t2, t1, ALU.mult)
            BRr, BIr, CRr, CIr = RAW[:, 0], RAW[:, 1], RAW[:, 2], RAW[:, 3]
            lo, hi = slice(0, 64), slice(64, 128)

            def bq(a2, psl):
                return a2[psl].unsqueeze(2).to_broadcast([64, 96, 16])

            S.tt(dv, X1b[lo], BRr[lo], bq(qr, lo), ALU.mult)
            S.tt(dv, TQ[lo], BIr[lo], bq(qi, lo), ALU.mult)
            S.tt(dv, X1b[lo], X1b[lo], TQ[lo], ALU.subtract)
            S.tt(dv, X1b[hi], BIr[hi], bq(qr, hi), ALU.mult)
            S.tt(dv, TQ[hi], BRr[hi], bq(qi, hi), ALU.mult)
            S.tt(dv, X1b[hi], X1b[hi], TQ[hi], ALU.add)
            S.tt(dv, X2b[lo], BIr[lo], bq(qr, lo), ALU.mult)
            S.tt(dv, TQ[lo], BRr[lo], bq(qi, lo), ALU.mult)
            S.tt(dv, X2b[lo], X2b[lo], TQ[lo], ALU.add)
            S.ts(dv, X2b[lo], X2b[lo], -1.0, None, ALU.mult)
            S.tt(dv, X2b[hi], BRr[hi], bq(qr, hi), ALU.mult)
            S.tt(dv, TQ[hi], BIr[hi], bq(qi, hi), ALU.mult)
            S.tt(dv, X2b[hi], X2b[hi], TQ[hi], ALU.subtract)
            S.act(X1c[lo], CRr[lo], AF.Copy)
            S.act(X1c[hi], CIr[hi], AF.Copy, scale=-1.0)
            S.act(X2c[lo], CIr[lo], AF.Copy, scale=-1.0)
            S.act(X2c[hi], CRr[hi], AF.Copy, scale=-1.0)

            def cmul(eng, outr, outi, ar, ai, br, bi, ta, tb):
                S.tt(eng, ta, ar, br, ALU.mult)
                S.tt(eng, tb, ai, bi, ALU.mult)
                S.tt(eng, outr, ta, tb, ALU.subtract)
                S.tt(eng, ta, ar, bi, ALU.mult)
                S.tt(eng, tb, ai, br, ALU.mult)
                S.tt(eng, outi, ta, tb, ALU.add)

            def bc(ap2, n_):
                return ap2.unsqueeze(2).to_broadcast([128, 96, n_])

            def twv(s_, n_):
                return tw[s_][:, 0:96 * n_].rearrange("p (q k) -> p q k", q=96)

            S.memset(dv, E1r[:, :, 7:8], 1.0)
            S.memset(dv, E1i[:, :, 7:8], 0.0)
            S.copy(dv, E1r[:, :, 8], lbr)
            S.copy(dv, E1i[:, :, 8], lbi)
            cmul(dv, E1r[:, :, 9:10], E1i[:, :, 9:10], E1r[:, :, 8:9], E1i[:, :, 8:9], E1r[:, :, 8:9], E1i[:, :, 8:9],
                 twv(0, 1), twv(1, 1))
            cmul(dv, E1r[:, :, 10:12], E1i[:, :, 10:12], E1r[:, :, 8:10], E1i[:, :, 8:10],
                 bc(E1r[:, :, 9], 2), bc(E1i[:, :, 9], 2), twv(0, 2), twv(1, 2))
            cmul(dv, E1r[:, :, 12:16], E1i[:, :, 12:16], E1r[:, :, 8:12], E1i[:, :, 8:12],
                 bc(E1r[:, :, 11], 4), bc(E1i[:, :, 11], 4), twv(0, 4), twv(1, 4))
            S.tt(dv, t0, mag, mag, ALU.mult)
            S.recip(t0, t0)
            S.tt(dv, E1r[:, :, 6], lbr, t0, ALU.mult)
            S.tt(dv, t1, lbi, t0, ALU.mult)
            S.ts(dv, E1i[:, :, 6], t1, -1.0, None, ALU.mult)
            cmul(dv, E1r[:, :, 5:6], E1i[:, :, 5:6], E1r[:, :, 6:7], E1i[:, :, 6:7], E1r[:, :, 6:7], E1i[:, :, 6:7],
                 twv(0, 1), twv(1, 1))
            cmul(dv, E1r[:, :, 3:5], E1i[:, :, 3:5], E1r[:, :, 5:7], E1i[:, :, 5:7],
                 bc(E1r[:, :, 5], 2), bc(E1i[:, :, 5], 2), twv(0, 2), twv(1, 2))
            cmul(dv, E1r[:, :, 0:3], E1i[:, :, 0:3], E1r[:, :, 4:7], E1i[:, :, 4:7],
                 bc(E1r[:, :, 3], 3), bc(E1i[:, :, 3], 3), twv(0, 3), twv(1, 3))
            S.memset(dv, E8r[:, :, 0:1], 1.0)
            S.memset(dv, E8i[:, :, 0:1], 0.0)
            S.copy(dv, E8r[:, :, 1], E1r[:, :, 15])
            S.copy(dv, E8i[:, :, 1], E1i[:, :, 15])
            cmul(dv, E8r[:, :, 2:3], E8i[:, :, 2:3], E8r[:, :, 1:2], E8i[:, :, 1:2], E8r[:, :, 1:2], E8i[:, :, 1:2],
                 twv(0, 1), twv(1, 1))
            cmul(dv, E8r[:, :, 3:4], E8i[:, :, 3:4], E8r[:, :, 2:3], E8i[:, :, 2:3], E8r[:, :, 1:2], E8i[:, :, 1:2],
                 twv(0, 1), twv(1, 1))
            cmul(dv, E8r[:, :, 4:5], E8i[:, :, 4:5], E8r[:, :, 2:3], E8i[:, :, 2:3], E8r[:, :, 2:3], E8i[:, :, 2:3],
                 twv(0, 1), twv(1, 1))
            S.ts(dv, NAIs[0:64, :], E8i[0:64, :, 4], -1.0, None, ALU.mult)
            S.copy(dv, NAIs[64:128, :], E8i[64:128, :, 4])

            SPf = BA[:, 0:6912].rearrange("p (q c) -> p q c", q=96)
            UGb = [BA[:, 6912 + s_ * 1152:6912 + (s_ + 1) * 1152].rearrange("p (g c) -> p g c", g=4) for s_ in range(2)]
            o_ = 9216
            WZT = [BA[:, o_ + s_ * 2048:o_ + (s_ + 1) * 2048].rearrange("p (g f) -> p g f", g=4) for s_ in range(4)]
            WZ = [BA[:, o_ + 8192 + s_ * 512:o_ + 8192 + (s_ + 1) * 512].rearrange("p (a f) -> p a f", a=4)
                  for s_ in range(2)]
            TBA = [BA[:, o_ + 9216 + s_ * 2048:o_ + 9216 + (s_ + 1) * 2048] for s_ in range(4)]
            assert o_ + 9216 + 8192 <= 36864
            RB = [BA[:, o_ + s_ * 2048:o_ + (s_ + 1) * 2048].rearrange("p (g f) -> p g f", g=4) for s_ in range(4)]
            o_ += 8192
            QB = [BA[:, o_ + s_ * 512:o_ + (s_ + 1) * 512].rearrange("p (g f) -> p g f", g=4) for s_ in range(4)]
            o_ += 2048
            MB_ = [BA[:, o_ + s_ * 2048:o_ + (s_ + 1) * 2048].rearrange("p (g f) -> p g f", g=4) for s_ in range(2)]
            o_ += 4096
            PG = BA[:, o_:o_ + 4096].rearrange("p (t g h) -> p t g h", t=32, g=8)
            o_ += 4096
            YGT = [BA[:, o_:o_ + TT]]
            o_ += TT
            TBB = [BA[:, o_ + 1024 + s_ * 2048:o_ + 1024 + (s_ + 1) * 2048] for s_ in range(2)]
            assert o_ + 1024 + 4096 <= 36864, o_
            PTs = [XLA[:, 11040 + s_ * 128:11040 + (s_ + 1) * 128] for s_ in range(8)]
            BIG = [XLA[:, 12064 + s_ * 2048:12064 + (s_ + 1) * 2048] for s_ in range(2)]
            SCN = [XLA[:, 12064 + s_ * 48:12064 + (s_ + 1) * 48] for s_ in range(8)]
            QTMP = [XLA[:, s_ * 512:(s_ + 1) * 512] for s_ in range(2)]
            GB = 4

            def ptable(eng, slot, q0, e8sl, e1sl):
                def v4(t_):
                    return t_[:, 0:GB * 32].rearrange("p (g x y) -> p g x y", g=GB, x=4)
                pr = v4(PTs[slot * 2])
                pi_ = v4(PTs[slot * 2 + 1])
                a_r = E8r[:, q0:q0 + GB, e8sl].unsqueeze(3).to_broadcast([128, GB, 4, 8])
                a_i = E8i[:, q0:q0 + GB, e8sl].unsqueeze(3).to_broadcast([128, GB, 4, 8])
                b_r = E1r[:, q0:q0 + GB, e1sl].unsqueeze(2).to_broadcast([128, GB, 4, 8])
                b_i = E1i[:, q0:q0 + GB, e1sl].unsqueeze(2).to_broadcast([128, GB, 4, 8])
                cmul(eng, pr, pi_, a_r, a_i, b_r, b_i, v4(PTs[4 + slot * 2]), v4(PTs[5 + slot * 2]))
                return (PTs[slot * 2][:, 0:GB * 32].rearrange("p (g k) -> p g k", g=GB),
                        PTs[slot * 2 + 1][:, 0:GB * 32].rearrange("p (g k) -> p g k", g=GB))

            def bigtable(eng, out_bf, pr, pi_, X1, X2, q0, nk, ta_, tb_, eng2=None):
                n_ = GB * nk * 16
                ta = ta_[:, 0:n_].rearrange("p (g k h) -> p g k h", g=GB, k=nk)
                tb = tb_[:, 0:n_].rearrange("p (g k h) -> p g k h", g=GB, k=nk)
                prb = pr.unsqueeze(3).to_broadcast([128, GB, nk, 16])
                pib = pi_.unsqueeze(3).to_broadcast([128, GB, nk, 16])
                x1 = X1[:, q0:q0 + GB, :].unsqueeze(2).to_broadcast([128, GB, nk, 16])
                x2 = X2[:, q0:q0 + GB, :].unsqueeze(2).to_broadcast([128, GB, nk, 16])
                S.tt(eng, ta, prb, x1, ALU.mult)
                if eng2 is None or eng2 == eng:
                    S.tt(eng, tb, pib, x2, ALU.mult)
                else:
                    hg = GB // 2
                    S.tt(eng2, tb[:, 0:hg], pib[:, 0:hg], x2[:, 0:hg], ALU.mult)
                    S.tt(eng, tb[:, hg:GB], pib[:, hg:GB], x2[:, hg:GB], ALU.mult)
                S.tt(eng, out_bf[:, :, 0:nk * 16].rearrange("p g (k h) -> p g k h", k=nk), ta, tb, ALU.add)

            def tbl_mults(out_bf, pr, pi_, X1, X2, q0, tb_, eng_b):
                nk = 32
                o4 = out_bf[:, :, 0:512].rearrange("p g (k h) -> p g k h", k=nk)
                tb = tb_[:, 0:GB * 512].rearrange("p (g k h) -> p g k h", g=GB, k=nk)
                prb = pr.unsqueeze(3).to_broadcast([128, GB, nk, 16])
                pib = pi_.unsqueeze(3).to_broadcast([128, GB, nk, 16])
                x1 = X1[:, q0:q0 + GB, :].unsqueeze(2).to_broadcast([128, GB, nk, 16])
                x2 = X2[:, q0:q0 + GB, :].unsqueeze(2).to_broadcast([128, GB, nk, 16])
                S.tt("dve", o4, prb, x1, ALU.mult)
                S.tt(eng_b, tb, pib, x2, ALU.mult)

            def tbl_add(out_bf, tb_):
                o2 = out_bf[:, :, 0:512]
                S.tt("dve", o2, o2, tb_[:, 0:GB * 512].rearrange("p (g f) -> p g f", g=GB), ALU.add)

            DESC4 = slice(3, None, -1)

            def _main():
                rr["hi"] = 3
                chk("E3prep")
                for gb in range(12):
                    g0 = gb * GB
                    ug = UGb[gb % 2]
                    S.dma(ug, UGLs[:, g0:g0 + GB, :])
                    for dr in range(2):
                        q0 = dr * 48 + g0
                        if dr == 0:
                            pr, pi_ = ptable("dve", dr, q0, DESC4, slice(15, 7, -1))
                        else:
                            pr, pi_ = ptable("dve", dr, q0, slice(1, 5), slice(7, 15))
                        tbl_mults(WZT[(gb % 2) * 2 + dr], pr, pi_, X1b, X2b, q0, TBA[(gb % 2) * 2 + dr], "dve")
                    for dr in range(2):
                        q0 = dr * 48 + g0
                        wzt = WZT[(gb % 2) * 2 + dr]
                        tbl_add(wzt, TBA[(gb % 2) * 2 + dr])
                        pz = PS[4 + dr]
                        for gl in range(GB):
                            pb = psb_next()
                            for a in range(4):
                                S.tr(pb[:, a * 128:(a + 1) * 128], wzt[:, gl, a * 128:(a + 1) * 128], IDB)
                            wz = WZ[gl % 2]
                            S.copy("act", wz, pb[:, 0:512].rearrange("p (a f) -> p a f", a=4))
                            for a in range(4):
                                S.mm(pz[:, gl * 72:(gl + 1) * 72], wz[:, a, :], ug[:, gl, a::4],
                                     start=(a == 0), stop=(a == 3))
                        S.copy("act", Z[:, q0:q0 + GB, :], pz[:, 0:GB * 72].rearrange("p (g c) -> p g c", g=GB))
                    ada_steps(l + 2, 2)

                chk("E3A")
                orders = [list(range(64, 72)) + list(range(0, 64)),
                          list(range(71, 63, -1)) + list(range(63, -1, -1))]
                prevs = [None, None]
                for st_i in range(NCH):
                    for dr in range(2):
                        eng = ("dve", "pool")[dr]
                        qs = slice(dr * 48, (dr + 1) * 48)
                        AR = E8r[:, qs, 4]
                        AIs_ = NAIs[:, qs]
                        SBu = SCN[dr * 4 + 0]
                        tA = SCN[dr * 4 + 1]
                        tB = SCN[dr * 4 + 2]
                        c = orders[dr][st_i]
                        prev = prevs[dr]
                        if prev is not None:
                            P = Z[:, qs, prev]
                            ce_ = "act" if dr == 1 else eng
                            S.copy(ce_, SBu[0:64, :], Z[64:128, qs, prev])
                            S.copy(ce_, SBu[64:128, :], Z[0:64, qs, prev])
                            S.tt(eng, tA, AR, P, ALU.mult)
                            S.tt(eng, tB, AIs_, SBu, ALU.mult)
                            S.tt(eng, tA, tA, tB, ALU.add)
                            S.tt(eng, Z[:, qs, c], Z[:, qs, c], tA, ALU.add)
                        prevs[dr] = c
                S.memset("dve", SPf[:, 0:48, 64:65], 0.0)
                S.copy("dve", SPf[:, 0:48, 65:72], Z[:, 0:48, 64:71])
                S.copy("dve", SPf[:, 0:48, 0:1], Z[:, 0:48, 71:72])
                S.copy("dve", SPf[:, 0:48, 1:64], Z[:, 0:48, 0:63])
                S.memset("pool", SPf[:, 48:96, 71:72], 0.0)
                S.copy("act", SPf[:, 48:96, 64:71], Z[:, 48:96, 65:72])
                S.copy("act", SPf[:, 48:96, 0:64], Z[:, 48:96, 1:65])

                ada_steps(l + 1, 24)
                chk("E3S")
                dtab = VEC[:, V_DTAB + i * 48:V_DTAB + (i + 1) * 48]
                DGD = BA[:, o_:o_ + 1024].rearrange("p (g f) -> p g f", g=8)
                assert o_ + 1024 <= 36864

                def gen_tables(n_):
                    g0 = n_ * GB
                    sl = n_ % 2
                    S.dma(UGb[sl], UGLs[:, g0:g0 + GB, :])
                    for dr in range(2):
                        q0 = dr * 48 + g0
                        bs = sl * 2 + dr
                        rs_ = (n_ % 2) * 2 + dr
                        pr = PTRr[:, q0:q0 + GB, :]
                        pi_ = PTRi[:, q0:q0 + GB, :]
                        tbl_mults(RB[rs_], pr, pi_, X1c, X2c, q0, TBB[dr], "dve")
                    for dr in range(2):
                        q0 = dr * 48 + g0
                        bs = sl * 2 + dr
                        rs_ = (n_ % 2) * 2 + dr
                        e1 = slice(7, None, -1) if dr == 0 else slice(7, 15)
                        qe = "dve"
                        bigtable(qe, QB[bs], E1r[:, q0:q0 + GB, e1], E1i[:, q0:q0 + GB, e1], X1b, X2b, q0, 8,
                                 QTMP[dr * 2], QTMP[dr * 2 + 1])
                    for dr in range(2):
                        rs_ = (n_ % 2) * 2 + dr
                        tbl_add(RB[rs_], TBB[dr])

                def m_chain(n_):
                    sl = n_ % 2
                    for dr in range(2):
                        bs = sl * 2 + dr
                        rs_ = (n_ % 2) * 2 + dr
                        for gl in range(GB):
                            pm = ps_next()
                            S.mm(pm[:, 0:512], QB[bs][:, gl, :], RB[rs_][:, gl, :])
                            mdst = MB_[dr][:, gl, :]
                            S.copy("act", mdst[:, 0:512], pm[:, 0:512])
                            if dr == 0:
                                S.tt("dve", mdst[:, 0:128], mdst[:, 0:128], MSKF, ALU.mult)
                            else:
                                S.tt("dve", mdst[:, 384:512], mdst[:, 384:512], MSKB, ALU.mult)

                def y_part(n_):
                    g0 = n_ * GB
                    sl = n_ % 2
                    ug = UGb[sl]
                    hb = n_ % 2
                    for gl in range(GB):
                        g = g0 + gl
                        gi = hb * GB + gl
                        py = PS[4 + g % 2]
                        bf_, bb_ = 0, 1
                        rf_, rb_ = (n_ % 2) * 2 + 0, (n_ % 2) * 2 + 1
                        S.mm(py[0:NCH, 0:512], SPf[:, g, :], RB[rf_][:, gl, :], start=True, stop=False)
                        S.mm(py[0:NCH, 0:512], SPf[:, 48 + g, :], RB[rb_][:, gl, :], start=False, stop=False)
                        for a2 in range(4):
                            lh = ug[:, gl, a2::4]
                            S.mm(py[0:NCH, a2 * 128:(a2 + 1) * 128], lh, DGD[:, gi, :], start=False, stop=False)
                            S.mm(py[0:NCH, a2 * 128:512], lh, MB_[bf_][:, gl, 0:(4 - a2) * 128], start=False, stop=False)
                            S.mm(py[0:NCH, 0:(a2 + 1) * 128], lh, MB_[bb_][:, gl, (3 - a2) * 128:512],
                                 start=False, stop=(a2 == 3))
                        S.act(PG[0:NCH, :, gi, :], py[0:NCH, 0:512].rearrange("p (t h) -> p t h", t=32),
                              AF.Gelu_apprx_tanh)

                PTRr = XLA[:, 0:3072].rearrange("p (q k) -> p q k", q=96)
                PTRi = XLA[:, 3072:6144].rearrange("p (q k) -> p q k", q=96)
                ptmp = [XLA[:, 12064 + s_ * 1536:12064 + (s_ + 1) * 1536] for s_ in range(2)]

                def v4f(t_, q0_):
                    return t_[:, q0_:q0_ + 48, :].rearrange("p q (x y) -> p q x y", x=4)

                def pv(t_):
                    return t_[:, 0:1536].rearrange("p (q x y) -> p q x y", q=48, x=4)

                for dr, (e8sl, e1sl) in enumerate(((slice(0, 4), slice(7, 15)), (DESC4, slice(7, None, -1)))):
                    q0_ = dr * 48
                    a_r = E8r[:, q0_:q0_ + 48, e8sl].unsqueeze(3).to_broadcast([128, 48, 4, 8])
                    a_i = E8i[:, q0_:q0_ + 48, e8sl].unsqueeze(3).to_broadcast([128, 48, 4, 8])
                    b_r = E1r[:, q0_:q0_ + 48, e1sl].unsqueeze(2).to_broadcast([128, 48, 4, 8])
                    b_i = E1i[:, q0_:q0_ + 48, e1sl].unsqueeze(2).to_broadcast([128, 48, 4, 8])
                    cmul("dve", v4f(PTRr, q0_), v4f(PTRi, q0_), a_r, a_i, b_r, b_i, pv(ptmp[0]), pv(ptmp[1]))
                QTMP = [XLA[:, 12064 + s_ * 512:12064 + (s_ + 1) * 512] for s_ in range(4)]
                gen_tables(0)
                for n_ in range(12):
                    ct = n_ // 2
                    if n_ % 2 == 0:
                        for gi in range(8):
                            S.act(DGD[:, gi, :], IDF, AF.Copy, scale=dtab[:, ct * 8 + gi:ct * 8 + gi + 1])
                    m_chain(n_)
                    if n_ + 1 < 12:
                        gen_tables(n_ + 1)
                    chk("B_tab")
                    y_part(n_)
                    if n_ % 2 == 1:
                        ygt = YGT[0]
                        ygv = ygt.rearrange("p (c t) -> p t c", t=32)
                        for t0_, nt in ((0, 14), (14, 14), (28, 4)):
                            pb = psb_next()
                            for tk in range(nt):
                                S.tr(pb[:, tk * NCH:(tk + 1) * NCH],
                                     PG[0:NCH, t0_ + tk, :, :].rearrange("p g h -> p (g h)"), IDB[0:NCH, 0:NCH])
                            S.copy(("act", "dve")[(t0_ // 14) % 2], ygv[:, t0_:t0_ + nt, :],
                                   pb[:, 0:nt * NCH].rearrange("p (t c) -> p t c", t=nt))
                        S.dma(YGs[:, ct, :], ygt, eng=STQ)

            return _main

        def odd_layer(l):
            i = l // 2
            need_ctx = NEED_CTX[l]
            layer_mod(l)
            pw = FT[:, 0:512].rearrange("p (t d) -> p t d", t=4)
            S.dma(pw, poolw_d[i])
            for t_ in range(4):
                S.ts("dve", PWS[:, 0, t_, :], pw[:, t_, :], 1.0 / POOL_W[t_], None, ALU.mult)
                S.ts("dve", PWS[:, 1, t_, :], pw[:, t_, :], -1.0, None, ALU.mult)
                for k in range(3):
                    col = V_CONVW + (i * 3 + k) * 4 + t_
                    S.act(DG[:, t_, k, :], IDF, AF.Copy, scale=VEC[:, col:col + 1])
            act_pieces = [pj for pj, pc in enumerate(pieces) if not (pc[0] == "ctx" and not need_ctx)]

            def odd_norm(pj):
                kind, s0, s1, goff = pieces[pj]
                n = s1 - s0
                w_ = 0 if kind == "lat" else 1
                hl_ = 8 if (kind == "lat" and s0 > 0) else 0
                hr_ = 8 if (kind == "lat" and s1 < T) else 0
                ne = n + hl_ + hr_
                e0 = s0 - hl_
                HLe = BA[:, 0:KC * ne].rearrange("p (k t) -> p k t", k=KC)
                HALO = BA[:, 29184:29248].rearrange("p (k c) -> p k c", k=KC)
                hb0 = e0 + hl_
                normalize(kind, lambda k, a, nb, H=HLe, e0=e0: H[:, k, a - e0:a - e0 + nb],
                          AM[:, w_, :], ADA[:, l, 0:8, w_], [(hb0 + o, nb) for (o, nb) in nblocks(ne - hl_)])
                if kind == "lat" and s0 == 0:
                    S.copy("pool", HALO, HLe[:, :, 1016:1024])
                if hl_:
                    S.copy("pool", HLe[:, :, 0:hl_], HALO)

            layer_stats(need_ctx)
            for pi, (kind, s0, s1, goff) in enumerate(pieces):
                if kind == "ctx" and not need_ctx:
                    continue
                n = s1 - s0
                w_ = 0 if kind == "lat" else 1
                hl_ = 8 if (kind == "lat" and s0 > 0) else 0
                hr_ = 8 if (kind == "lat" and s1 < T) else 0
                ne = n + hl_ + hr_
                e0 = s0 - hl_
                left_edge = (hl_ == 0)
                right_edge = (hr_ == 0)
                HLe = BA[:, 0:KC * ne].rearrange("p (k t) -> p k t", k=KC)
                ZS = BA[:, 8320:8320 + KC * n].rearrange("p (k t) -> p k t", k=KC)
                wu = ne + 16
                UCb = BA[:, 16512:16512 + 4 * wu].rearrange("p (t c) -> p t c", t=4)
                wc_ = ne + 2
                CXb = BA[:, 20736:20736 + 4 * wc_].rearrange("p (t c) -> p t c", t=4)
                XPb = BA[:, 24896:24896 + 2 * ne].rearrange("p (t c) -> p t c", t=2)
                SA_ = BA[:, 26976:26976 + wu]
                SB_ = BA[:, 28032:28032 + wu]
                SQ = BA[:, 16512:16512 + 4096].rearrange("p (k c) -> p k c", k=KC)
                assert 28032 + wu <= BA_FREE
                HALO = BA[:, 29184:29248].rearrange("p (k c) -> p k c", k=KC)
                if pi == act_pieces[0]:
                    odd_norm(pi)
                core = lambda o, nb, hl_=hl_: slice(hl_ + o, hl_ + o + nb)

                def inproj(wv, t2, mov, nb):
                    pt = ps_next()
                    for k in range(KC):
                        S.mm(pt[:, 0:nb], wv[:, k, t2 * 128:(t2 + 1) * 128], mov(k), start=(k == 0), stop=(k == KC - 1))
                    return pt

                S.memset("pool", UCb[:, :, 0:8], 0.0)
                S.memset("pool", UCb[:, :, 8 + ne:16 + ne], 0.0)
                for wg in range(2):
                    w = load_w(owin_d[i], wg * 256, key=("owin", i))
                    for t2 in range(2):
                        t_ = wg * 2 + t2
                        for (o, nb) in nblocks(ne):
                            pt = inproj(w, t2, lambda k, o=o, nb=nb: HLe[:, k, o:o + nb], nb)
                            S.copy("act", UCb[:, t_, 8 + o:8 + o + nb], pt[:, 0:nb])
                c0 = 8 + hl_
                for t_ in range(4):
                    u = UCb[:, t_, :]
                    pe_ = "dve"
                    S.tt(pe_, SA_[:, 1:wu], u[:, 0:wu - 1], u[:, 1:wu], ALU.add)
                    cur, oth = SA_, SB_
                    if t_ >= 1:
                        S.tt(pe_, SB_[:, 2:wu - 1], SA_[:, 1:wu - 2], SA_[:, 3:wu], ALU.add)
                        cur, oth = SB_, SA_
                    if t_ >= 2:
                        S.tt(pe_, SA_[:, 4:wu - 3], SB_[:, 2:wu - 5], SB_[:, 6:wu - 1], ALU.add)
                        cur, oth = SA_, SB_
                    if t_ >= 3:
                        S.tt(pe_, SB_[:, 8:wu - 7], SA_[:, 4:wu - 11], SA_[:, 12:wu - 3], ALU.add)
                        cur, oth = SB_, SA_
                    fx = CST[:, C_FIX + t_ * 16:C_FIX + (t_ + 1) * 16]
                    if left_edge:
                        S.tt(pe_, cur[:, c0:c0 + 8], cur[:, c0:c0 + 8], fx[:, 0:8], ALU.mult)
                    if right_edge:
                        S.tt(pe_, cur[:, c0 + n - 8:c0 + n], cur[:, c0 + n - 8:c0 + n], fx[:, 8:16], ALU.mult)
                    S.ts(pe_, oth[:, c0:c0 + n], u[:, c0:c0 + n], -float(POOL_W[t_]), None, ALU.mult)
                    S.tt(pe_, u[:, c0:c0 + n], cur[:, c0:c0 + n], oth[:, c0:c0 + n], ALU.add)
                for wg in range(4):
                    w = load_w(owin_d[i], 2048 + wg * 256, key=("owin", i))
                    for t2 in range(2):
                        zt = wg * 2 + t2
                        for (o, nb) in nblocks(n):
                            pt = inproj(w, t2, lambda k, o=o, nb=nb: HLe[:, k, core(o, nb)], nb)
                            S.act(ZS[:, zt, o:o + nb], pt[:, 0:nb], AF.Silu)
                for wg in range(2):
                    w = load_w(owin_d[i], 1536 + wg * 256, key=("owin", i))
                    for t2 in range(2):
                        t_ = wg * 2 + t2
                        for (o, nb) in nblocks(n):
                            pt = inproj(w, t2, lambda k, o=o, nb=nb: HLe[:, k, core(o, nb)], nb)
                            S.tt("dve", ZS[:, 4 + t_, o:o + nb], pt[:, 0:nb], ZS[:, 4 + t_, o:o + nb], ALU.mult)
                for t_ in range(4):
                    for (o, nb) in nblocks(n):
                        pt = ps_next()
                        S.mm(pt[:, 0:nb], PWS[:, 0, t_, :], UCb[:, t_, c0 + o:c0 + o + nb])
                        S.stt("dve", ZS[:, t_, o:o + nb], pt[:, 0:nb],
                              VEC[:, V_PSCALE + i * 4 + t_:V_PSCALE + i * 4 + t_ + 1], ZS[:, t_, o:o + nb],
                              ALU.mult, ALU.mult)
                S.memset("pool", CXb[:, :, 0:1], 0.0)
                S.memset("pool", CXb[:, :, 1 + ne:2 + ne], 0.0)
                for wg in range(2):
                    wx = load_w(owin_d[i], 1024 + wg * 256, key=("owin", i))
                    for t2 in range(2):
                        for (o, nb) in nblocks(ne):
                            pt = inproj(wx, t2, lambda k, o=o, nb=nb: HLe[:, k, o:o + nb], nb)
                            S.copy("act", XPb[:, t2, o:o + nb], pt[:, 0:nb])
                    wc = load_w(owin_d[i], 512 + wg * 256, key=("owin", i))
                    for t2 in range(2):
                        t_ = wg * 2 + t2
                        for (o, nb) in nblocks(ne):
                            pt = inproj(wc, t2, lambda k, o=o, nb=nb: HLe[:, k, o:o + nb], nb)
                            S.tt("dve", CXb[:, t_, 1 + o:1 + o + nb], pt[:, 0:nb], XPb[:, t2, o:o + nb], ALU.mult)
                nxt = [pj for pj in act_pieces if pj > pi]
                if nxt:
                    odd_norm(nxt[0])
                for t_ in range(4):
                    for (o, nb) in nblocks(n):
                        pt = ps_next()
                        for k in range(3):
                            S.mm(pt[:, 0:nb], DG[:, t_, k, :], CXb[:, t_, hl_ + o + k:hl_ + o + k + nb],
                                 start=(k == 0), stop=(k == 2))
                        S.tt("dve", ZS[:, 4 + t_, o:o + nb], pt[:, 0:nb], ZS[:, 4 + t_, o:o + nb], ALU.mult)
                out_proj(owout_d[i], ZS, n, kind, s0, w_, l, wkey=("owout", i))

        try:
            chk("prologue")
            for l in range(nlayers):
                if l % 2 == 0:
                    even_layer(l)
                else:
                    odd_layer(l)
                if l % 2 == 0:
                    ada_steps(l + 1, 24)
                    ada_steps(l + 2, 24)
        except _Stop:
            print("stopped at op", len(S.ops))
            S.limit = None

        outv = outT_d.rearrange("(k p) t -> p k t", p=128)
        if final_norm:
            fg = VEC[:, V_FINALG:V_FINALG + 8]
            layer_stats(False)
            for bi, (o, nb) in enumerate(nblocks(T, 256)):
                ob = FA[:, (bi % 2) * 2048:(bi % 2) * 2048 + 2048].rearrange("p (k c) -> p k c", k=KC)
                for k in range(KC):
                    S.stt("dve", ob[:, k, 0:nb], XL[:, k, o:o + nb], fg[:, k:k + 1], RSL[:, o:o + nb],
                          ALU.mult, ALU.mult)
                S.dma(outv[:, :, o:o + nb], ob[:, :, 0:nb], eng=(STQ, "sp")[bi % 2])
        else:
            S.dma(outv, XL)

        S.emit(sems, dsems)
        build_program.stats = (len(S.ops), S.n_waits, S.n_signals)
    return nc


_PROGRAM_CACHE = {}


def kernel(**inputs):
    inp = {k: np.asarray(v) for k, v in inputs.items()}
    nb = inp["x"].shape[0]
    shared = _prep_shared_inputs(inp)
    in_maps = []
    for b in range(nb):
        m = dict(shared)
        m.update(_prep_core_inputs(inp, b))
        in_maps.append(m)
    if "nc" not in _PROGRAM_CACHE:
        _PROGRAM_CACHE["nc"] = build_program()
    nc = _PROGRAM_CACHE["nc"]
    res = run_bass_kernel_spmd(nc, in_maps, core_ids=list(range(nb)))
    out = np.stack([np.ascontiguousarray(res.results[b]["outT"].T) for b in range(nb)], axis=0)
    return out.astype(np.float32)
```
